# Optimizing a Trainium2 kernel written in Bass

```python
import jax
import jax.numpy as jnp
from jax import lax
import numpy as np

D_MODEL = 1024
BATCH = 8
SEQ = 4096
DEPTH = 1

HEAD_DIM = 64
MIX_WIDTH = D_MODEL
NSA_HEADS = MIX_WIDTH // 2 // HEAD_DIM
NSA_KV_HEADS = 2
SWA_HEADS = MIX_WIDTH // 2 // HEAD_DIM
SWA_KV_HEADS = 2
CMP_BLOCK = 32
CMP_STRIDE = 16
CMP_HIDDEN = 256
SEL_BLOCK = 64
SEL_TOPK = 16
SEL_Q_CHUNK = 64
NSA_WINDOW = 512
SWA_WINDOW = 128
BAND_BLOCK = 128
N_NSA_BRANCHES = 3
D_FF = 2816
NORM_EPS = 1e-6
NEG_INF = -1e30
FORCE_SCORE = 1e9
ATTN_SCALE = HEAD_DIM ** -0.5
PROJ_SPLITS = (NSA_HEADS * HEAD_DIM, 2 * N_NSA_BRANCHES * NSA_KV_HEADS * HEAD_DIM, N_NSA_BRANCHES * NSA_HEADS, SWA_HEADS * HEAD_DIM, 2 * SWA_KV_HEADS * HEAD_DIM)
PROJ_WIDTH = sum(PROJ_SPLITS)

kernel_name = "hybrid_nsa_swa_sink_macaron_alibi"


def alibi_slopes(n_heads, n_groups):
    slopes = 2.0 ** (-8.0 * np.arange(1, n_heads + 1) / n_heads)
    return jnp.asarray(slopes.reshape(n_groups, n_heads // n_groups), jnp.float32)


def rmsnorm(x, g):
    xf = x.astype(jnp.float32)
    y = xf * lax.rsqrt(jnp.mean(xf * xf, axis=-1, keepdims=True) + NORM_EPS)
    return (y * g.astype(jnp.float32)).astype(x.dtype)


def swiglu(x, w_in, w_out):
    gate, up = jnp.split(x @ w_in, 2, axis=-1)
    return (jax.nn.silu(gate) * up) @ w_out


def banded_attention(q, k, v, slopes, window, sinks=None):
    B, T, G, R, D = q.shape
    nb = T // BAND_BLOCK
    n_prev = -(-(window - 1) // BAND_BLOCK)
    pad = n_prev * BAND_BLOCK
    kw_len = (n_prev + 1) * BAND_BLOCK
    kp = jnp.pad(k, ((0, 0), (pad, 0), (0, 0), (0, 0))).reshape(B, nb + n_prev, BAND_BLOCK, G, D)
    vp = jnp.pad(v, ((0, 0), (pad, 0), (0, 0), (0, 0))).reshape(B, nb + n_prev, BAND_BLOCK, G, D)
    kw = jnp.concatenate([kp[:, j:j + nb] for j in range(n_prev + 1)], axis=2)
    vw = jnp.concatenate([vp[:, j:j + nb] for j in range(n_prev + 1)], axis=2)
    qb = q.reshape(B, nb, BAND_BLOCK, G, R, D)
    s = jnp.einsum('bnqgrd,bnkgd->bngrqk', qb, kw) * ATTN_SCALE
    t_pos = np.arange(nb)[:, None] * BAND_BLOCK + np.arange(BAND_BLOCK)[None, :]
    s_pos = np.arange(nb)[:, None] * BAND_BLOCK - pad + np.arange(kw_len)[None, :]
    dist = t_pos[:, :, None] - s_pos[:, None, :]
    mask = jnp.asarray((dist >= 0) & (dist < window) & (s_pos[:, None, :] >= 0))
    bias = -slopes[None, :, :, None, None] * jnp.asarray(dist, jnp.float32)[:, None, None]
    s = jnp.where(mask[:, None, None], s + bias, NEG_INF)
    if sinks is not None:
        sink = jnp.broadcast_to(sinks[None, None, :, :, None, None], s.shape[:-1] + (1,))
        p = jax.nn.softmax(jnp.concatenate([s, sink], axis=-1), axis=-1)[..., :-1]
    else:
        p = jax.nn.softmax(s, axis=-1)
    o = jnp.einsum('bngrqk,bnkgd->bnqgrd', p, vw)
    return o.reshape(B, T, G, R, D)


def compress_blocks(kr, pos, w1, b1, w2):
    B, T, G, D = kr.shape
    nc = (T - CMP_BLOCK) // CMP_STRIDE + 1
    idx = np.arange(nc)[:, None] * CMP_STRIDE + np.arange(CMP_BLOCK)[None, :]
    blk = kr[:, idx] + pos[None, None, :, None, :]
    flat = blk.transpose(0, 1, 3, 2, 4).reshape(B, nc, G, CMP_BLOCK * D)
    return jax.nn.gelu(flat @ w1 + b1) @ w2


def selected_attention(q, k, v, sel_idx, slopes):
    B, T, G, R, D = q.shape
    ns = T // SEL_BLOCK
    n_sel = sel_idx.shape[-1]
    nq = T // SEL_Q_CHUNK
    kb = k.reshape(B, ns, SEL_BLOCK, G, D).transpose(0, 3, 1, 2, 4)
    vb = v.reshape(B, ns, SEL_BLOCK, G, D).transpose(0, 3, 1, 2, 4)
    qc = q.reshape(B, nq, SEL_Q_CHUNK, G, R, D).transpose(1, 0, 2, 3, 4, 5)
    ic = sel_idx.reshape(B, nq, SEL_Q_CHUNK, G, n_sel).transpose(1, 0, 2, 3, 4)
    tc = jnp.arange(T).reshape(nq, SEL_Q_CHUNK)
    b_ix = jnp.arange(B)[:, None, None, None]
    g_ix = jnp.arange(G)[None, None, :, None]

    def chunk(args):
        qx, ix, tx = args
        kg = kb[b_ix, g_ix, ix]
        vg = vb[b_ix, g_ix, ix]
        s = jnp.einsum('bcgrd,bcgkld->bcgrkl', qx, kg) * ATTN_SCALE
        spos = ix[..., None] * SEL_BLOCK + jnp.arange(SEL_BLOCK)
        dist = tx[None, :, None, None, None] - spos
        mask = (dist >= 0)[:, :, :, None]
        bias = -slopes[None, None, :, :, None, None] * dist.astype(jnp.float32)[:, :, :, None]
        s = jnp.where(mask, s + bias, NEG_INF)
        C = qx.shape[1]
        p = jax.nn.softmax(s.reshape(B, C, G, R, n_sel * SEL_BLOCK), axis=-1).reshape(s.shape)
        return jnp.einsum('bcgrkl,bcgkld->bcgrd', p, vg)

    o = lax.map(chunk, (qc, ic, tc))
    return o.transpose(1, 0, 2, 3, 4, 5).reshape(B, T, G, R, D)


def nsa_group(q, kc_raw, vc_raw, ks, vs, kw, vw, gate_logits, ck_pos, ck_w1, ck_b1, ck_w2, cv_pos, cv_w1, cv_b1, cv_w2, slopes):
    B, T, G, R, D = q.shape
    kc = compress_blocks(kc_raw, ck_pos, ck_w1, ck_b1, ck_w2).astype(jnp.float32)
    vc = compress_blocks(vc_raw, cv_pos, cv_w1, cv_b1, cv_w2).astype(jnp.float32)
    nc = kc.shape[1]
    blk_end = np.arange(nc) * CMP_STRIDE + CMP_BLOCK - 1
    dist_c = np.arange(T)[:, None] - blk_end[None, :]
    mask_c = jnp.asarray(dist_c >= 0)
    s = jnp.einsum('btgrd,bcgd->bgrtc', q, kc) * ATTN_SCALE
    s = jnp.where(mask_c, s - slopes[:, :, None, None] * jnp.asarray(dist_c, jnp.float32), NEG_INF)
    p_cmp = jnp.where(mask_c, jax.nn.softmax(s, axis=-1), 0.0)
    o_cmp = jnp.einsum('bgrtc,bcgd->btgrd', p_cmp, vc)
    ns = T // SEL_BLOCK
    c_start = np.arange(nc) * CMP_STRIDE
    s_start = np.arange(ns) * SEL_BLOCK
    overlap = np.clip(np.minimum(c_start[:, None] + CMP_BLOCK, s_start[None, :] + SEL_BLOCK) - np.maximum(c_start[:, None], s_start[None, :]), 0, None) / CMP_BLOCK
    p_slc = jnp.einsum('bgrtc,cs->btgs', p_cmp, jnp.asarray(overlap, jnp.float32))
    cur = np.arange(T) // SEL_BLOCK
    blk = np.arange(ns)
    valid = jnp.asarray(blk[None, :] <= cur[:, None])[None, :, None, :]
    forced = jnp.asarray((blk[None, :] == 0) | (blk[None, :] == cur[:, None]) | (blk[None, :] == cur[:, None] - 1))[None, :, None, :]
    score = jnp.where(forced, FORCE_SCORE, jnp.where(valid, p_slc, NEG_INF))
    _, sel_idx = lax.top_k(score, min(SEL_TOPK, ns))
    o_slc = selected_attention(q, ks, vs, sel_idx, slopes)
    o_win = banded_attention(q, kw, vw, slopes, NSA_WINDOW)
    g = jax.nn.sigmoid(gate_logits)
    return g[..., 0:1] * o_cmp + g[..., 1:2] * o_slc + g[..., 2:3] * o_win


def hybrid_mixer(xn, w_in, ck_pos, ck_w1, ck_b1, ck_w2, cv_pos, cv_w1, cv_b1, cv_w2, sinks, w_out):
    B, T, _ = xn.shape
    D = HEAD_DIM
    ga, ra = NSA_KV_HEADS, NSA_HEADS // NSA_KV_HEADS
    gb, rb = SWA_KV_HEADS, SWA_HEADS // SWA_KV_HEADS
    proj = (xn @ w_in).astype(jnp.float32)
    offsets = [int(o) for o in np.cumsum(PROJ_SPLITS)[:-1]]
    q_a, kv_a, gate_a, q_b, kv_b = jnp.split(proj, offsets, axis=-1)
    q_a = q_a.reshape(B, T, ga, ra, D)
    kc_raw, vc_raw, ks, vs, kw, vw = [t.reshape(B, T, ga, D) for t in jnp.split(kv_a, 2 * N_NSA_BRANCHES, axis=-1)]
    gate_a = gate_a.reshape(B, T, ga, ra, N_NSA_BRANCHES)
    o_a = nsa_group(q_a, kc_raw, vc_raw, ks, vs, kw, vw, gate_a, ck_pos, ck_w1, ck_b1, ck_w2, cv_pos, cv_w1, cv_b1, cv_w2, alibi_slopes(NSA_HEADS, NSA_KV_HEADS))
    q_b = q_b.reshape(B, T, gb, rb, D)
    k_b, v_b = [t.reshape(B, T, gb, D) for t in jnp.split(kv_b, 2, axis=-1)]
    o_b = banded_attention(q_b, k_b, v_b, alibi_slopes(SWA_HEADS, SWA_KV_HEADS), SWA_WINDOW, sinks.astype(jnp.float32).reshape(gb, rb))
    o = jnp.concatenate([o_a.reshape(B, T, NSA_HEADS * D), o_b.reshape(B, T, SWA_HEADS * D)], axis=-1)
    return o.astype(xn.dtype) @ w_out


def setup_inputs(seed: int = 0) -> dict:
    key = jax.random.key(seed)
    ks = jax.random.split(key, 24)
    L = DEPTH

    def nrm(k, shape, fan_in):
        return jax.random.normal(k, shape, jnp.float32) * fan_in ** -0.5

    def gain(k, shape):
        return 1.0 + 0.1 * jax.random.normal(k, shape, jnp.float32)

    return {
        'x': jax.random.normal(ks[0], (BATCH, SEQ, D_MODEL), jnp.float32),
        'ffn1_norm': gain(ks[1], (L, D_MODEL)),
        'ffn1_w_in': nrm(ks[2], (L, D_MODEL, 2 * D_FF), D_MODEL),
        'ffn1_w_out': nrm(ks[3], (L, D_FF, D_MODEL), D_FF),
        'mix_norm': gain(ks[4], (L, D_MODEL)),
        'w_mix_in': nrm(ks[5], (L, D_MODEL, PROJ_WIDTH), D_MODEL),
        'cmp_k_pos': 0.5 * jax.random.normal(ks[6], (L, CMP_BLOCK, HEAD_DIM), jnp.float32),
        'cmp_k_w1': nrm(ks[7], (L, CMP_BLOCK * HEAD_DIM, CMP_HIDDEN), CMP_BLOCK * HEAD_DIM),
        'cmp_k_b1': 0.02 * jax.random.normal(ks[8], (L, CMP_HIDDEN), jnp.float32),
        'cmp_k_w2': nrm(ks[9], (L, CMP_HIDDEN, HEAD_DIM), CMP_HIDDEN),
        'cmp_v_pos': 0.5 * jax.random.normal(ks[10], (L, CMP_BLOCK, HEAD_DIM), jnp.float32),
        'cmp_v_w1': nrm(ks[11], (L, CMP_BLOCK * HEAD_DIM, CMP_HIDDEN), CMP_BLOCK * HEAD_DIM),
        'cmp_v_b1': 0.02 * jax.random.normal(ks[12], (L, CMP_HIDDEN), jnp.float32),
        'cmp_v_w2': nrm(ks[13], (L, CMP_HIDDEN, HEAD_DIM), CMP_HIDDEN),
        'swa_sinks': 0.5 * jax.random.normal(ks[14], (L, SWA_HEADS), jnp.float32),
        'w_mix_out': nrm(ks[15], (L, MIX_WIDTH, D_MODEL), MIX_WIDTH),
        'ffn2_norm': gain(ks[16], (L, D_MODEL)),
        'ffn2_w_in': nrm(ks[17], (L, D_MODEL, 2 * D_FF), D_MODEL),
        'ffn2_w_out': nrm(ks[18], (L, D_FF, D_MODEL), D_FF),
        'final_norm': gain(ks[19], (D_MODEL,)),
    }


def reference(x, ffn1_norm, ffn1_w_in, ffn1_w_out, mix_norm, w_mix_in, cmp_k_pos, cmp_k_w1, cmp_k_b1, cmp_k_w2, cmp_v_pos, cmp_v_w1, cmp_v_b1, cmp_v_w2, swa_sinks, w_mix_out, ffn2_norm, ffn2_w_in, ffn2_w_out, final_norm):
    h = x
    for l in range(DEPTH):
        h = h + 0.5 * swiglu(rmsnorm(h, ffn1_norm[l]), ffn1_w_in[l], ffn1_w_out[l])
        h = h + hybrid_mixer(rmsnorm(h, mix_norm[l]), w_mix_in[l], cmp_k_pos[l], cmp_k_w1[l], cmp_k_b1[l], cmp_k_w2[l], cmp_v_pos[l], cmp_v_w1[l], cmp_v_b1[l], cmp_v_w2[l], swa_sinks[l], w_mix_out[l])
        h = h + 0.5 * swiglu(rmsnorm(h, ffn2_norm[l]), ffn2_w_in[l], ffn2_w_out[l])
    return rmsnorm(h, final_norm)
```

```python
import numpy as np
import ml_dtypes
import concourse.bass as bass
import concourse.mybir as mybir
from concourse.bass_utils import run_bass_kernel_spmd
from contextlib import ExitStack

F32 = mybir.dt.float32
BF16 = mybir.dt.bfloat16
AF = mybir.ActivationFunctionType
ALU = mybir.AluOpType

D = 1024
FF = 2816
NJ = FF // 128
PROJ_W = 2072
NEG = -30000.0
EPS = 1e-6
SLOPES = [2.0 ** (-(i + 1)) for i in range(8)]
SAME_ENGINE_SYNC = True
EPOCH = 100000


class Sem:
    def __init__(self, h):
        self.h = h
        self.count = 0


class Buf:
    __slots__ = ("name", "w", "r", "sem")

    def __init__(self, name):
        self.name = name
        self.w = []
        self.r = []
        self.sem = None


class Op:
    __slots__ = ("eng", "fn", "deps", "dma", "sem", "val", "signal")

    def __init__(self, eng, fn, dma):
        self.eng = eng
        self.fn = fn
        self.deps = []
        self.dma = dma
        self.sem = None
        self.val = 0
        self.signal = False


ENGS = ["tensor", "vector", "scalar", "gpsimd", "sync"]


class Prog:
    def __init__(self, nc, sems):
        self.nc = nc
        self.free_sems = [Sem(h) for h in sems]
        self.eng_sem = {}
        self.eng_cnt = {}
        for e in ENGS:
            self.eng_sem[e] = self.free_sems.pop()
            self.eng_cnt[e] = 0
        self.waited = {e: {} for e in ENGS}
        self.ops = {e: [] for e in ENGS}
        self.pending = {e: [] for e in ENGS}
        self.dma_bufs = []
        self.last = {e: None for e in ENGS}

    def buf(self, name):
        return Buf(name)

    def reg(self, eng, val):
        if val not in self.regcache:
            self.regcache[val] = eng.to_reg(val)
        return self.regcache[val]

    def op(self, eng, fn, reads=(), writes=(), dma_owner=None):
        o = Op(eng, fn, dma_owner is not None)
        deps = o.deps
        for b in reads:
            deps.extend(b.w)
        for b in writes:
            deps.extend(b.w)
            deps.extend(b.r)
        if self.pending[eng]:
            deps.extend(self.pending[eng])
            self.pending[eng] = []
        for b in reads:
            if o.dma:
                b.r.append(o)
            else:
                b.r = [x for x in b.r if x.dma or x.eng != eng] + [o]
        for b in writes:
            b.w = [o]
            b.r = []
        if dma_owner is not None:
            if dma_owner.sem is None:
                dma_owner.sem = self.free_sems.pop()
                self.dma_bufs.append(dma_owner)
            o.sem = dma_owner.sem
            o.sem.count += 16
            o.val = o.sem.count
        self.ops[eng].append(o)
        self.last[eng] = o
        return o

    def emit(self, final=False):
        nc = self.nc
        self.regcache = {}
        for e in ENGS:
            for o in self.ops[e]:
                for d in o.deps:
                    if not d.dma and not (d.eng == e and (e == "tensor" or not SAME_ENGINE_SYNC)):
                        d.signal = True
        for e in ENGS:
            if self.last[e] is not None and not self.last[e].dma:
                self.last[e].signal = True
        for e in ENGS:
            for o in self.ops[e]:
                if o.dma:
                    continue
                if o.signal:
                    if self.eng_cnt[e] >= EPOCH:
                        self.eng_sem[e] = self.free_sems.pop()
                        self.eng_cnt[e] = 0
                    self.eng_cnt[e] += 1
                    o.sem = self.eng_sem[e]
                    o.val = self.eng_cnt[e]
        end_tokens = []
        for e in ENGS:
            if self.last[e] is not None and not self.last[e].dma:
                end_tokens.append(self.last[e])
        dma_final = [(b.sem, b.sem.count) for b in self.dma_bufs]

        def run(e, eng):
            waited = self.waited[e]
            for o in self.ops[e]:
                need = {}
                for d in o.deps:
                    if d.sem is None:
                        continue
                    if (not d.dma) and d.eng == e:
                        if e == "tensor" or not SAME_ENGINE_SYNC:
                            continue
                    k = d.sem
                    if need.get(k, 0) < d.val:
                        need[k] = d.val
                for k, v in need.items():
                    if waited.get(k, 0) >= v:
                        continue
                    eng.wait_ge(k.h, v)
                    waited[k] = v
                ins = o.fn(eng)
                if o.dma:
                    ins.then_inc(o.sem.h, 16)
                elif o.signal:
                    ins.then_inc(o.sem.h, 1)
            if e == "sync":
                for s, v in dma_final:
                    if waited.get(s, 0) < v:
                        eng.wait_ge(s.h, v)
                        waited[s] = v
                for t in end_tokens:
                    if t.eng != e and waited.get(t.sem, 0) < t.val:
                        eng.wait_ge(t.sem.h, t.val)
                        waited[t.sem] = t.val

        with nc.Block() as block:
            @block.tensor
            def _(eng):
                run("tensor", eng)

            @block.vector
            def _(eng):
                run("vector", eng)

            @block.scalar
            def _(eng):
                run("scalar", eng)

            @block.gpsimd
            def _(eng):
                run("gpsimd", eng)

            @block.sync
            def _(eng):
                run("sync", eng)

        for b in self.dma_bufs:
            self.free_sems.insert(0, b.sem)
            b.sem = None
        self.dma_bufs = []
        self.ops = {e: [] for e in ENGS}
        self.last = {e: None for e in ENGS}


def apx(ap, extra):
    return bass.AP(ap.tensor, ap.offset, [list(x) for x in ap.ap] + [list(x) for x in extra])


def ffn_phase(nc, P, tag, src, dst, g_ap, w_in, w_out, T, final_g=None):
    NB = T // 256
    with ExitStack() as es:
        def sb(name, shape, dt):
            return es.enter_context(nc.sbuf_tensor(f"{tag}_{name}", shape, dt))

        def ps(name, shape, dt):
            return es.enter_context(nc.psum_tensor(f"{tag}_{name}", shape, dt))

        WIN = sb("win", [128, 8, 2 * FF], BF16)
        WOUT = sb("wout", [128, NJ, D], BF16)
        G = sb("g", [128, 8], F32)
        IDENT = sb("ident", [128, 128], BF16)
        NEGH = sb("negh", [128, 1], F32)
        XB = [sb(f"xb{i}", [128, 2, D], F32) for i in range(3)]
        XN = [sb(f"xn{i}", [128, D], BF16) for i in range(2)]
        XNT = [sb(f"xnt{i}", [128, 8, 256], BF16) for i in range(2)]
        SQ = sb("sq", [128, D], BF16)
        ST = [sb(f"st{i}", [128, 4], F32) for i in range(4)]
        SIL = [sb(f"sil{i}", [128, 256], F32) for i in range(2)]
        HT = [sb(f"ht{i}", [128, 256], BF16) for i in range(3)]
        if final_g is not None:
            FG = sb("fg", [128, D], F32)
            OUTT = [sb(f"outt{i}", [128, D], F32) for i in range(2)]
        TPS = ps("tps", [128, 4, 256], BF16)
        GU = [ps(f"gu{i}", [128, 512], F32) for i in range(3)]
        ACC = [ps(f"acc{i}", [128, 512], F32) for i in range(4)]

        bWIN = [P.buf(f"win{k}") for k in range(8)]
        bWOUT = [P.buf(f"wout{i}") for i in range(NJ)]
        bG = P.buf("g")
        bC = P.buf("const")
        bXB = [P.buf(f"xb{i}") for i in range(3)]
        bXN = [P.buf(f"xn{i}") for i in range(2)]
        bXNT = [P.buf(f"xnt{i}") for i in range(2)]
        bSQ = P.buf("sq")
        bST = [P.buf(f"st{i}") for i in range(4)]
        bSIL = [P.buf(f"sil{i}") for i in range(2)]
        bHT = [P.buf(f"ht{i}") for i in range(3)]
        bTPS = P.buf("tps")
        bGU = [P.buf(f"gu{i}") for i in range(3)]
        bACC = [P.buf(f"acc{i}") for i in range(4)]
        bDST = P.buf("dst")
        if final_g is not None:
            bFG = P.buf("fg")
            bOUTT = [P.buf(f"outt{i}") for i in range(2)]

        P.op("gpsimd", lambda e: e.memset(IDENT[:], 1.0), writes=[bC])
        P.op("gpsimd", lambda e: e.affine_select(out=IDENT[:], in_=IDENT[:], pattern=[[-1, 128]],
                                                 compare_op=ALU.is_equal, fill=P.reg(e, 0.0), base=0, channel_multiplier=1),
             writes=[bC])
        P.op("gpsimd", lambda e: e.memset(NEGH[:], -0.5), writes=[bC])
        P.op("sync", lambda e: e.dma_start(out=G[:], in_=g_ap.rearrange("(c p) -> p c", p=128),
                                           allow_slow_non_contiguous=True), writes=[bG], dma_owner=bG)
        if final_g is not None:
            P.op("sync", lambda e: e.dma_start(out=FG[:], in_=final_g.partition_broadcast(128)),
                 writes=[bFG], dma_owner=bFG)

        def load(b):
            s = b % 3
            for t in range(2):
                r0 = (2 * b + t) * 128
                P.op("sync", lambda e, s=s, t=t, r0=r0: e.dma_start(out=XB[s][:, t, :], in_=src[r0:r0 + 128, :]),
                     writes=[bXB[s]], dma_owner=bXB[s])

        load(0)
        for k in range(8):
            for hh in range(4):
                c0 = hh * 1408
                P.op("gpsimd", lambda e, k=k, c0=c0: e.dma_start(out=WIN[:, k, c0:c0 + 1408],
                                                                in_=w_in[k * 128:(k + 1) * 128, c0:c0 + 1408]),
                     writes=[bWIN[k]], dma_owner=bWIN[k])
        if NB > 1:
            load(1)
        for j in range(NJ):
            P.op("gpsimd", lambda e, j=j: e.dma_start(out=WOUT[:, j, :], in_=w_out[j * 128:(j + 1) * 128, :]),
                 writes=[bWOUT[j]], dma_owner=bWOUT[j])

        def norm_tile(x_ap, bx, si, out_ap, bout, extra_reads=()):
            st, bst = ST[si], bST[si]
            P.op("scalar", lambda e: e.activation(out=SQ[:], in_=x_ap, func=AF.Square, accum_out=st[:, 0:1]),
                 reads=[bx], writes=[bSQ, bst])
            P.op("vector", lambda e: e.tensor_scalar(out=st[:, 1:2], in0=st[:, 0:1], scalar1=1.0 / D, scalar2=EPS,
                                                     op0=ALU.mult, op1=ALU.add), reads=[bst], writes=[bst])
            P.op("gpsimd", lambda e: e.tensor_tensor(out=st[:, 2:3], in0=st[:, 1:2], in1=NEGH[:], op=ALU.pow),
                 reads=[bst, bC], writes=[bst])
            P.op("scalar", lambda e: e.activation(out=out_ap, in_=x_ap, func=AF.Copy, scale=st[:, 2:3]),
                 reads=[bx, bst] + list(extra_reads), writes=[bout])

        def prologue(b):
            s = b % 3
            x2 = b % 2
            for t in range(2):
                norm_tile(XB[s][:, t, :], bXB[s], (2 * b + t) % 4, XN[t][:], bXN[t])
            for h in range(2):
                for t in range(2):
                    for kk in range(4):
                        k = 4 * h + kk
                        P.op("tensor", lambda e, t=t, k=k, kk=kk: e.transpose(
                            out=TPS[:, kk, t * 128:(t + 1) * 128], in_=XN[t][:, k * 128:(k + 1) * 128],
                            identity=IDENT[:]), reads=[bXN[t], bC], writes=[bTPS])
                for kk in range(4):
                    k = 4 * h + kk
                    P.op("vector", lambda e, k=k, kk=kk, x2=x2: e.tensor_scalar(
                        out=XNT[x2][:, k, :], in0=TPS[:, kk, :], scalar1=G[:, k:k + 1], scalar2=None, op0=ALU.mult),
                         reads=[bTPS, bG], writes=[bXNT[x2]])

        def gu(b, j):
            x2 = b % 2
            gb = (b * NJ + j) % 3
            for half in range(2):
                c0 = half * FF + j * 128
                for k in range(8):
                    P.op("tensor", lambda e, k=k, c0=c0, half=half, gb=gb, x2=x2: e.matmul(
                        GU[gb][:, half * 256:(half + 1) * 256], lhsT=WIN[:, k, c0:c0 + 128], rhs=XNT[x2][:, k, :],
                        start=(k == 0), stop=(k == 7)), reads=[bWIN[k], bXNT[x2]], writes=[bGU[gb]])

        def second(b, j):
            gb = (b * NJ + j) % 3
            hb = (b * NJ + j) % 3
            sl = (b * NJ + j) % 2
            P.op("scalar", lambda e: e.activation(out=SIL[sl][:], in_=GU[gb][:, 0:256], func=AF.Silu),
                 reads=[bGU[gb]], writes=[bSIL[sl]])
            P.op("vector", lambda e: e.tensor_tensor(out=HT[hb][:], in0=SIL[sl][:], in1=GU[gb][:, 256:512],
                                                     op=ALU.mult), reads=[bSIL[sl], bGU[gb]], writes=[bHT[hb]])
            for t in range(2):
                for n in range(2):
                    P.op("tensor", lambda e, t=t, n=n: e.matmul(
                        ACC[t * 2 + n][:], lhsT=HT[hb][:, t * 128:(t + 1) * 128], rhs=WOUT[:, j, n * 512:(n + 1) * 512],
                        start=(j == 0), stop=(j == NJ - 1)), reads=[bHT[hb], bWOUT[j]], writes=[bACC[t * 2 + n]])

        def epilogue(b):
            s = b % 3
            for t in range(2):
                for n in range(2):
                    P.op("vector", lambda e, t=t, n=n: e.scalar_tensor_tensor(
                        out=XB[s][:, t, n * 512:(n + 1) * 512], in0=ACC[t * 2 + n][:], scalar=0.5,
                        in1=XB[s][:, t, n * 512:(n + 1) * 512], op0=ALU.mult, op1=ALU.add),
                         reads=[bACC[t * 2 + n], bXB[s]], writes=[bXB[s]])
                r0 = (2 * b + t) * 128
                if final_g is None:
                    P.op("sync", lambda e, t=t, r0=r0: e.dma_start(out=dst[r0:r0 + 128, :], in_=XB[s][:, t, :]),
                         reads=[bXB[s]], writes=[bDST], dma_owner=bXB[s])
                else:
                    o2 = (2 * b + t) % 2
                    norm_tile(XB[s][:, t, :], bXB[s], (2 * b + t) % 4, OUTT[o2][:], bOUTT[o2])
                    P.op("vector", lambda e, o2=o2: e.tensor_tensor(out=OUTT[o2][:], in0=OUTT[o2][:], in1=FG[:],
                                                                    op=ALU.mult),
                         reads=[bOUTT[o2], bFG], writes=[bOUTT[o2]])
                    P.op("sync", lambda e, o2=o2, r0=r0: e.dma_start(out=dst[r0:r0 + 128, :], in_=OUTT[o2][:]),
                         reads=[bOUTT[o2]], writes=[bDST], dma_owner=bOUTT[o2])

        prologue(0)
        gu(0, 0)
        for b in range(NB):
            if b + 2 < NB:
                load(b + 2)
            for j in range(NJ):
                if j + 1 < NJ:
                    gu(b, j + 1)
                elif b + 1 < NB:
                    gu(b + 1, 0)
                if j == 6 and b + 1 < NB:
                    prologue(b + 1)
                second(b, j)
            epilogue(b)
        P.emit()


def norm_ops(P, x_ap, bx, SQ, bSQ, st, bst, NEGH, bC, out_ap, bout):
    P.op("scalar", lambda e: e.activation(out=SQ[:], in_=x_ap, func=AF.Square, accum_out=st[:, 0:1]),
         reads=[bx], writes=[bSQ, bst])
    P.op("vector", lambda e: e.tensor_scalar(out=st[:, 1:2], in0=st[:, 0:1], scalar1=1.0 / D, scalar2=EPS,
                                             op0=ALU.mult, op1=ALU.add), reads=[bst], writes=[bst])
    P.op("gpsimd", lambda e: e.tensor_tensor(out=st[:, 2:3], in0=st[:, 1:2], in1=NEGH[:], op=ALU.pow),
         reads=[bst, bC], writes=[bst])
    P.op("scalar", lambda e: e.activation(out=out_ap, in_=x_ap, func=AF.Copy, scale=st[:, 2:3]),
         reads=[bx, bst], writes=[bout])


KOFF = [512, 640, 768, 1024, 1816]
VOFF = [896, 1152, 1944]


def proj_phase(nc, P, h1, g_ap, w_mix, S, T):
    NB = T // 512
    with ExitStack() as es:
        def sb(name, shape, dt):
            return es.enter_context(nc.sbuf_tensor(f"pp_{name}", shape, dt))

        def ps(name, shape, dt):
            return es.enter_context(nc.psum_tensor(f"pp_{name}", shape, dt))

        WM = sb("wm", [128, 8, PROJ_W], BF16)
        WMQ = sb("wmq", [128, 8, 1024], BF16)
        G = sb("g", [128, 8], F32)
        IDENT = sb("ident", [128, 128], BF16)
        NEGH = sb("negh", [128, 1], F32)
        HB = [sb(f"hb{i}", [128, 4, D], F32) for i in range(2)]
        XN = [sb(f"xn{i}", [128, D], BF16) for i in range(4)]
        XNT = [sb(f"xnt{i}", [128, 8, 512], BF16) for i in range(2)]
        SQ = sb("sq", [128, D], BF16)
        ST = [sb(f"st{i}", [128, 4], F32) for i in range(4)]
        QST = [sb(f"qst{i}", [128, 4, 4, 128], BF16) for i in range(2)]
        QBST = [sb(f"qbst{i}", [128, 4, 4, 128], BF16) for i in range(2)]
        KST = [sb(f"kst{i}", [128, 5, 512], BF16) for i in range(2)]
        VST = [sb(f"vst{i}", [128, 4, 3, 2, 65], BF16) for i in range(2)]
        GST = [sb(f"gst{i}", [128, 4, 24], F32) for i in range(2)]
        TPS = [ps(f"tps{i}", [128, 2, 512], BF16) for i in range(2)]
        PJ = [ps(f"pj{i}", [128, 512], F32) for i in range(3)]
        PT = [ps(f"pt{i}", [128, 512], F32) for i in range(2)]

        bWM = [P.buf(f"wm{k}") for k in range(8)]
        bWMQ = [P.buf(f"wmq{k}") for k in range(8)]
        bG = P.buf("g"); bC = P.buf("c")
        bHB = [P.buf("hb") for i in range(2)]
        bXN = [P.buf("xn") for i in range(4)]
        bXNT = [P.buf("xnt") for i in range(2)]
        bSQ = P.buf("sq")
        bST = [P.buf("st") for i in range(4)]
        bQST = [P.buf("qst") for i in range(2)]
        bQBST = [P.buf("qbst") for i in range(2)]
        bKST = [P.buf("kst") for i in range(2)]
        bVST = [P.buf("vst") for i in range(2)]
        bGST = [P.buf("gst") for i in range(2)]
        bTPS = [P.buf("tps") for i in range(2)]
        bPJ = [P.buf("pj") for i in range(3)]
        bPT = [P.buf("pt") for i in range(2)]
        bOUT = P.buf("out")

        P.op("gpsimd", lambda e: e.memset(IDENT[:], 1.0), writes=[bC])
        P.op("gpsimd", lambda e: e.affine_select(out=IDENT[:], in_=IDENT[:], pattern=[[-1, 128]],
                                                 compare_op=ALU.is_equal, fill=P.reg(e, 0.0), base=0, channel_multiplier=1),
             writes=[bC])
        P.op("gpsimd", lambda e: e.memset(NEGH[:], -0.5), writes=[bC])
        for i in range(2):
            P.op("gpsimd", lambda e, i=i: e.memset(VST[i][:], 1.0), writes=[bVST[i]])
        P.op("sync", lambda e: e.dma_start(out=G[:], in_=g_ap.rearrange("(c p) -> p c", p=128),
                                           allow_slow_non_contiguous=True), writes=[bG], dma_owner=bG)

        def load(b):
            s = b % 2
            for t in range(4):
                r0 = (4 * b + t) * 128
                P.op("sync", lambda e, t=t, r0=r0: e.dma_start(out=HB[s][:, t, :], in_=h1[r0:r0 + 128, :]),
                     writes=[bHB[s]], dma_owner=bHB[s])

        load(0)
        for k in range(8):
            rows = slice(k * 128, (k + 1) * 128)
            for hh in range(2):
                c0 = hh * 1036
                P.op("gpsimd", lambda e, k=k, rows=rows, c0=c0: e.dma_start(
                    out=WM[:, k, c0:c0 + 1036], in_=w_mix[rows, c0:c0 + 1036]), writes=[bWM[k]], dma_owner=bWM[k])
        for k in range(8):
            for h, base in enumerate((0, 1304)):
                eng = "vector" if h == 0 else "gpsimd"
                P.op(eng, lambda e, k=k, h=h, base=base: e.tensor_copy(
                    out=WMQ[:, k, h * 512:(h + 1) * 512].rearrange("p (r g d) -> p r g d", r=4, g=2),
                    in_=WM[:, k, base:base + 512].rearrange("p (g r d) -> p r g d", g=2, r=4)),
                     reads=[bWM[k]], writes=[bWMQ[k]])
        cnt = {"pj": 0, "pt": 0, "tp": 0}
        def block(b):
            s = b % 2
            if b + 1 < NB:
                load(b + 1)
            for t in range(4):
                norm_ops(P, HB[s][:, t, :], bHB[s], SQ, bSQ, ST[t], bST[t], NEGH, bC, XN[t][:], bXN[t])
            for kp in range(4):
                tp = cnt["tp"] % 2
                cnt["tp"] += 1
                for t in range(4):
                    for kk in range(2):
                        k = 2 * kp + kk
                        P.op("tensor", lambda e, t=t, k=k, kk=kk, tp=tp: e.transpose(
                            out=TPS[tp][:, kk, t * 128:(t + 1) * 128], in_=XN[t][:, k * 128:(k + 1) * 128],
                            identity=IDENT[:]), reads=[bXN[t], bC], writes=[bTPS[tp]])
                for kk in range(2):
                    k = 2 * kp + kk
                    P.op("vector", lambda e, k=k, kk=kk, tp=tp: e.tensor_scalar(
                        out=XNT[s][:, k, :], in0=TPS[tp][:, kk, :], scalar1=G[:, k:k + 1], scalar2=None,
                        op0=ALU.mult), reads=[bTPS[tp], bG], writes=[bXNT[s]])

            def fm_group(colfn, evac, ei, bsrc=bWM):
                pj = cnt["pj"] % 3
                cnt["pj"] += 1
                for k in range(8):
                    P.op("tensor", lambda e, k=k, pj=pj: e.matmul(PJ[pj][:], lhsT=colfn(k), rhs=XNT[s][:, k, :],
                                                                 start=(k == 0), stop=(k == 7)),
                         reads=[bsrc[k], bXNT[s]], writes=[bPJ[pj]])
                evac(pj, ei)

            def qcols(base):
                def f(k):
                    return WMQ[:, k, base:base + 128]
                return f

            ei = 0
            for r in range(4):
                for (base, QS, bQS) in ((0, QST, bQST), (512, QBST, bQBST)):
                    def evac(pj, ei, r=r, QS=QS, bQS=bQS):
                        src = PJ[pj][:].rearrange("p (n t) -> p n t", t=128)
                        if ei % 2 == 0:
                            P.op("scalar", lambda e: e.activation(out=QS[s][:, :, r, :], in_=src, func=AF.Copy,
                                                                  scale=0.125), reads=[bPJ[pj]], writes=[bQS[s]])
                        else:
                            P.op("vector", lambda e: e.tensor_scalar(out=QS[s][:, :, r, :], in0=src, scalar1=0.125,
                                                                     scalar2=None, op0=ALU.mult),
                                 reads=[bPJ[pj]], writes=[bQS[s]])
                    fm_group(qcols(base + r * 128), evac, ei, bsrc=bWMQ)
                    ei += 1
            for w in range(5):
                def evac(pj, ei, w=w):
                    if ei % 2 == 0:
                        P.op("scalar", lambda e: e.copy(out=KST[s][:, w, :], in_=PJ[pj][:]),
                             reads=[bPJ[pj]], writes=[bKST[s]])
                    else:
                        P.op("vector", lambda e: e.tensor_copy(out=KST[s][:, w, :], in_=PJ[pj][:]),
                             reads=[bPJ[pj]], writes=[bKST[s]])
                fm_group(lambda k, w=w: WM[:, k, KOFF[w]:KOFF[w] + 128], evac, ei)
                ei += 1
            for t in range(4):
                pt = cnt["pt"] % 2
                cnt["pt"] += 1
                for (c0, n, o0) in ((896, 128, 0), (1152, 152, 128), (1944, 128, 280)):
                    for k in range(8):
                        P.op("tensor", lambda e, k=k, c0=c0, n=n, o0=o0, pt=pt, t=t: e.matmul(
                            PT[pt][:, o0:o0 + n], lhsT=XNT[s][:, k, t * 128:(t + 1) * 128], rhs=WM[:, k, c0:c0 + n],
                            start=(k == 0), stop=(k == 7)), reads=[bWM[k], bXNT[s]], writes=[bPT[pt]])
                for wi, o0 in enumerate((0, 128, 280)):
                    src = PT[pt][:, o0:o0 + 128].rearrange("p (g d) -> p g d", d=64)
                    if wi == 1:
                        P.op("scalar", lambda e, src=src, wi=wi, t=t: e.copy(out=VST[s][:, t, wi, :, 0:64], in_=src),
                             reads=[bPT[pt]], writes=[bVST[s]])
                    else:
                        P.op("vector", lambda e, src=src, wi=wi, t=t: e.tensor_copy(out=VST[s][:, t, wi, :, 0:64],
                                                                                   in_=src),
                             reads=[bPT[pt]], writes=[bVST[s]])
                P.op("vector", lambda e, t=t, pt=pt: e.tensor_copy(out=GST[s][:, t, :], in_=PT[pt][:, 256:280]),
                     reads=[bPT[pt]], writes=[bGST[s]])
            c0 = b * 2048
            P.op("sync", lambda e, c0=c0: e.dma_start(out=S["QA"][:, c0:c0 + 2048],
                                                      in_=QST[s][:].rearrange("p n r t -> p (n r t)")),
                 reads=[bQST[s]], writes=[bOUT], dma_owner=bQST[s])
            P.op("sync", lambda e, c0=c0: e.dma_start(out=S["QB"][:, c0:c0 + 2048],
                                                      in_=QBST[s][:].rearrange("p n r t -> p (n r t)")),
                 reads=[bQBST[s]], writes=[bOUT], dma_owner=bQBST[s])
            for w in range(5):
                P.op("sync", lambda e, w=w: e.dma_start(out=S["KT"][w, :, b * 512:(b + 1) * 512], in_=KST[s][:, w, :]),
                     reads=[bKST[s]], writes=[bOUT], dma_owner=bKST[s])
            for w in range(3):
                P.op("sync", lambda e, w=w: e.dma_start(out=S["VT"][w, :, 4 * b:4 * b + 4, :, :],
                                                        in_=VST[s][:, :, w, :, :]),
                     reads=[bVST[s]], writes=[bOUT], dma_owner=bVST[s])
            P.op("sync", lambda e: e.dma_start(
                out=S["GT"].rearrange("(n p) c -> p n c", p=128)[:, 4 * b:4 * b + 4, :], in_=GST[s][:]),
                 reads=[bGST[s]], writes=[bOUT], dma_owner=bGST[s])

        for b in range(NB):
            block(b)
        P.emit()


def attn_phase(nc, P, h1, h2, S, C, W, T):
    NT = T // 128
    NCC = T // 16
    NCH = NCC // 128
    with ExitStack() as es:
        def sb(name, shape, dt):
            return es.enter_context(nc.sbuf_tensor(f"at_{name}", shape, dt))

        def ps(name, shape, dt):
            return es.enter_context(nc.psum_tensor(f"at_{name}", shape, dt))

        KG = [[sb(f"kg{br}{g}", [68, T], BF16) for g in range(2)] for br in range(3)]
        VR = [sb(f"vr{br}", [128, NT, 2, 65], BF16) for br in range(3)]
        KCG = [sb(f"kcg{g}", [68, NCC], BF16) for g in range(2)]
        VCX = [sb(f"vcx{g}", [128, NCH, 129], BF16) for g in range(2)]
        RAWT = sb("rawt", [128, 2, T], BF16)
        W1 = [sb(f"w1{i}", [128, 32, 256], BF16) for i in range(2)]
        W2 = [sb(f"w2{i}", [128, 2, 64], BF16) for i in range(2)]
        POSF = [sb(f"posf{i}", [64, 32], F32) for i in range(2)]
        POST = [sb(f"post{i}", [64, 32], BF16) for i in range(2)]
        B1 = [sb(f"b1{i}", [128, 2], F32) for i in range(2)]
        BIAS = [sb(f"bias{i}", [128, 2], F32) for i in range(2)]
        GX = sb("gx", [128, NCC], F32)
        T1 = sb("t1", [128, NCC], F32)
        HTC = [sb(f"htc{i}", [128, NCC], BF16) for i in range(2)]
        WO = sb("wo", [128, 8, D], BF16)
        EBIG = sb("ebig", [128, NT * 128], BF16)
        IDENT = sb("ident", [128, 128], BF16)
        CAUS = sb("caus", [128, 4, 128], BF16)
        CAUSC = sb("causc", [128, 4, 128], BF16)
        ZER = sb("zer", [128, 4, 128], BF16)
        SINKE = sb("sinke", [128, 8, 1], F32)
        QT = [[sb(f"qt{q}{i}", [68, 512], BF16) for i in range(2)] for q in range(4)]
        GATE = [sb(f"gate{i}", [128, 24], F32) for i in range(2)]
        FT = [sb(f"ft{i}", [128, 2, 64], F32) for i in range(2)]
        H1T = [sb(f"h1t{i}", [128, D], F32) for i in range(2)]
        PTT = [sb(f"ptt{i}", [128, 512], BF16) for i in range(4)]
        MASKC = [sb(f"maskc{i}", [128, 4, 128], BF16) for i in range(2)]
        NEGMT = [sb(f"negmt{g}", [128, 4, 128], BF16) for g in range(2)]
        NEGMS = [sb(f"negm{g}", [128, 64], BF16) for g in range(2)]
        OMIX = sb("omix", [128, 16, 64], F32)
        OMIXB = sb("omixb", [128, D], BF16)
        OT = sb("ot", [128, 8, 128], BF16)
        SG = sb("sg", [128, 24], F32)
        OAS = sb("oas", [128, 4, 129], F32)
        L4 = sb("l4", [128, 4, 1], F32)
        RL4 = sb("rl4", [128, 4, 1], F32)
        W4 = sb("w4", [128, 4, 1], F32)
        TMP = sb("tmp", [128, 4, 64], F32)
        PSLC = sb("pslc", [128, 64], F32)
        SCORE = sb("score", [128, 64], F32)
        WORK = sb("work", [128, 64], F32)
        M1 = sb("m1", [128, 8], F32)
        M2 = sb("m2", [128, 8], F32)

        NSTP = 3
        DEPTH = 3
        STP = [ps(f"stp{i}", [128, 512], F32) for i in range(NSTP)]
        OA = ps("oa", [128, 4, 512], F32)
        MF = ps("mf", [128, 512], F32)
        MBV = MF.bitcast(BF16)

        def MBc(c):
            return MBV[:, c * 128:(c + 1) * 128]

        B = P.buf
        bKG = [[B("kg") for g in range(2)] for br in range(3)]
        bVR = [B("vr") for br in range(3)]
        bKCG = [B("kcg") for g in range(2)]
        bVCX = [B("vcx") for g in range(2)]
        bRAWT = B("rawt"); bW1 = [B("w1"), B("w1")]; bW2 = [B("w2"), B("w2")]
        bPOSF = [B("pf"), B("pf")]; bPOST = [B("pt"), B("pt")]; bB1 = [B("b1"), B("b1")]; bBIAS = [B("bi"), B("bi")]
        bGX = B("gx"); bT1 = B("t1"); bHTC = [B("htc"), B("htc")]
        bWO = B("wo"); bC = B("c"); bSINK = B("sink")
        bQT = [[B("qt") for i in range(2)] for q in range(4)]
        bGATE = [B("gate"), B("gate")]; bFT = [B("ft"), B("ft")]; bH1T = [B("h1t"), B("h1t")]
        bPTT = [B("ptt") for i in range(4)]
        bMASKC = [B("mc"), B("mc")]
        bNEGMT = [B("nm"), B("nm")]; bNEGMS = [B("negm"), B("negm")]
        bOMIX = B("omix"); bOMIXB = B("omixb"); bOT = B("ot"); bSG = B("sg")
        bOAS = B("oas"); bL4 = B("l4"); bTMP = B("tmp"); bPSLC = B("pslc"); bSCORE = B("score"); bWORK = B("work")
        bM = B("m")
        bSTP = [B("stp") for _ in range(3)]; bOA = B("oa"); bMF = B("mf"); bMB = bMF
        bH2 = B("h2")

        P.op("gpsimd", lambda e: e.memset(IDENT[:], 1.0), writes=[bC])
        P.op("gpsimd", lambda e: e.affine_select(out=IDENT[:], in_=IDENT[:], pattern=[[-1, 128]],
                                                 compare_op=ALU.is_equal, fill=P.reg(e, 0.0), base=0, channel_multiplier=1),
             writes=[bC])
        P.op("gpsimd", lambda e: e.memset(EBIG[:], 1.0), writes=[bC])
        P.op("gpsimd", lambda e: e.affine_select(out=EBIG[:], in_=EBIG[:], pattern=[[-1, NT * 2], [0, 64]],
                                                 compare_op=ALU.is_equal, fill=P.reg(e, 0.0), base=0, channel_multiplier=1),
             writes=[bC])
        P.op("gpsimd", lambda e: e.memset(ZER[:], 0.0), writes=[bC])
        P.op("gpsimd", lambda e: e.memset(CAUS[:], NEG), writes=[bC])
        P.op("gpsimd", lambda e: e.affine_select(out=CAUS[:], in_=CAUS[:], pattern=[[0, 4], [-1, 128]],
                                                 compare_op=ALU.is_gt, fill=P.reg(e, 0.0), base=0, channel_multiplier=1),
             writes=[bC])
        P.op("gpsimd", lambda e: e.memset(CAUSC[:], NEG), writes=[bC])
        P.op("gpsimd", lambda e: e.affine_select(out=CAUSC[:], in_=CAUSC[:], pattern=[[0, 4], [1, 128]],
                                                 compare_op=ALU.is_ge, fill=P.reg(e, 0.0), base=0, channel_multiplier=-1),
             writes=[bC])
        for g in range(2):
            P.op("gpsimd", lambda e, g=g: e.memset(NEGMT[g][:], 0.0), writes=[bNEGMT[g]])
            P.op("gpsimd", lambda e, g=g: e.memset(KCG[g][:], 0.0), writes=[bKCG[g]])
            P.op("gpsimd", lambda e, g=g: e.memset(VCX[g][:], 0.0), writes=[bVCX[g]])
            P.op("gpsimd", lambda e, g=g: e.memset(VCX[g][:, :, 64:65], 1.0), writes=[bVCX[g]])

        P.op("sync", lambda e: e.dma_start(out=RAWT[:, 0, :], in_=S["KT"][0, :, :]), writes=[bRAWT], dma_owner=bRAWT)
        P.op("sync", lambda e: e.dma_start(out=RAWT[:, 1, :], in_=S["KT"][1, :, :]), writes=[bRAWT], dma_owner=bRAWT)
        for i in range(2):
            w1 = W["w1"][i].rearrange("(j d) h -> d j h", d=64)
            for half in range(2):
                for jj in range(4):
                    P.op("gpsimd", lambda e, i=i, half=half, jj=jj, w1=w1: e.dma_start(
                        out=W1[i][half * 64:(half + 1) * 64, jj * 8:(jj + 1) * 8, :], in_=w1[:, jj * 8:(jj + 1) * 8, :]),
                         writes=[bW1[i]], dma_owner=bW1[i])
            P.op("gpsimd", lambda e, i=i: e.dma_start(out=W2[i][:], in_=W["w2"][i].rearrange("(c p) d -> p c d", p=128)),
                 writes=[bW2[i]], dma_owner=bW2[i])
            P.op("sync", lambda e, i=i: e.dma_start(out=POSF[i][:], in_=W["pos"][i].rearrange("j d -> d j"),
                                                    allow_slow_non_contiguous=True), writes=[bPOSF[i]], dma_owner=bPOSF[i])
            P.op("sync", lambda e, i=i: e.dma_start(out=B1[i][:], in_=W["b1"][i].rearrange("(c p) -> p c", p=128),
                                                    allow_slow_non_contiguous=True), writes=[bB1[i]], dma_owner=bB1[i])
            P.op("vector", lambda e, i=i: e.tensor_copy(out=POST[i][:], in_=POSF[i][:]), reads=[bPOSF[i]],
                 writes=[bPOST[i]])
        for g in range(2):
            P.op("sync", lambda e, g=g: e.dma_start(out=KCG[g][64:68, :], in_=C["kaugc"][:, :]),
                 writes=[bKCG[g]], dma_owner=bKCG[g])
            P.op("sync", lambda e, g=g: e.dma_start(out=VCX[g][:, :, 65:129],
                                                    in_=C["ovl"].rearrange("(n p) s -> p n s", p=128)),
                 writes=[bVCX[g]], dma_owner=bVCX[g])
        for br, w in enumerate((2, 3, 4)):
            for g in range(2):
                P.op("sync", lambda e, br=br, w=w, g=g: e.dma_start(out=KG[br][g][0:64, :],
                                                                   in_=S["KT"][w, g * 64:(g + 1) * 64, :]),
                     writes=[bKG[br][g]], dma_owner=bKG[br][g])
                P.op("sync", lambda e, br=br, g=g: e.dma_start(out=KG[br][g][64:68, :], in_=C["kaug"][:, :]),
                     writes=[bKG[br][g]], dma_owner=bKG[br][g])
            P.op("sync", lambda e, br=br: e.dma_start(out=VR[br][:].rearrange("p n g d -> p (n g d)"),
                                                      in_=S["VT"][br].rearrange("p n g d -> p (n g d)")),
                 writes=[bVR[br]], dma_owner=bVR[br])
        for k in range(8):
            P.op("gpsimd", lambda e, k=k: e.dma_start(out=WO[:, k, :], in_=W["wo"][k * 128:(k + 1) * 128, :]),
                 writes=[bWO], dma_owner=bWO)
        P.op("sync", lambda e: e.dma_start(out=SINKE[:, :, 0], in_=W["sinks"].partition_broadcast(128)),
             writes=[bSINK], dma_owner=bSINK)
        P.op("scalar", lambda e: e.activation(out=SINKE[:], in_=SINKE[:], func=AF.Exp), reads=[bSINK], writes=[bSINK])

        def loads(i):
            s = i % 2
            for q, (src, g) in enumerate((("QA", 0), ("QA", 1), ("QB", 0), ("QB", 1))):
                P.op("sync", lambda e, q=q, src=src, g=g: e.dma_start(
                    out=QT[q][s][0:64, :], in_=S[src][g * 64:(g + 1) * 64, i * 512:(i + 1) * 512]),
                     writes=[bQT[q][s]], dma_owner=bQT[q][s])
                P.op("sync", lambda e, q=q, g=g: e.dma_start(out=QT[q][s][64:68, :],
                                                             in_=C["qaug"][g, :, i * 512:(i + 1) * 512]),
                     writes=[bQT[q][s]], dma_owner=bQT[q][s])
            P.op("sync", lambda e: e.dma_start(out=GATE[s][:], in_=S["GT"][i * 128:(i + 1) * 128, :]),
                 writes=[bGATE[s]], dma_owner=bGATE[s])
            P.op("sync", lambda e: e.dma_start(out=FT[s][:, 0, :], in_=C["ftab"][i, :, :]),
                 writes=[bFT[s]], dma_owner=bFT[s])
            P.op("sync", lambda e: e.dma_start(out=H1T[s][:], in_=h1[i * 128:(i + 1) * 128, :]),
                 writes=[bH1T[s]], dma_owner=bH1T[s])

        loads(0)

        cnt = {"st": 0, "pt": 0, "mc": 0}

        def st_next():
            v = cnt["st"] % 3
            cnt["st"] += 1
            return v

        for i in range(2):
            for hc in range(2):
                for j in range(32):
                    P.op("tensor", lambda e, i=i, hc=hc, j=j: e.matmul(
                        MF[:, hc:hc + 1], lhsT=W1[i][0:64, j, hc * 128:(hc + 1) * 128], rhs=POST[i][:, j:j + 1],
                        start=(j == 0), stop=(j == 31)), reads=[bW1[i], bPOST[i]], writes=[bMF])
            P.op("vector", lambda e, i=i: e.tensor_tensor(out=BIAS[i][:], in0=MF[:, 0:2], in1=B1[i][:], op=ALU.add),
                 reads=[bMF, bB1[i]], writes=[bBIAS[i]])
            for g in range(2):
                for hc in range(2):
                    sp = st_next()
                    for j in range(32):
                        P.op("tensor", lambda e, i=i, g=g, hc=hc, j=j, sp=sp: e.matmul(
                            STP[sp][:, 0:NCC - 1], lhsT=W1[i][g * 64:(g + 1) * 64, j, hc * 128:(hc + 1) * 128],
                            rhs=RAWT[g * 64:(g + 1) * 64, i, j:j + 16 * (NCC - 2) + 1:16],
                            start=(j == 0), stop=(j == 31)), reads=[bW1[i], bRAWT], writes=[bSTP[sp]])
                    n = NCC - 1
                    P.op("scalar", lambda e, i=i, hc=hc, sp=sp: e.activation(
                        out=GX[:, 0:n], in_=STP[sp][:, 0:n], func=AF.Identity, bias=BIAS[i][:, hc:hc + 1]),
                         reads=[bSTP[sp], bBIAS[i]], writes=[bGX])
                    P.op("vector", lambda e: e.tensor_tensor(out=T1[:, 0:n], in0=GX[:, 0:n], in1=GX[:, 0:n], op=ALU.mult),
                         reads=[bGX], writes=[bT1])
                    P.op("vector", lambda e: e.tensor_scalar(out=T1[:, 0:n], in0=T1[:, 0:n], scalar1=0.044715,
                                                             scalar2=1.0, op0=ALU.mult, op1=ALU.add),
                         reads=[bT1], writes=[bT1])
                    P.op("vector", lambda e: e.tensor_tensor(out=T1[:, 0:n], in0=T1[:, 0:n], in1=GX[:, 0:n], op=ALU.mult),
                         reads=[bT1, bGX], writes=[bT1])
                    P.op("scalar", lambda e: e.activation(out=T1[:, 0:n], in_=T1[:, 0:n], func=AF.Exp,
                                                          scale=-1.5957691216057308), reads=[bT1], writes=[bT1])
                    P.op("vector", lambda e: e.tensor_scalar(out=T1[:, 0:n], in0=T1[:, 0:n], scalar1=1.0, scalar2=None,
                                                             op0=ALU.add), reads=[bT1], writes=[bT1])
                    P.op("vector", lambda e: e.reciprocal(out=T1[:, 0:n], in_=T1[:, 0:n]), reads=[bT1], writes=[bT1])
                    P.op("gpsimd", lambda e, hc=hc: e.memset(HTC[hc][:, n:NCC], 0.0), writes=[bHTC[hc]])
                    P.op("vector", lambda e, hc=hc: e.tensor_tensor(out=HTC[hc][:, 0:n], in0=T1[:, 0:n], in1=GX[:, 0:n],
                                                                    op=ALU.mult), reads=[bT1, bGX], writes=[bHTC[hc]])
                if i == 0:
                    for hc in range(2):
                        P.op("tensor", lambda e, hc=hc: e.matmul(MF[0:64, 0:NCC], lhsT=W2[0][:, hc, :], rhs=HTC[hc][:],
                                                                 start=(hc == 0), stop=(hc == 1)),
                             reads=[bW2[0], bHTC[hc]], writes=[bMF])
                    P.op("vector", lambda e, g=g: e.tensor_copy(out=KCG[g][0:64, :], in_=MF[0:64, 0:NCC]),
                         reads=[bMF], writes=[bKCG[g]])
                else:
                    for cc in range(NCH):
                        for hc in range(2):
                            P.op("tensor", lambda e, hc=hc, cc=cc: e.matmul(
                                MF[:, 0:64], lhsT=HTC[hc][:, cc * 128:(cc + 1) * 128], rhs=W2[1][:, hc, :],
                                start=(hc == 0), stop=(hc == 1)), reads=[bW2[1], bHTC[hc]], writes=[bMF])
                        P.op("vector", lambda e, g=g, cc=cc: e.tensor_copy(out=VCX[g][:, cc, 0:64], in_=MF[:, 0:64]),
                             reads=[bMF], writes=[bVCX[g]])

        def unit(i, kt_ap, bk, q, s, extra, vt_ap, bv, first, last, width):
            sp = st_next()
            pt = cnt["pt"] % 4
            cnt["pt"] += 1
            nmm = 1 + len(extra)
            P.op("tensor", lambda e: e.matmul(STP[sp][:], lhsT=kt_ap, rhs=QT[q][s][:], start=True, stop=(nmm == 1)),
                 reads=[bk, bQT[q][s]], writes=[bSTP[sp]])
            for n, (l_ap, r_ap, rb) in enumerate(extra):
                P.op("tensor", lambda e, l_ap=l_ap, r_ap=r_ap, n=n: e.matmul(
                    STP[sp][:], lhsT=l_ap, rhs=r_ap, start=False, stop=(n == nmm - 2)),
                     reads=[bC] + rb, writes=[bSTP[sp]])
            P.op("scalar", lambda e: e.activation(out=PTT[pt][:], in_=STP[sp][:], func=AF.Exp),
                 reads=[bSTP[sp]], writes=[bPTT[pt]])
            return pt

        pend = []

        def flush():
            for f in pend:
                f()
            del pend[:]

        def pv_now(pt, vt_ap, bv, first, last, width):
            for r in range(4):
                P.op("tensor", lambda e, r=r: e.matmul(OA[:, r, 0:width], lhsT=PTT[pt][:, r * 128:(r + 1) * 128],
                                                       rhs=vt_ap, start=first, stop=last),
                     reads=[bPTT[pt], bv], writes=[bOA])

        def pv(pt, vt_ap, bv, first, last, width):
            prev = list(pend)
            del pend[:]
            pend.append(lambda: pv_now(pt, vt_ap, bv, first, last, width))
            for f in prev:
                f()

        def flat(t3):
            return t3[:].rearrange("p r t -> p (r t)")

        def finish(col0, gate_col, first_branch, sink_g=None, width=65, need_rl=False):
            P.op("vector", lambda e: e.tensor_copy(out=OAS[:, :, 0:width], in_=OA[:, :, 0:width]),
                 reads=[bOA], writes=[bOAS])
            if need_rl:
                P.op("vector", lambda e: e.tensor_scalar(out=L4[:], in0=OAS[:, :, 64:65], scalar1=1e-30, scalar2=None,
                                                         op0=ALU.max), reads=[bOAS], writes=[bL4])
                P.op("vector", lambda e: e.reciprocal(out=RL4[:], in_=L4[:]), reads=[bL4], writes=[bL4])
                P.op("vector", lambda e: e.tensor_tensor(out=W4[:, :, 0], in0=RL4[:, :, 0],
                                                         in1=SG[:, gate_col:gate_col + 10:3], op=ALU.mult),
                     reads=[bL4, bSG], writes=[bL4])
            elif sink_g is not None:
                P.op("vector", lambda e: e.tensor_tensor(out=L4[:], in0=OAS[:, :, 64:65],
                                                         in1=SINKE[:, sink_g * 4:sink_g * 4 + 4, :], op=ALU.add),
                     reads=[bOAS, bSINK], writes=[bL4])
                P.op("vector", lambda e: e.reciprocal(out=W4[:], in_=L4[:]), reads=[bL4], writes=[bL4])
            else:
                P.op("vector", lambda e: e.reciprocal(out=RL4[:], in_=OAS[:, :, 64:65]), reads=[bOAS], writes=[bL4])
                P.op("vector", lambda e: e.tensor_tensor(out=W4[:, :, 0], in0=RL4[:, :, 0],
                                                         in1=SG[:, gate_col:gate_col + 10:3], op=ALU.mult),
                     reads=[bL4, bSG], writes=[bL4])
            wb = apx(W4[:, :, 0], [[0, 64]])
            if first_branch:
                P.op("vector", lambda e: e.tensor_tensor(out=OMIX[:, col0:col0 + 4, :], in0=OAS[:, :, 0:64], in1=wb,
                                                         op=ALU.mult), reads=[bOAS, bL4], writes=[bOMIX])
            else:
                P.op("vector", lambda e: e.tensor_tensor(out=TMP[:], in0=OAS[:, :, 0:64], in1=wb, op=ALU.mult),
                     reads=[bOAS, bL4], writes=[bTMP])
                P.op("vector", lambda e: e.tensor_tensor(out=OMIX[:, col0:col0 + 4, :], in0=OMIX[:, col0:col0 + 4, :],
                                                         in1=TMP[:], op=ALU.add), reads=[bTMP, bOMIX], writes=[bOMIX])

        items = []

        def U(qk_fn, pv_fn, needs=None):
            items.append(("u", qk_fn, pv_fn, needs))

        def Bar(fn, provides=None, releases=0):
            items.append(("b", fn, provides, releases))

        def tile(i):
            s = i % 2

            def tile_start():
                P.op("scalar", lambda e: e.activation(out=SG[:], in_=GATE[s][:], func=AF.Exp, scale=-1.0),
                     reads=[bGATE[s]], writes=[bSG])
                P.op("vector", lambda e: e.tensor_scalar(out=SG[:], in0=SG[:], scalar1=1.0, scalar2=None, op0=ALU.add),
                     reads=[bSG], writes=[bSG])
                P.op("vector", lambda e: e.reciprocal(out=SG[:], in_=SG[:]), reads=[bSG], writes=[bSG])
            Bar(tile_start)

            def std_branch(br, g, q, j0, far, col0, gate_col, first_branch, sink_g=None, needs=None):
                for j in range(j0, i + 1):
                    def qk(j=j):
                        extra = []
                        if br == 0:
                            extra.append((EBIG[:, j * 128:(j + 1) * 128], flat(NEGMT[g]), [bNEGMT[g]]))
                        if j == i:
                            extra.append((IDENT[:], flat(CAUS), []))
                        if far is not None and j == i - far:
                            extra.append((IDENT[:], flat(CAUSC), []))
                        return unit(i, KG[br][g][:, j * 128:(j + 1) * 128], bKG[br][g], q, s, extra, None, None, 0, 0, 0)

                    def pvf(pt, j=j):
                        pv_now(pt, VR[br][:, j, g, :], bVR[br], j == j0, j == i, 65)
                    U(qk, pvf, needs)
                Bar(lambda: finish(col0, gate_col, first_branch, sink_g=sink_g))

            def cmp_group(g):
                q = g
                nch = min(NCH, (8 * i + 6) // 128 + 1)
                pts = []
                for cc in range(nch):
                    def qk(cc=cc):
                        shift = 128 * cc - 8 * i
                        extra = []
                        if shift + 127 >= -1:
                            m = cnt["mc"] % 2
                            cnt["mc"] += 1
                            P.op("gpsimd", lambda e, m=m, shift=shift: e.affine_select(
                                out=MASKC[m][:], in_=ZER[:], pattern=[[0, 4], [1, 128]], compare_op=ALU.is_ge,
                                fill=P.reg(e, NEG), base=-31 - 16 * shift, channel_multiplier=-16),
                                 reads=[bC], writes=[bMASKC[m]])
                            extra.append((IDENT[:], flat(MASKC[m]), [bMASKC[m]]))
                        pt = unit(i, KCG[g][:, cc * 128:(cc + 1) * 128], bKCG[g], q, s, extra, None, None, 0, 0, 0)
                        pts.append(pt)
                        return pt
                    U(qk, None)

                def cmp_finish():
                    for r in range(4):
                        for cc in range(nch):
                            P.op("tensor", lambda e, r=r, cc=cc: e.matmul(
                                OA[:, r, 0:129], lhsT=PTT[pts[cc]][:, r * 128:(r + 1) * 128], rhs=VCX[g][:, cc, :],
                                start=(cc == 0), stop=(cc == nch - 1)), reads=[bPTT[pts[cc]], bVCX[g]], writes=[bOA])
                    finish(g * 4, g * 12 + 0, True, width=129, need_rl=True)
                    P.op("vector", lambda e: e.tensor_tensor(out=TMP[:], in0=OAS[:, :, 65:129],
                                                             in1=apx(RL4[:, :, 0], [[0, 64]]), op=ALU.mult),
                         reads=[bOAS, bL4], writes=[bTMP])
                    P.op("vector", lambda e: e.tensor_tensor(out=TMP[:, 0:2, :], in0=TMP[:, 0:2, :], in1=TMP[:, 2:4, :],
                                                             op=ALU.add), reads=[bTMP], writes=[bTMP])
                    P.op("vector", lambda e: e.tensor_tensor(out=PSLC[:], in0=TMP[:, 0, :], in1=TMP[:, 1, :], op=ALU.add),
                         reads=[bTMP], writes=[bPSLC])
                    P.op("vector", lambda e: e.tensor_tensor(out=SCORE[:], in0=PSLC[:], in1=FT[s][:, 0, :], op=ALU.add),
                         reads=[bPSLC, bFT[s]], writes=[bSCORE])
                    P.op("vector", lambda e: e.max(out=M1[:], in_=SCORE[:]), reads=[bSCORE], writes=[bM])
                    P.op("vector", lambda e: e.match_replace(out=WORK[:], in_to_replace=M1[:], in_values=SCORE[:],
                                                             imm_value=-3.0e38), reads=[bSCORE, bM], writes=[bWORK])
                    P.op("vector", lambda e: e.max(out=M2[:], in_=WORK[:]), reads=[bWORK], writes=[bM])
                    P.op("vector", lambda e: e.tensor_scalar(out=NEGMS[g][:], in0=SCORE[:], scalar1=M2[:, 7:8],
                                                             scalar2=NEG, op0=ALU.is_lt, op1=ALU.mult),
                         reads=[bSCORE, bM], writes=[bNEGMS[g]])
                return cmp_finish, nch

            def negmt_make(g):
                def f():
                    P.op("tensor", lambda e: e.transpose(out=MBc(0)[0:64, :], in_=NEGMS[g][:], identity=IDENT[:]),
                         reads=[bNEGMS[g], bC], writes=[bMB])
                    for r in range(4):
                        P.op("scalar", lambda e, r=r: e.copy(out=NEGMT[g][0:64, r, :], in_=MBc(0)[0:64, :]),
                             reads=[bMB], writes=[bNEGMT[g]])
                Bar(f, provides=("negmt", i, g))

            fins = [cmp_group(g) for g in range(2)]
            if i > 0:
                Bar(prev_out[0])
            for (fn, nrel) in fins:
                Bar(fn, releases=nrel)
            if i > 0:
                Bar(prev_out[1])
            for g in range(2):
                std_branch(1, g, g, max(0, i - 4), 4, g * 4, g * 12 + 2, False)
            if i > 0:
                Bar(prev_out[2])
            if i + 1 < NT:
                Bar(lambda: loads(i + 1))
            for g in range(2):
                std_branch(2, g, 2 + g, max(0, i - 1), 1, 8 + g * 4, None, True, sink_g=g)
            for g in range(2):
                negmt_make(g)
                std_branch(0, g, g, 0, None, g * 4, g * 12 + 1, False, needs=("negmt", i, g))

            def out_a():
                P.op("scalar", lambda e: e.copy(out=OMIXB[:], in_=OMIX[:].rearrange("p h d -> p (h d)")),
                     reads=[bOMIX], writes=[bOMIXB])
                for c in range(8):
                    P.op("tensor", lambda e, c=c: e.transpose(out=MBc(c), in_=OMIXB[:, c * 128:(c + 1) * 128],
                                                              identity=IDENT[:]), reads=[bOMIXB, bC], writes=[bMB])
                P.op("vector", lambda e: e.tensor_copy(out=OT[:], in_=MBV[:, :].rearrange("p (c t) -> p c t", c=8)),
                     reads=[bMB], writes=[bOT])

            def out_half(n):
                def f():
                    for c in range(8):
                        P.op("tensor", lambda e, c=c: e.matmul(MF[:], lhsT=OT[:, c, :],
                                                              rhs=WO[:, c, n * 512:(n + 1) * 512],
                                                              start=(c == 0), stop=(c == 7)),
                             reads=[bOT, bWO], writes=[bMF])
                    P.op("vector", lambda e: e.tensor_tensor(out=H1T[s][:, n * 512:(n + 1) * 512], in0=MF[:],
                                                             in1=H1T[s][:, n * 512:(n + 1) * 512], op=ALU.add),
                         reads=[bMF, bH1T[s]], writes=[bH1T[s]])
                    if n == 1:
                        P.op("sync", lambda e: e.dma_start(out=h2[i * 128:(i + 1) * 128, :], in_=H1T[s][:]),
                             reads=[bH1T[s]], writes=[bH2], dma_owner=bH1T[s])
                return f
            stages = (out_a, out_half(0), out_half(1))
            if i == NT - 1:
                for st in stages:
                    Bar(st)
            return stages

        prev_out = None
        for i in range(NT):
            prev_out = tile(i)

        n_items = len(items)
        issued = [False] * n_items
        ptsl = [None] * n_items
        provided = set()
        inflight = 0
        for k in range(n_items):
            it = items[k]
            if it[0] == "b":
                it[1]()
                if it[2] is not None:
                    provided.add(it[2])
                inflight -= it[3]
                continue
            if not issued[k]:
                ptsl[k] = it[1]()
                issued[k] = True
                inflight += 1
            m = k + 1
            while m < n_items and m - k <= 8 and inflight < DEPTH:
                im = items[m]
                if im[0] == "u" and not issued[m]:
                    if im[3] is not None and im[3] not in provided:
                        break
                    ptsl[m] = im[1]()
                    issued[m] = True
                    inflight += 1
                m += 1
            if it[2] is not None:
                it[2](ptsl[k])
                inflight -= 1
        P.emit()


def make_consts(T):
    NT = T // 128
    NCC = T // 16
    bf = ml_dtypes.bfloat16
    k = np.arange(T)
    kaug = np.stack([np.ones(T), k % 128, np.ones(T), k // 128]).astype(np.float32)
    pc = 16 * np.arange(NCC) + 31
    kaugc = np.stack([np.ones(NCC), pc % 128, np.ones(NCC), pc // 128]).astype(np.float32)
    qaug = np.zeros((2, 4, NT, 4, 128), np.float32)
    tl = np.arange(128)
    for g in range(2):
        for r in range(4):
            sl = SLOPES[g * 4 + r]
            for i in range(NT):
                qaug[g, 0, i, r] = -sl * tl
                qaug[g, 1, i, r] = sl
                qaug[g, 2, i, r] = -sl * 128.0 * i
                qaug[g, 3, i, r] = sl * 128.0
    qaug = qaug.reshape(2, 4, NT * 512)
    ns = T // 64
    t = np.arange(T)
    cur = t // 64
    blk = np.arange(64)
    ftab = np.zeros((T, 64), np.float32)
    valid = blk[None, :] <= cur[:, None]
    forced = (blk[None, :] == 0) | (blk[None, :] == cur[:, None]) | (blk[None, :] == cur[:, None] - 1)
    ftab[~valid] = -1e30
    ftab[forced] = 1e9
    ftab[:, ns:] = -1e30
    ftab = ftab.reshape(NT, 128, 64)
    c_start = np.arange(NCC) * 16
    s_start = np.arange(64) * 64
    ovl = np.clip(np.minimum(c_start[:, None] + 32, s_start[None, :] + 64)
                  - np.maximum(c_start[:, None], s_start[None, :]), 0, None) / 32.0
    ovl[NCC - 1, :] = 0.0
    return {"kaug": kaug.astype(bf), "kaugc": kaugc.astype(bf), "qaug": qaug.astype(bf), "ftab": ftab,
            "ovl": ovl.astype(np.float32).astype(bf)}


def build(T, debug=False, phases=("fa", "pp", "at", "fc")):
    NT = T // 128
    nc = bass.Bass("TRN2", target_bir_lowering=False)

    def din(name, shape, dt=F32):
        return nc.dram_tensor(name, list(shape), dt, kind="ExternalInput").ap()

    x = din("x", [T, D])
    ffn1_norm = din("ffn1_norm", [D])
    ffn1_w_in = din("ffn1_w_in", [D, 2 * FF])
    ffn1_w_out = din("ffn1_w_out", [FF, D])
    ffn2_norm = din("ffn2_norm", [D])
    ffn2_w_in = din("ffn2_w_in", [D, 2 * FF])
    ffn2_w_out = din("ffn2_w_out", [FF, D])
    final_norm = din("final_norm", [D])
    y = nc.dram_tensor("y", [T, D], F32, kind="ExternalOutput").ap()
    mix_norm = din("mix_norm", [D])
    w_mix_in = din("w_mix_in", [D, PROJ_W])
    w_mix_out = din("w_mix_out", [D, D])
    swa_sinks = din("swa_sinks", [8])
    Wd = {"w1": [din("cmp_k_w1", [2048, 256]), din("cmp_v_w1", [2048, 256])],
          "w2": [din("cmp_k_w2", [256, 64]), din("cmp_v_w2", [256, 64])],
          "pos": [din("cmp_k_pos", [32, 64]), din("cmp_v_pos", [32, 64])],
          "b1": [din("cmp_k_b1", [256]), din("cmp_v_b1", [256])],
          "wo": w_mix_out, "sinks": swa_sinks}
    NCC = T // 16
    Cd = {"kaug": din("c_kaug", [4, T], BF16), "kaugc": din("c_kaugc", [4, NCC], BF16),
          "qaug": din("c_qaug", [2, 4, NT * 512], BF16), "ftab": din("c_ftab", [NT, 128, 64]),
          "ovl": din("c_ovl", [NCC, 64], BF16)}
    kind = "ExternalOutput" if debug else "Internal"

    def scr(name, shape, dt):
        return nc.dram_tensor(name, list(shape), dt, kind=kind).ap()

    h1 = scr("h1s", [T, D], F32)
    h2 = scr("h2s", [T, D], F32)
    Sd = {"QA": scr("s_qa", [128, NT * 512], BF16), "QB": scr("s_qb", [128, NT * 512], BF16),
          "KT": scr("s_kt", [5, 128, T], BF16), "VT": scr("s_vt", [3, 128, NT, 2, 65], BF16),
          "GT": scr("s_gt", [T, 24], F32)}

    with ExitStack() as es:
        sems = [es.enter_context(nc.semaphore(f"s{i}")) for i in range(88)]
        P = Prog(nc, sems)
        if "fa" in phases:
            ffn_phase(nc, P, "fa", x, h1, ffn1_norm, ffn1_w_in, ffn1_w_out, T)
        if "pp" in phases:
            proj_phase(nc, P, h1, mix_norm, w_mix_in, Sd, T)
        if "at" in phases:
            attn_phase(nc, P, h1, h2, Sd, Cd, Wd, T)
        if "fc" in phases:
            ffn_phase(nc, P, "fc", h2, y, ffn2_norm, ffn2_w_in, ffn2_w_out, T, final_g=final_norm)
    return nc


_CACHE = {}


def kernel(**inputs):
    T = inputs["x"].shape[1]
    nb = inputs["x"].shape[0]
    if T not in _CACHE:
        _CACHE[T] = (build(T), make_consts(T))
    nc, consts = _CACHE[T]
    shared = {}
    for k, v in inputs.items():
        if k == "x":
            continue
        a = np.ascontiguousarray(np.asarray(v, dtype=np.float32))
        if k != "final_norm":
            a = a[0]
        shared[k] = np.ascontiguousarray(a)
    for k, v in consts.items():
        shared["c_" + k] = v
    xs = np.asarray(inputs["x"], dtype=np.float32)
    in_maps = []
    for b in range(nb):
        m = dict(shared)
        m["x"] = np.ascontiguousarray(xs[b])
        in_maps.append(m)
    res = run_bass_kernel_spmd(nc, in_maps, core_ids=list(range(nb)))
    return np.stack([np.asarray(r["y"], dtype=np.float32) for r in res.results], axis=0)
```

```python
import numpy as np
import ml_dtypes
import concourse.bass as bass
import concourse.mybir as mybir
from concourse.bass_utils import run_bass_kernel_spmd
from contextlib import ExitStack

F32 = mybir.dt.float32
BF16 = mybir.dt.bfloat16
AF = mybir.ActivationFunctionType
ALU = mybir.AluOpType

D = 1024
FF = 2816
NJ = FF // 128
PROJ_W = 2072
NEG = -30000.0
EPS = 1e-6
SLOPES = [2.0 ** (-(i + 1)) for i in range(8)]
SAME_ENGINE_SYNC = True
EPOCH = 100000


class Sem:
    def __init__(self, h):
        self.h = h
        self.count = 0


class Buf:
    __slots__ = ("name", "w", "r", "sem")

    def __init__(self, name):
        self.name = name
        self.w = []
        self.r = []
        self.sem = None


class Op:
    __slots__ = ("eng", "fn", "deps", "dma", "sem", "val", "signal")

    def __init__(self, eng, fn, dma):
        self.eng = eng
        self.fn = fn
        self.deps = []
        self.dma = dma
        self.sem = None
        self.val = 0
        self.signal = False


ENGS = ["tensor", "vector", "scalar", "gpsimd", "sync"]


class Prog:
    def __init__(self, nc, sems):
        self.nc = nc
        self.free_sems = [Sem(h) for h in sems]
        self.eng_sem = {}
        self.eng_cnt = {}
        for e in ENGS:
            self.eng_sem[e] = self.free_sems.pop()
            self.eng_cnt[e] = 0
        self.waited = {e: {} for e in ENGS}
        self.ops = {e: [] for e in ENGS}
        self.pending = {e: [] for e in ENGS}
        self.dma_bufs = []
        self.last = {e: None for e in ENGS}

    def buf(self, name):
        return Buf(name)

    def reg(self, eng, val):
        if val not in self.regcache:
            self.regcache[val] = eng.to_reg(val)
        return self.regcache[val]

    def op(self, eng, fn, reads=(), writes=(), dma_owner=None):
        o = Op(eng, fn, dma_owner is not None)
        deps = o.deps
        for b in reads:
            deps.extend(b.w)
        for b in writes:
            deps.extend(b.w)
            deps.extend(b.r)
        if self.pending[eng]:
            deps.extend(self.pending[eng])
            self.pending[eng] = []
        for b in reads:
            if o.dma:
                b.r.append(o)
            else:
                b.r = [x for x in b.r if x.dma or x.eng != eng] + [o]
        for b in writes:
            b.w = [o]
            b.r = []
        if dma_owner is not None:
            if dma_owner.sem is None:
                dma_owner.sem = self.free_sems.pop()
                self.dma_bufs.append(dma_owner)
            o.sem = dma_owner.sem
            o.sem.count += 16
            o.val = o.sem.count
        self.ops[eng].append(o)
        self.last[eng] = o
        return o

    def emit(self, final=False):
        nc = self.nc
        self.regcache = {}
        for e in ENGS:
            for o in self.ops[e]:
                for d in o.deps:
                    if not d.dma and not (d.eng == e and (e == "tensor" or not SAME_ENGINE_SYNC)):
                        d.signal = True
        for e in ENGS:
            if self.last[e] is not None and not self.last[e].dma:
                self.last[e].signal = True
        for e in ENGS:
            for o in self.ops[e]:
                if o.dma:
                    continue
                if o.signal:
                    if self.eng_cnt[e] >= EPOCH:
                        self.eng_sem[e] = self.free_sems.pop()
                        self.eng_cnt[e] = 0
                    self.eng_cnt[e] += 1
                    o.sem = self.eng_sem[e]
                    o.val = self.eng_cnt[e]
        end_tokens = []
        for e in ENGS:
            if self.last[e] is not None and not self.last[e].dma:
                end_tokens.append(self.last[e])
        dma_final = [(b.sem, b.sem.count) for b in self.dma_bufs]

        def run(e, eng):
            waited = self.waited[e]
            for o in self.ops[e]:
                need = {}
                for d in o.deps:
                    if d.sem is None:
                        continue
                    if (not d.dma) and d.eng == e:
                        if e == "tensor" or not SAME_ENGINE_SYNC:
                            continue
                    k = d.sem
                    if need.get(k, 0) < d.val:
                        need[k] = d.val
                for k, v in need.items():
                    if waited.get(k, 0) >= v:
                        continue
                    eng.wait_ge(k.h, v)
                    waited[k] = v
                ins = o.fn(eng)
                if o.dma:
                    ins.then_inc(o.sem.h, 16)
                elif o.signal:
                    ins.then_inc(o.sem.h, 1)
            if e == "sync":
                for s, v in dma_final:
                    if waited.get(s, 0) < v:
                        eng.wait_ge(s.h, v)
                        waited[s] = v
                for t in end_tokens:
                    if t.eng != e and waited.get(t.sem, 0) < t.val:
                        eng.wait_ge(t.sem.h, t.val)
                        waited[t.sem] = t.val

        with nc.Block() as block:
            @block.tensor
            def _(eng):
                run("tensor", eng)

            @block.vector
            def _(eng):
                run("vector", eng)

            @block.scalar
            def _(eng):
                run("scalar", eng)

            @block.gpsimd
            def _(eng):
                run("gpsimd", eng)

            @block.sync
            def _(eng):
                run("sync", eng)

        for b in self.dma_bufs:
            self.free_sems.insert(0, b.sem)
            b.sem = None
        self.dma_bufs = []
        self.ops = {e: [] for e in ENGS}
        self.last = {e: None for e in ENGS}


def apx(ap, extra):
    return bass.AP(ap.tensor, ap.offset, [list(x) for x in ap.ap] + [list(x) for x in extra])


def ffn_phase(nc, P, tag, src, dst, g_ap, w_in, w_out, T, final_g=None):
    NB = T // 256
    with ExitStack() as es:
        def sb(name, shape, dt):
            return es.enter_context(nc.sbuf_tensor(f"{tag}_{name}", shape, dt))

        def ps(name, shape, dt):
            return es.enter_context(nc.psum_tensor(f"{tag}_{name}", shape, dt))

        WIN = sb("win", [128, 8, 2 * FF], BF16)
        WOUT = sb("wout", [128, NJ, D], BF16)
        G = sb("g", [128, 8], F32)
        IDENT = sb("ident", [128, 128], BF16)
        NEGH = sb("negh", [128, 1], F32)
        XB = [sb(f"xb{i}", [128, 2, D], F32) for i in range(3)]
        XN = [sb(f"xn{i}", [128, D], BF16) for i in range(2)]
        XNT = [sb(f"xnt{i}", [128, 8, 256], BF16) for i in range(2)]
        SQ = sb("sq", [128, D], BF16)
        ST = [sb(f"st{i}", [128, 4], F32) for i in range(4)]
        SIL = [sb(f"sil{i}", [128, 256], F32) for i in range(2)]
        HT = [sb(f"ht{i}", [128, 256], BF16) for i in range(3)]
        if final_g is not None:
            FG = sb("fg", [128, D], F32)
            OUTT = [sb(f"outt{i}", [128, D], F32) for i in range(2)]
        TPS = ps("tps", [128, 4, 256], BF16)
        GU = [ps(f"gu{i}", [128, 512], F32) for i in range(3)]
        ACC = [ps(f"acc{i}", [128, 512], F32) for i in range(4)]

        bWIN = [P.buf(f"win{k}") for k in range(8)]
        bWOUT = [P.buf(f"wout{i}") for i in range(NJ)]
        bG = P.buf("g")
        bC = P.buf("const")
        bXB = [P.buf(f"xb{i}") for i in range(3)]
        bXN = [P.buf(f"xn{i}") for i in range(2)]
        bXNT = [P.buf(f"xnt{i}") for i in range(2)]
        bSQ = P.buf("sq")
        bST = [P.buf(f"st{i}") for i in range(4)]
        bSIL = [P.buf(f"sil{i}") for i in range(2)]
        bHT = [P.buf(f"ht{i}") for i in range(3)]
        bTPS = P.buf("tps")
        bGU = [P.buf(f"gu{i}") for i in range(3)]
        bACC = [P.buf(f"acc{i}") for i in range(4)]
        bDST = P.buf("dst")
        if final_g is not None:
            bFG = P.buf("fg")
            bOUTT = [P.buf(f"outt{i}") for i in range(2)]

        P.op("gpsimd", lambda e: e.memset(IDENT[:], 1.0), writes=[bC])
        P.op("gpsimd", lambda e: e.affine_select(out=IDENT[:], in_=IDENT[:], pattern=[[-1, 128]],
                                                 compare_op=ALU.is_equal, fill=P.reg(e, 0.0), base=0, channel_multiplier=1),
             writes=[bC])
        P.op("gpsimd", lambda e: e.memset(NEGH[:], -0.5), writes=[bC])
        P.op("sync", lambda e: e.dma_start(out=G[:], in_=g_ap.rearrange("(c p) -> p c", p=128),
                                           allow_slow_non_contiguous=True), writes=[bG], dma_owner=bG)
        if final_g is not None:
            P.op("sync", lambda e: e.dma_start(out=FG[:], in_=final_g.partition_broadcast(128)),
                 writes=[bFG], dma_owner=bFG)

        def load(b):
            s = b % 3
            for t in range(2):
                r0 = (2 * b + t) * 128
                P.op("sync", lambda e, s=s, t=t, r0=r0: e.dma_start(out=XB[s][:, t, :], in_=src[r0:r0 + 128, :]),
                     writes=[bXB[s]], dma_owner=bXB[s])

        load(0)
        for k in range(8):
            for hh in range(4):
                c0 = hh * 1408
                P.op("gpsimd", lambda e, k=k, c0=c0: e.dma_start(out=WIN[:, k, c0:c0 + 1408],
                                                                in_=w_in[k * 128:(k + 1) * 128, c0:c0 + 1408]),
                     writes=[bWIN[k]], dma_owner=bWIN[k])
        if NB > 1:
            load(1)
        for j in range(NJ):
            P.op("gpsimd", lambda e, j=j: e.dma_start(out=WOUT[:, j, :], in_=w_out[j * 128:(j + 1) * 128, :]),
                 writes=[bWOUT[j]], dma_owner=bWOUT[j])

        def norm_tile(x_ap, bx, si, out_ap, bout, extra_reads=()):
            st, bst = ST[si], bST[si]
            P.op("scalar", lambda e: e.activation(out=SQ[:], in_=x_ap, func=AF.Square, accum_out=st[:, 0:1]),
                 reads=[bx], writes=[bSQ, bst])
            P.op("vector", lambda e: e.tensor_scalar(out=st[:, 1:2], in0=st[:, 0:1], scalar1=1.0 / D, scalar2=EPS,
                                                     op0=ALU.mult, op1=ALU.add), reads=[bst], writes=[bst])
            P.op("gpsimd", lambda e: e.tensor_tensor(out=st[:, 2:3], in0=st[:, 1:2], in1=NEGH[:], op=ALU.pow),
                 reads=[bst, bC], writes=[bst])
            P.op("scalar", lambda e: e.activation(out=out_ap, in_=x_ap, func=AF.Copy, scale=st[:, 2:3]),
                 reads=[bx, bst] + list(extra_reads), writes=[bout])

        def prologue(b):
            s = b % 3
            x2 = b % 2
            for t in range(2):
                norm_tile(XB[s][:, t, :], bXB[s], (2 * b + t) % 4, XN[t][:], bXN[t])
            for h in range(2):
                for t in range(2):
                    for kk in range(4):
                        k = 4 * h + kk
                        P.op("tensor", lambda e, t=t, k=k, kk=kk: e.transpose(
                            out=TPS[:, kk, t * 128:(t + 1) * 128], in_=XN[t][:, k * 128:(k + 1) * 128],
                            identity=IDENT[:]), reads=[bXN[t], bC], writes=[bTPS])
                for kk in range(4):
                    k = 4 * h + kk
                    P.op("vector", lambda e, k=k, kk=kk, x2=x2: e.tensor_scalar(
                        out=XNT[x2][:, k, :], in0=TPS[:, kk, :], scalar1=G[:, k:k + 1], scalar2=None, op0=ALU.mult),
                         reads=[bTPS, bG], writes=[bXNT[x2]])

        def gu(b, j):
            x2 = b % 2
            gb = (b * NJ + j) % 3
            for half in range(2):
                c0 = half * FF + j * 128
                for k in range(8):
                    P.op("tensor", lambda e, k=k, c0=c0, half=half, gb=gb, x2=x2: e.matmul(
                        GU[gb][:, half * 256:(half + 1) * 256], lhsT=WIN[:, k, c0:c0 + 128], rhs=XNT[x2][:, k, :],
                        start=(k == 0), stop=(k == 7)), reads=[bWIN[k], bXNT[x2]], writes=[bGU[gb]])

        def second(b, j):
            gb = (b * NJ + j) % 3
            hb = (b * NJ + j) % 3
            sl = (b * NJ + j) % 2
            P.op("scalar", lambda e: e.activation(out=SIL[sl][:], in_=GU[gb][:, 0:256], func=AF.Silu),
                 reads=[bGU[gb]], writes=[bSIL[sl]])
            P.op("vector", lambda e: e.tensor_tensor(out=HT[hb][:], in0=SIL[sl][:], in1=GU[gb][:, 256:512],
                                                     op=ALU.mult), reads=[bSIL[sl], bGU[gb]], writes=[bHT[hb]])
            for t in range(2):
                for n in range(2):
                    P.op("tensor", lambda e, t=t, n=n: e.matmul(
                        ACC[t * 2 + n][:], lhsT=HT[hb][:, t * 128:(t + 1) * 128], rhs=WOUT[:, j, n * 512:(n + 1) * 512],
                        start=(j == 0), stop=(j == NJ - 1)), reads=[bHT[hb], bWOUT[j]], writes=[bACC[t * 2 + n]])

        def epilogue(b):
            s = b % 3
            for t in range(2):
                for n in range(2):
                    P.op("vector", lambda e, t=t, n=n: e.scalar_tensor_tensor(
                        out=XB[s][:, t, n * 512:(n + 1) * 512], in0=ACC[t * 2 + n][:], scalar=0.5,
                        in1=XB[s][:, t, n * 512:(n + 1) * 512], op0=ALU.mult, op1=ALU.add),
                         reads=[bACC[t * 2 + n], bXB[s]], writes=[bXB[s]])
                r0 = (2 * b + t) * 128
                if final_g is None:
                    P.op("sync", lambda e, t=t, r0=r0: e.dma_start(out=dst[r0:r0 + 128, :], in_=XB[s][:, t, :]),
                         reads=[bXB[s]], writes=[bDST], dma_owner=bXB[s])
                else:
                    o2 = (2 * b + t) % 2
                    norm_tile(XB[s][:, t, :], bXB[s], (2 * b + t) % 4, OUTT[o2][:], bOUTT[o2])
                    P.op("vector", lambda e, o2=o2: e.tensor_tensor(out=OUTT[o2][:], in0=OUTT[o2][:], in1=FG[:],
                                                                    op=ALU.mult),
                         reads=[bOUTT[o2], bFG], writes=[bOUTT[o2]])
                    P.op("sync", lambda e, o2=o2, r0=r0: e.dma_start(out=dst[r0:r0 + 128, :], in_=OUTT[o2][:]),
                         reads=[bOUTT[o2]], writes=[bDST], dma_owner=bOUTT[o2])

        prologue(0)
        gu(0, 0)
        for b in range(NB):
            if b + 2 < NB:
                load(b + 2)
            for j in range(NJ):
                if j + 1 < NJ:
                    gu(b, j + 1)
                elif b + 1 < NB:
                    gu(b + 1, 0)
                if j == 6 and b + 1 < NB:
                    prologue(b + 1)
                second(b, j)
            epilogue(b)
        P.emit()


def norm_ops(P, x_ap, bx, SQ, bSQ, st, bst, NEGH, bC, out_ap, bout):
    P.op("scalar", lambda e: e.activation(out=SQ[:], in_=x_ap, func=AF.Square, accum_out=st[:, 0:1]),
         reads=[bx], writes=[bSQ, bst])
    P.op("vector", lambda e: e.tensor_scalar(out=st[:, 1:2], in0=st[:, 0:1], scalar1=1.0 / D, scalar2=EPS,
                                             op0=ALU.mult, op1=ALU.add), reads=[bst], writes=[bst])
    P.op("gpsimd", lambda e: e.tensor_tensor(out=st[:, 2:3], in0=st[:, 1:2], in1=NEGH[:], op=ALU.pow),
         reads=[bst, bC], writes=[bst])
    P.op("scalar", lambda e: e.activation(out=out_ap, in_=x_ap, func=AF.Copy, scale=st[:, 2:3]),
         reads=[bx, bst], writes=[bout])


KOFF = [512, 640, 768, 1024, 1816]
VOFF = [896, 1152, 1944]


def proj_phase(nc, P, h1, g_ap, w_mix, S, T):
    NB = T // 512
    with ExitStack() as es:
        def sb(name, shape, dt):
            return es.enter_context(nc.sbuf_tensor(f"pp_{name}", shape, dt))

        def ps(name, shape, dt):
            return es.enter_context(nc.psum_tensor(f"pp_{name}", shape, dt))

        WM = sb("wm", [128, 8, PROJ_W], BF16)
        WMQ = sb("wmq", [128, 8, 1024], BF16)
        G = sb("g", [128, 8], F32)
        IDENT = sb("ident", [128, 128], BF16)
        NEGH = sb("negh", [128, 1], F32)
        HB = [sb(f"hb{i}", [128, 4, D], F32) for i in range(2)]
        XN = [sb(f"xn{i}", [128, D], BF16) for i in range(4)]
        XNT = [sb(f"xnt{i}", [128, 8, 512], BF16) for i in range(2)]
        SQ = sb("sq", [128, D], BF16)
        ST = [sb(f"st{i}", [128, 4], F32) for i in range(4)]
        QST = [sb(f"qst{i}", [128, 4, 4, 128], BF16) for i in range(2)]
        QBST = [sb(f"qbst{i}", [128, 4, 4, 128], BF16) for i in range(2)]
        KST = [sb(f"kst{i}", [128, 5, 512], BF16) for i in range(2)]
        VST = [sb(f"vst{i}", [128, 4, 3, 2, 65], BF16) for i in range(2)]
        GST = [sb(f"gst{i}", [128, 4, 24], F32) for i in range(2)]
        TPS = [ps(f"tps{i}", [128, 2, 512], BF16) for i in range(2)]
        PJ = [ps(f"pj{i}", [128, 512], F32) for i in range(3)]
        PT = [ps(f"pt{i}", [128, 512], F32) for i in range(2)]

        bWM = [P.buf(f"wm{k}") for k in range(8)]
        bWMQ = [P.buf(f"wmq{k}") for k in range(8)]
        bG = P.buf("g"); bC = P.buf("c")
        bHB = [P.buf("hb") for i in range(2)]
        bXN = [P.buf("xn") for i in range(4)]
        bXNT = [P.buf("xnt") for i in range(2)]
        bSQ = P.buf("sq")
        bST = [P.buf("st") for i in range(4)]
        bQST = [P.buf("qst") for i in range(2)]
        bQBST = [P.buf("qbst") for i in range(2)]
        bKST = [P.buf("kst") for i in range(2)]
        bVST = [P.buf("vst") for i in range(2)]
        bGST = [P.buf("gst") for i in range(2)]
        bTPS = [P.buf("tps") for i in range(2)]
        bPJ = [P.buf("pj") for i in range(3)]
        bPT = [P.buf("pt") for i in range(2)]
        bOUT = P.buf("out")

        P.op("gpsimd", lambda e: e.memset(IDENT[:], 1.0), writes=[bC])
        P.op("gpsimd", lambda e: e.affine_select(out=IDENT[:], in_=IDENT[:], pattern=[[-1, 128]],
                                                 compare_op=ALU.is_equal, fill=P.reg(e, 0.0), base=0, channel_multiplier=1),
             writes=[bC])
        P.op("gpsimd", lambda e: e.memset(NEGH[:], -0.5), writes=[bC])
        for i in range(2):
            P.op("gpsimd", lambda e, i=i: e.memset(VST[i][:], 1.0), writes=[bVST[i]])
        P.op("sync", lambda e: e.dma_start(out=G[:], in_=g_ap.rearrange("(c p) -> p c", p=128),
                                           allow_slow_non_contiguous=True), writes=[bG], dma_owner=bG)

        def load(b):
            s = b % 2
            for t in range(4):
                r0 = (4 * b + t) * 128
                P.op("sync", lambda e, t=t, r0=r0: e.dma_start(out=HB[s][:, t, :], in_=h1[r0:r0 + 128, :]),
                     writes=[bHB[s]], dma_owner=bHB[s])

        load(0)
        for k in range(8):
            rows = slice(k * 128, (k + 1) * 128)
            for hh in range(2):
                c0 = hh * 1036
                P.op("gpsimd", lambda e, k=k, rows=rows, c0=c0: e.dma_start(
                    out=WM[:, k, c0:c0 + 1036], in_=w_mix[rows, c0:c0 + 1036]), writes=[bWM[k]], dma_owner=bWM[k])
        for k in range(8):
            for h, base in enumerate((0, 1304)):
                eng = "vector" if h == 0 else "gpsimd"
                P.op(eng, lambda e, k=k, h=h, base=base: e.tensor_copy(
                    out=WMQ[:, k, h * 512:(h + 1) * 512].rearrange("p (r g d) -> p r g d", r=4, g=2),
                    in_=WM[:, k, base:base + 512].rearrange("p (g r d) -> p r g d", g=2, r=4)),
                     reads=[bWM[k]], writes=[bWMQ[k]])
        cnt = {"pj": 0, "pt": 0, "tp": 0}
        def pro(b):
            s = b % 2
            if b + 1 < NB:
                load(b + 1)
            for t in range(4):
                norm_ops(P, HB[s][:, t, :], bHB[s], SQ, bSQ, ST[t], bST[t], NEGH, bC, XN[t][:], bXN[t])
            for kp in range(4):
                tp = cnt["tp"] % 2
                cnt["tp"] += 1
                for t in range(4):
                    for kk in range(2):
                        k = 2 * kp + kk
                        P.op("tensor", lambda e, t=t, k=k, kk=kk, tp=tp: e.transpose(
                            out=TPS[tp][:, kk, t * 128:(t + 1) * 128], in_=XN[t][:, k * 128:(k + 1) * 128],
                            identity=IDENT[:]), reads=[bXN[t], bC], writes=[bTPS[tp]])
                for kk in range(2):
                    k = 2 * kp + kk
                    P.op("vector", lambda e, k=k, kk=kk, tp=tp: e.tensor_scalar(
                        out=XNT[s][:, k, :], in0=TPS[tp][:, kk, :], scalar1=G[:, k:k + 1], scalar2=None,
                        op0=ALU.mult), reads=[bTPS[tp], bG], writes=[bXNT[s]])

        def block(b):
            s = b % 2

            def fm_group(colfn, evac, ei, bsrc=bWM):
                pj = cnt["pj"] % 3
                cnt["pj"] += 1
                for k in range(8):
                    P.op("tensor", lambda e, k=k, pj=pj: e.matmul(PJ[pj][:], lhsT=colfn(k), rhs=XNT[s][:, k, :],
                                                                 start=(k == 0), stop=(k == 7)),
                         reads=[bsrc[k], bXNT[s]], writes=[bPJ[pj]])
                evac(pj, ei)

            def qcols(base):
                def f(k):
                    return WMQ[:, k, base:base + 128]
                return f

            ei = 0
            for r in range(4):
                for (base, QS, bQS) in ((0, QST, bQST), (512, QBST, bQBST)):
                    def evac(pj, ei, r=r, QS=QS, bQS=bQS):
                        src = PJ[pj][:].rearrange("p (n t) -> p n t", t=128)
                        if ei % 2 == 0:
                            P.op("scalar", lambda e: e.activation(out=QS[s][:, :, r, :], in_=src, func=AF.Copy,
                                                                  scale=0.125), reads=[bPJ[pj]], writes=[bQS[s]])
                        else:
                            P.op("vector", lambda e: e.tensor_scalar(out=QS[s][:, :, r, :], in0=src, scalar1=0.125,
                                                                     scalar2=None, op0=ALU.mult),
                                 reads=[bPJ[pj]], writes=[bQS[s]])
                    fm_group(qcols(base + r * 128), evac, ei, bsrc=bWMQ)
                    ei += 1
            if b + 1 < NB:
                pro(b + 1)
            for w in range(5):
                def evac(pj, ei, w=w):
                    if ei % 2 == 0:
                        P.op("scalar", lambda e: e.copy(out=KST[s][:, w, :], in_=PJ[pj][:]),
                             reads=[bPJ[pj]], writes=[bKST[s]])
                    else:
                        P.op("vector", lambda e: e.tensor_copy(out=KST[s][:, w, :], in_=PJ[pj][:]),
                             reads=[bPJ[pj]], writes=[bKST[s]])
                fm_group(lambda k, w=w: WM[:, k, KOFF[w]:KOFF[w] + 128], evac, ei)
                ei += 1
            for t in range(4):
                pt = cnt["pt"] % 2
                cnt["pt"] += 1
                for (c0, n, o0) in ((896, 128, 0), (1152, 152, 128), (1944, 128, 280)):
                    for k in range(8):
                        P.op("tensor", lambda e, k=k, c0=c0, n=n, o0=o0, pt=pt, t=t: e.matmul(
                            PT[pt][:, o0:o0 + n], lhsT=XNT[s][:, k, t * 128:(t + 1) * 128], rhs=WM[:, k, c0:c0 + n],
                            start=(k == 0), stop=(k == 7)), reads=[bWM[k], bXNT[s]], writes=[bPT[pt]])
                for wi, o0 in enumerate((0, 128, 280)):
                    src = PT[pt][:, o0:o0 + 128].rearrange("p (g d) -> p g d", d=64)
                    if wi == 1:
                        P.op("scalar", lambda e, src=src, wi=wi, t=t: e.copy(out=VST[s][:, t, wi, :, 0:64], in_=src),
                             reads=[bPT[pt]], writes=[bVST[s]])
                    else:
                        P.op("vector", lambda e, src=src, wi=wi, t=t: e.tensor_copy(out=VST[s][:, t, wi, :, 0:64],
                                                                                   in_=src),
                             reads=[bPT[pt]], writes=[bVST[s]])
                P.op("vector", lambda e, t=t, pt=pt: e.tensor_copy(out=GST[s][:, t, :], in_=PT[pt][:, 256:280]),
                     reads=[bPT[pt]], writes=[bGST[s]])
            c0 = b * 2048
            P.op("sync", lambda e, c0=c0: e.dma_start(out=S["QA"][:, c0:c0 + 2048],
                                                      in_=QST[s][:].rearrange("p n r t -> p (n r t)")),
                 reads=[bQST[s]], writes=[bOUT], dma_owner=bQST[s])
            P.op("sync", lambda e, c0=c0: e.dma_start(out=S["QB"][:, c0:c0 + 2048],
                                                      in_=QBST[s][:].rearrange("p n r t -> p (n r t)")),
                 reads=[bQBST[s]], writes=[bOUT], dma_owner=bQBST[s])
            for w in range(5):
                P.op("sync", lambda e, w=w: e.dma_start(out=S["KT"][w, :, b * 512:(b + 1) * 512], in_=KST[s][:, w, :]),
                     reads=[bKST[s]], writes=[bOUT], dma_owner=bKST[s])
            for w in range(3):
                P.op("sync", lambda e, w=w: e.dma_start(out=S["VT"][w, :, 4 * b:4 * b + 4, :, :],
                                                        in_=VST[s][:, :, w, :, :]),
                     reads=[bVST[s]], writes=[bOUT], dma_owner=bVST[s])
            P.op("sync", lambda e: e.dma_start(
                out=S["GT"].rearrange("(n p) c -> p n c", p=128)[:, 4 * b:4 * b + 4, :], in_=GST[s][:]),
                 reads=[bGST[s]], writes=[bOUT], dma_owner=bGST[s])

        pro(0)
        for b in range(NB):
            block(b)
        P.emit()


def attn_phase(nc, P, h1, h2, S, C, W, T):
    NT = T // 128
    NCC = T // 16
    NCH = NCC // 128
    with ExitStack() as es:
        def sb(name, shape, dt):
            return es.enter_context(nc.sbuf_tensor(f"at_{name}", shape, dt))

        def ps(name, shape, dt):
            return es.enter_context(nc.psum_tensor(f"at_{name}", shape, dt))

        KG = [[sb(f"kg{br}{g}", [68, T], BF16) for g in range(2)] for br in range(3)]
        VR = [sb(f"vr{br}", [128, NT, 2, 65], BF16) for br in range(3)]
        KCG = [sb(f"kcg{g}", [68, NCC], BF16) for g in range(2)]
        VCX = [sb(f"vcx{g}", [128, NCH, 129], BF16) for g in range(2)]
        RAWT = sb("rawt", [128, 2, T], BF16)
        W1 = [sb(f"w1{i}", [128, 32, 256], BF16) for i in range(2)]
        W2 = [sb(f"w2{i}", [128, 2, 64], BF16) for i in range(2)]
        POSF = [sb(f"posf{i}", [64, 32], F32) for i in range(2)]
        POST = [sb(f"post{i}", [64, 32], BF16) for i in range(2)]
        B1 = [sb(f"b1{i}", [128, 2], F32) for i in range(2)]
        BIAS = [sb(f"bias{i}", [128, 2], F32) for i in range(2)]
        GX = sb("gx", [128, NCC], F32)
        T1 = sb("t1", [128, NCC], F32)
        HTC = [sb(f"htc{i}", [128, NCC], BF16) for i in range(2)]
        WO = sb("wo", [128, 8, D], BF16)
        EBIG = sb("ebig", [128, NT * 128], BF16)
        IDENT = sb("ident", [128, 128], BF16)
        CAUS = sb("caus", [128, 4, 128], BF16)
        CAUSC = sb("causc", [128, 4, 128], BF16)
        ZER = sb("zer", [128, 4, 128], BF16)
        SINKE = sb("sinke", [128, 8, 1], F32)
        QT = [[sb(f"qt{q}{i}", [68, 512], BF16) for i in range(2)] for q in range(4)]
        GATE = [sb(f"gate{i}", [128, 24], F32) for i in range(2)]
        FT = [sb(f"ft{i}", [128, 2, 64], F32) for i in range(2)]
        H1T = [sb(f"h1t{i}", [128, D], F32) for i in range(2)]
        PTT = [sb(f"ptt{i}", [128, 512], BF16) for i in range(4)]
        MASKC = [sb(f"maskc{i}", [128, 4, 128], BF16) for i in range(2)]
        NEGMT = [sb(f"negmt{g}", [128, 4, 128], BF16) for g in range(2)]
        NEGMS = [sb(f"negm{g}", [128, 64], BF16) for g in range(2)]
        OMIX = sb("omix", [128, 16, 64], F32)
        OMIXB = sb("omixb", [128, D], BF16)
        OT = sb("ot", [128, 8, 128], BF16)
        SG = sb("sg", [128, 24], F32)
        OAS = sb("oas", [128, 4, 129], F32)
        L4 = sb("l4", [128, 4, 1], F32)
        RL4 = sb("rl4", [128, 4, 1], F32)
        W4 = sb("w4", [128, 4, 1], F32)
        TMP = sb("tmp", [128, 4, 64], F32)
        PSLC = sb("pslc", [128, 64], F32)
        SCORE = sb("score", [128, 64], F32)
        WORK = sb("work", [128, 64], F32)
        M1 = sb("m1", [128, 8], F32)
        M2 = sb("m2", [128, 8], F32)

        NSTP = 3
        DEPTH = 3
        STP = [ps(f"stp{i}", [128, 512], F32) for i in range(NSTP)]
        OA = ps("oa", [128, 4, 512], F32)
        MF = ps("mf", [128, 512], F32)
        MBV = MF.bitcast(BF16)

        def MBc(c):
            return MBV[:, c * 128:(c + 1) * 128]

        B = P.buf
        bKG = [[B("kg") for g in range(2)] for br in range(3)]
        bVR = [B("vr") for br in range(3)]
        bKCG = [B("kcg") for g in range(2)]
        bVCX = [B("vcx") for g in range(2)]
        bRAWT = B("rawt"); bW1 = [B("w1"), B("w1")]; bW2 = [B("w2"), B("w2")]
        bPOSF = [B("pf"), B("pf")]; bPOST = [B("pt"), B("pt")]; bB1 = [B("b1"), B("b1")]; bBIAS = [B("bi"), B("bi")]
        bGX = B("gx"); bT1 = B("t1"); bHTC = [B("htc"), B("htc")]
        bWO = B("wo"); bC = B("c"); bSINK = B("sink")
        bQT = [[B("qt") for i in range(2)] for q in range(4)]
        bGATE = [B("gate"), B("gate")]; bFT = [B("ft"), B("ft")]; bH1T = [B("h1t"), B("h1t")]
        bPTT = [B("ptt") for i in range(4)]
        bMASKC = [B("mc"), B("mc")]
        bNEGMT = [B("nm"), B("nm")]; bNEGMS = [B("negm"), B("negm")]
        bOMIX = B("omix"); bOMIXB = B("omixb"); bOT = B("ot"); bSG = B("sg")
        bOAS = B("oas"); bL4 = B("l4"); bTMP = B("tmp"); bPSLC = B("pslc"); bSCORE = B("score"); bWORK = B("work")
        bM = B("m")
        bSTP = [B("stp") for _ in range(3)]; bOA = B("oa"); bMF = B("mf"); bMB = bMF
        bH2 = B("h2")

        P.op("gpsimd", lambda e: e.memset(IDENT[:], 1.0), writes=[bC])
        P.op("gpsimd", lambda e: e.affine_select(out=IDENT[:], in_=IDENT[:], pattern=[[-1, 128]],
                                                 compare_op=ALU.is_equal, fill=P.reg(e, 0.0), base=0, channel_multiplier=1),
             writes=[bC])
        P.op("gpsimd", lambda e: e.memset(EBIG[:], 1.0), writes=[bC])
        P.op("gpsimd", lambda e: e.affine_select(out=EBIG[:], in_=EBIG[:], pattern=[[-1, NT * 2], [0, 64]],
                                                 compare_op=ALU.is_equal, fill=P.reg(e, 0.0), base=0, channel_multiplier=1),
             writes=[bC])
        P.op("gpsimd", lambda e: e.memset(ZER[:], 0.0), writes=[bC])
        P.op("gpsimd", lambda e: e.memset(CAUS[:], NEG), writes=[bC])
        P.op("gpsimd", lambda e: e.affine_select(out=CAUS[:], in_=CAUS[:], pattern=[[0, 4], [-1, 128]],
                                                 compare_op=ALU.is_gt, fill=P.reg(e, 0.0), base=0, channel_multiplier=1),
             writes=[bC])
        P.op("gpsimd", lambda e: e.memset(CAUSC[:], NEG), writes=[bC])
        P.op("gpsimd", lambda e: e.affine_select(out=CAUSC[:], in_=CAUSC[:], pattern=[[0, 4], [1, 128]],
                                                 compare_op=ALU.is_ge, fill=P.reg(e, 0.0), base=0, channel_multiplier=-1),
             writes=[bC])
        for g in range(2):
            P.op("gpsimd", lambda e, g=g: e.memset(NEGMT[g][:], 0.0), writes=[bNEGMT[g]])
            P.op("gpsimd", lambda e, g=g: e.memset(KCG[g][:], 0.0), writes=[bKCG[g]])
            P.op("gpsimd", lambda e, g=g: e.memset(VCX[g][:], 0.0), writes=[bVCX[g]])
            P.op("gpsimd", lambda e, g=g: e.memset(VCX[g][:, :, 64:65], 1.0), writes=[bVCX[g]])

        P.op("sync", lambda e: e.dma_start(out=RAWT[:, 0, :], in_=S["KT"][0, :, :]), writes=[bRAWT], dma_owner=bRAWT)
        P.op("sync", lambda e: e.dma_start(out=RAWT[:, 1, :], in_=S["KT"][1, :, :]), writes=[bRAWT], dma_owner=bRAWT)
        for i in range(2):
            w1 = W["w1"][i].rearrange("(j d) h -> d j h", d=64)
            for half in range(2):
                for jj in range(4):
                    P.op("gpsimd", lambda e, i=i, half=half, jj=jj, w1=w1: e.dma_start(
                        out=W1[i][half * 64:(half + 1) * 64, jj * 8:(jj + 1) * 8, :], in_=w1[:, jj * 8:(jj + 1) * 8, :]),
                         writes=[bW1[i]], dma_owner=bW1[i])
            P.op("gpsimd", lambda e, i=i: e.dma_start(out=W2[i][:], in_=W["w2"][i].rearrange("(c p) d -> p c d", p=128)),
                 writes=[bW2[i]], dma_owner=bW2[i])
            P.op("sync", lambda e, i=i: e.dma_start(out=POSF[i][:], in_=W["pos"][i].rearrange("j d -> d j"),
                                                    allow_slow_non_contiguous=True), writes=[bPOSF[i]], dma_owner=bPOSF[i])
            P.op("sync", lambda e, i=i: e.dma_start(out=B1[i][:], in_=W["b1"][i].rearrange("(c p) -> p c", p=128),
                                                    allow_slow_non_contiguous=True), writes=[bB1[i]], dma_owner=bB1[i])
            P.op("vector", lambda e, i=i: e.tensor_copy(out=POST[i][:], in_=POSF[i][:]), reads=[bPOSF[i]],
                 writes=[bPOST[i]])
        for g in range(2):
            P.op("sync", lambda e, g=g: e.dma_start(out=KCG[g][64:68, :], in_=C["kaugc"][:, :]),
                 writes=[bKCG[g]], dma_owner=bKCG[g])
            P.op("sync", lambda e, g=g: e.dma_start(out=VCX[g][:, :, 65:129],
                                                    in_=C["ovl"].rearrange("(n p) s -> p n s", p=128)),
                 writes=[bVCX[g]], dma_owner=bVCX[g])
        for br, w in enumerate((2, 3, 4)):
            for g in range(2):
                P.op("sync", lambda e, br=br, w=w, g=g: e.dma_start(out=KG[br][g][0:64, :],
                                                                   in_=S["KT"][w, g * 64:(g + 1) * 64, :]),
                     writes=[bKG[br][g]], dma_owner=bKG[br][g])
                P.op("sync", lambda e, br=br, g=g: e.dma_start(out=KG[br][g][64:68, :], in_=C["kaug"][:, :]),
                     writes=[bKG[br][g]], dma_owner=bKG[br][g])
            P.op("sync", lambda e, br=br: e.dma_start(out=VR[br][:].rearrange("p n g d -> p (n g d)"),
                                                      in_=S["VT"][br].rearrange("p n g d -> p (n g d)")),
                 writes=[bVR[br]], dma_owner=bVR[br])
        for k in range(8):
            P.op("gpsimd", lambda e, k=k: e.dma_start(out=WO[:, k, :], in_=W["wo"][k * 128:(k + 1) * 128, :]),
                 writes=[bWO], dma_owner=bWO)
        P.op("sync", lambda e: e.dma_start(out=SINKE[:, :, 0], in_=W["sinks"].partition_broadcast(128)),
             writes=[bSINK], dma_owner=bSINK)
        P.op("scalar", lambda e: e.activation(out=SINKE[:], in_=SINKE[:], func=AF.Exp), reads=[bSINK], writes=[bSINK])

        def loads(i):
            s = i % 2
            for q, (src, g) in enumerate((("QA", 0), ("QA", 1), ("QB", 0), ("QB", 1))):
                P.op("sync", lambda e, q=q, src=src, g=g: e.dma_start(
                    out=QT[q][s][0:64, :], in_=S[src][g * 64:(g + 1) * 64, i * 512:(i + 1) * 512]),
                     writes=[bQT[q][s]], dma_owner=bQT[q][s])
                P.op("sync", lambda e, q=q, g=g: e.dma_start(out=QT[q][s][64:68, :],
                                                             in_=C["qaug"][g, :, i * 512:(i + 1) * 512]),
                     writes=[bQT[q][s]], dma_owner=bQT[q][s])
            P.op("sync", lambda e: e.dma_start(out=GATE[s][:], in_=S["GT"][i * 128:(i + 1) * 128, :]),
                 writes=[bGATE[s]], dma_owner=bGATE[s])
            P.op("sync", lambda e: e.dma_start(out=FT[s][:, 0, :], in_=C["ftab"][i, :, :]),
                 writes=[bFT[s]], dma_owner=bFT[s])
            P.op("sync", lambda e: e.dma_start(out=H1T[s][:], in_=h1[i * 128:(i + 1) * 128, :]),
                 writes=[bH1T[s]], dma_owner=bH1T[s])

        loads(0)

        cnt = {"st": 0, "pt": 0, "mc": 0}

        def st_next():
            v = cnt["st"] % 3
            cnt["st"] += 1
            return v

        for i in range(2):
            for hc in range(2):
                for j in range(32):
                    P.op("tensor", lambda e, i=i, hc=hc, j=j: e.matmul(
                        MF[:, hc:hc + 1], lhsT=W1[i][0:64, j, hc * 128:(hc + 1) * 128], rhs=POST[i][:, j:j + 1],
                        start=(j == 0), stop=(j == 31)), reads=[bW1[i], bPOST[i]], writes=[bMF])
            P.op("vector", lambda e, i=i: e.tensor_tensor(out=BIAS[i][:], in0=MF[:, 0:2], in1=B1[i][:], op=ALU.add),
                 reads=[bMF, bB1[i]], writes=[bBIAS[i]])
            for g in range(2):
                for hc in range(2):
                    sp = st_next()
                    for j in range(32):
                        P.op("tensor", lambda e, i=i, g=g, hc=hc, j=j, sp=sp: e.matmul(
                            STP[sp][:, 0:NCC - 1], lhsT=W1[i][g * 64:(g + 1) * 64, j, hc * 128:(hc + 1) * 128],
                            rhs=RAWT[g * 64:(g + 1) * 64, i, j:j + 16 * (NCC - 2) + 1:16],
                            start=(j == 0), stop=(j == 31)), reads=[bW1[i], bRAWT], writes=[bSTP[sp]])
                    n = NCC - 1
                    P.op("scalar", lambda e, i=i, hc=hc, sp=sp: e.activation(
                        out=GX[:, 0:n], in_=STP[sp][:, 0:n], func=AF.Identity, bias=BIAS[i][:, hc:hc + 1]),
                         reads=[bSTP[sp], bBIAS[i]], writes=[bGX])
                    P.op("vector", lambda e: e.tensor_tensor(out=T1[:, 0:n], in0=GX[:, 0:n], in1=GX[:, 0:n], op=ALU.mult),
                         reads=[bGX], writes=[bT1])
                    P.op("vector", lambda e: e.tensor_scalar(out=T1[:, 0:n], in0=T1[:, 0:n], scalar1=0.044715,
                                                             scalar2=1.0, op0=ALU.mult, op1=ALU.add),
                         reads=[bT1], writes=[bT1])
                    P.op("vector", lambda e: e.tensor_tensor(out=T1[:, 0:n], in0=T1[:, 0:n], in1=GX[:, 0:n], op=ALU.mult),
                         reads=[bT1, bGX], writes=[bT1])
                    P.op("scalar", lambda e: e.activation(out=T1[:, 0:n], in_=T1[:, 0:n], func=AF.Exp,
                                                          scale=-1.5957691216057308), reads=[bT1], writes=[bT1])
                    P.op("vector", lambda e: e.tensor_scalar(out=T1[:, 0:n], in0=T1[:, 0:n], scalar1=1.0, scalar2=None,
                                                             op0=ALU.add), reads=[bT1], writes=[bT1])
                    P.op("vector", lambda e: e.reciprocal(out=T1[:, 0:n], in_=T1[:, 0:n]), reads=[bT1], writes=[bT1])
                    P.op("gpsimd", lambda e, hc=hc: e.memset(HTC[hc][:, n:NCC], 0.0), writes=[bHTC[hc]])
                    P.op("vector", lambda e, hc=hc: e.tensor_tensor(out=HTC[hc][:, 0:n], in0=T1[:, 0:n], in1=GX[:, 0:n],
                                                                    op=ALU.mult), reads=[bT1, bGX], writes=[bHTC[hc]])
                if i == 0:
                    for hc in range(2):
                        P.op("tensor", lambda e, hc=hc: e.matmul(MF[0:64, 0:NCC], lhsT=W2[0][:, hc, :], rhs=HTC[hc][:],
                                                                 start=(hc == 0), stop=(hc == 1)),
                             reads=[bW2[0], bHTC[hc]], writes=[bMF])
                    P.op("vector", lambda e, g=g: e.tensor_copy(out=KCG[g][0:64, :], in_=MF[0:64, 0:NCC]),
                         reads=[bMF], writes=[bKCG[g]])
                else:
                    for cc in range(NCH):
                        for hc in range(2):
                            P.op("tensor", lambda e, hc=hc, cc=cc: e.matmul(
                                MF[:, 0:64], lhsT=HTC[hc][:, cc * 128:(cc + 1) * 128], rhs=W2[1][:, hc, :],
                                start=(hc == 0), stop=(hc == 1)), reads=[bW2[1], bHTC[hc]], writes=[bMF])
                        P.op("vector", lambda e, g=g, cc=cc: e.tensor_copy(out=VCX[g][:, cc, 0:64], in_=MF[:, 0:64]),
                             reads=[bMF], writes=[bVCX[g]])

        def unit(i, kt_ap, bk, q, s, extra, vt_ap, bv, first, last, width):
            sp = st_next()
            pt = cnt["pt"] % 4
            cnt["pt"] += 1
            nmm = 1 + len(extra)
            P.op("tensor", lambda e: e.matmul(STP[sp][:], lhsT=kt_ap, rhs=QT[q][s][:], start=True, stop=(nmm == 1)),
                 reads=[bk, bQT[q][s]], writes=[bSTP[sp]])
            for n, (l_ap, r_ap, rb) in enumerate(extra):
                P.op("tensor", lambda e, l_ap=l_ap, r_ap=r_ap, n=n: e.matmul(
                    STP[sp][:], lhsT=l_ap, rhs=r_ap, start=False, stop=(n == nmm - 2)),
                     reads=[bC] + rb, writes=[bSTP[sp]])
            P.op("scalar", lambda e: e.activation(out=PTT[pt][:], in_=STP[sp][:], func=AF.Exp),
                 reads=[bSTP[sp]], writes=[bPTT[pt]])
            return pt

        pend = []

        def flush():
            for f in pend:
                f()
            del pend[:]

        def pv_now(pt, vt_ap, bv, first, last, width):
            for r in range(4):
                P.op("tensor", lambda e, r=r: e.matmul(OA[:, r, 0:width], lhsT=PTT[pt][:, r * 128:(r + 1) * 128],
                                                       rhs=vt_ap, start=first, stop=last),
                     reads=[bPTT[pt], bv], writes=[bOA])

        def pv(pt, vt_ap, bv, first, last, width):
            prev = list(pend)
            del pend[:]
            pend.append(lambda: pv_now(pt, vt_ap, bv, first, last, width))
            for f in prev:
                f()

        def flat(t3):
            return t3[:].rearrange("p r t -> p (r t)")

        def finish(col0, gate_col, first_branch, sink_g=None, width=65, need_rl=False):
            P.op("vector", lambda e: e.tensor_copy(out=OAS[:, :, 0:width], in_=OA[:, :, 0:width]),
                 reads=[bOA], writes=[bOAS])
            if need_rl:
                P.op("vector", lambda e: e.tensor_scalar(out=L4[:], in0=OAS[:, :, 64:65], scalar1=1e-30, scalar2=None,
                                                         op0=ALU.max), reads=[bOAS], writes=[bL4])
                P.op("vector", lambda e: e.reciprocal(out=RL4[:], in_=L4[:]), reads=[bL4], writes=[bL4])
                P.op("vector", lambda e: e.tensor_tensor(out=W4[:, :, 0], in0=RL4[:, :, 0],
                                                         in1=SG[:, gate_col:gate_col + 10:3], op=ALU.mult),
                     reads=[bL4, bSG], writes=[bL4])
            elif sink_g is not None:
                P.op("vector", lambda e: e.tensor_tensor(out=L4[:], in0=OAS[:, :, 64:65],
                                                         in1=SINKE[:, sink_g * 4:sink_g * 4 + 4, :], op=ALU.add),
                     reads=[bOAS, bSINK], writes=[bL4])
                P.op("vector", lambda e: e.reciprocal(out=W4[:], in_=L4[:]), reads=[bL4], writes=[bL4])
            else:
                P.op("vector", lambda e: e.reciprocal(out=RL4[:], in_=OAS[:, :, 64:65]), reads=[bOAS], writes=[bL4])
                P.op("vector", lambda e: e.tensor_tensor(out=W4[:, :, 0], in0=RL4[:, :, 0],
                                                         in1=SG[:, gate_col:gate_col + 10:3], op=ALU.mult),
                     reads=[bL4, bSG], writes=[bL4])
            wb = apx(W4[:, :, 0], [[0, 64]])
            if first_branch:
                P.op("vector", lambda e: e.tensor_tensor(out=OMIX[:, col0:col0 + 4, :], in0=OAS[:, :, 0:64], in1=wb,
                                                         op=ALU.mult), reads=[bOAS, bL4], writes=[bOMIX])
            else:
                P.op("vector", lambda e: e.tensor_tensor(out=TMP[:], in0=OAS[:, :, 0:64], in1=wb, op=ALU.mult),
                     reads=[bOAS, bL4], writes=[bTMP])
                P.op("vector", lambda e: e.tensor_tensor(out=OMIX[:, col0:col0 + 4, :], in0=OMIX[:, col0:col0 + 4, :],
                                                         in1=TMP[:], op=ALU.add), reads=[bTMP, bOMIX], writes=[bOMIX])

        items = []

        def U(qk_fn, pv_fn, needs=None):
            items.append(("u", qk_fn, pv_fn, needs))

        def Bar(fn, provides=None, releases=0):
            items.append(("b", fn, provides, releases))

        def tile(i):
            s = i % 2

            def tile_start():
                P.op("scalar", lambda e: e.activation(out=SG[:], in_=GATE[s][:], func=AF.Exp, scale=-1.0),
                     reads=[bGATE[s]], writes=[bSG])
                P.op("vector", lambda e: e.tensor_scalar(out=SG[:], in0=SG[:], scalar1=1.0, scalar2=None, op0=ALU.add),
                     reads=[bSG], writes=[bSG])
                P.op("vector", lambda e: e.reciprocal(out=SG[:], in_=SG[:]), reads=[bSG], writes=[bSG])
            Bar(tile_start)

            def std_branch(br, g, q, j0, far, col0, gate_col, first_branch, sink_g=None, needs=None):
                for j in range(j0, i + 1):
                    def qk(j=j):
                        extra = []
                        if br == 0:
                            extra.append((EBIG[:, j * 128:(j + 1) * 128], flat(NEGMT[g]), [bNEGMT[g]]))
                        if j == i:
                            extra.append((IDENT[:], flat(CAUS), []))
                        if far is not None and j == i - far:
                            extra.append((IDENT[:], flat(CAUSC), []))
                        return unit(i, KG[br][g][:, j * 128:(j + 1) * 128], bKG[br][g], q, s, extra, None, None, 0, 0, 0)

                    def pvf(pt, j=j):
                        pv_now(pt, VR[br][:, j, g, :], bVR[br], j == j0, j == i, 65)
                    U(qk, pvf, needs)
                Bar(lambda: finish(col0, gate_col, first_branch, sink_g=sink_g))

            def cmp_group(g):
                q = g
                nch = min(NCH, (8 * i + 6) // 128 + 1)
                pts = []
                for cc in range(nch):
                    def qk(cc=cc):
                        shift = 128 * cc - 8 * i
                        extra = []
                        if shift + 127 >= -1:
                            m = cnt["mc"] % 2
                            cnt["mc"] += 1
                            P.op("gpsimd", lambda e, m=m, shift=shift: e.affine_select(
                                out=MASKC[m][:], in_=ZER[:], pattern=[[0, 4], [1, 128]], compare_op=ALU.is_ge,
                                fill=P.reg(e, NEG), base=-31 - 16 * shift, channel_multiplier=-16),
                                 reads=[bC], writes=[bMASKC[m]])
                            extra.append((IDENT[:], flat(MASKC[m]), [bMASKC[m]]))
                        pt = unit(i, KCG[g][:, cc * 128:(cc + 1) * 128], bKCG[g], q, s, extra, None, None, 0, 0, 0)
                        pts.append(pt)
                        return pt
                    U(qk, None)

                def cmp_finish():
                    for r in range(4):
                        for cc in range(nch):
                            P.op("tensor", lambda e, r=r, cc=cc: e.matmul(
                                OA[:, r, 0:129], lhsT=PTT[pts[cc]][:, r * 128:(r + 1) * 128], rhs=VCX[g][:, cc, :],
                                start=(cc == 0), stop=(cc == nch - 1)), reads=[bPTT[pts[cc]], bVCX[g]], writes=[bOA])
                    finish(g * 4, g * 12 + 0, True, width=129, need_rl=True)
                    P.op("vector", lambda e: e.tensor_tensor(out=TMP[:], in0=OAS[:, :, 65:129],
                                                             in1=apx(RL4[:, :, 0], [[0, 64]]), op=ALU.mult),
                         reads=[bOAS, bL4], writes=[bTMP])
                    P.op("vector", lambda e: e.tensor_tensor(out=TMP[:, 0:2, :], in0=TMP[:, 0:2, :], in1=TMP[:, 2:4, :],
                                                             op=ALU.add), reads=[bTMP], writes=[bTMP])
                    P.op("vector", lambda e: e.tensor_tensor(out=PSLC[:], in0=TMP[:, 0, :], in1=TMP[:, 1, :], op=ALU.add),
                         reads=[bTMP], writes=[bPSLC])
                    P.op("vector", lambda e: e.tensor_tensor(out=SCORE[:], in0=PSLC[:], in1=FT[s][:, 0, :], op=ALU.add),
                         reads=[bPSLC, bFT[s]], writes=[bSCORE])
                    P.op("vector", lambda e: e.max(out=M1[:], in_=SCORE[:]), reads=[bSCORE], writes=[bM])
                    P.op("vector", lambda e: e.match_replace(out=WORK[:], in_to_replace=M1[:], in_values=SCORE[:],
                                                             imm_value=-3.0e38), reads=[bSCORE, bM], writes=[bWORK])
                    P.op("vector", lambda e: e.max(out=M2[:], in_=WORK[:]), reads=[bWORK], writes=[bM])
                    P.op("vector", lambda e: e.tensor_scalar(out=NEGMS[g][:], in0=SCORE[:], scalar1=M2[:, 7:8],
                                                             scalar2=NEG, op0=ALU.is_lt, op1=ALU.mult),
                         reads=[bSCORE, bM], writes=[bNEGMS[g]])
                return cmp_finish, nch

            def negmt_make(g):
                def f():
                    P.op("tensor", lambda e: e.transpose(out=MBc(0)[0:64, :], in_=NEGMS[g][:], identity=IDENT[:]),
                         reads=[bNEGMS[g], bC], writes=[bMB])
                    for r in range(4):
                        P.op("scalar", lambda e, r=r: e.copy(out=NEGMT[g][0:64, r, :], in_=MBc(0)[0:64, :]),
                             reads=[bMB], writes=[bNEGMT[g]])
                Bar(f, provides=("negmt", i, g))

            fins = [cmp_group(g) for g in range(2)]
            if i > 0:
                Bar(prev_out[0])
            for (fn, nrel) in fins:
                Bar(fn, releases=nrel)
            if i > 0:
                Bar(prev_out[1])
            for g in range(2):
                std_branch(1, g, g, max(0, i - 4), 4, g * 4, g * 12 + 2, False)
            if i > 0:
                Bar(prev_out[2])
            if i + 1 < NT:
                Bar(lambda: loads(i + 1))
            for g in range(2):
                std_branch(2, g, 2 + g, max(0, i - 1), 1, 8 + g * 4, None, True, sink_g=g)
            for g in range(2):
                negmt_make(g)
                std_branch(0, g, g, 0, None, g * 4, g * 12 + 1, False, needs=("negmt", i, g))

            def out_a():
                P.op("scalar", lambda e: e.copy(out=OMIXB[:], in_=OMIX[:].rearrange("p h d -> p (h d)")),
                     reads=[bOMIX], writes=[bOMIXB])
                for c in range(8):
                    P.op("tensor", lambda e, c=c: e.transpose(out=MBc(c), in_=OMIXB[:, c * 128:(c + 1) * 128],
                                                              identity=IDENT[:]), reads=[bOMIXB, bC], writes=[bMB])
                P.op("vector", lambda e: e.tensor_copy(out=OT[:], in_=MBV[:, :].rearrange("p (c t) -> p c t", c=8)),
                     reads=[bMB], writes=[bOT])

            def out_half(n):
                def f():
                    for c in range(8):
                        P.op("tensor", lambda e, c=c: e.matmul(MF[:], lhsT=OT[:, c, :],
                                                              rhs=WO[:, c, n * 512:(n + 1) * 512],
                                                              start=(c == 0), stop=(c == 7)),
                             reads=[bOT, bWO], writes=[bMF])
                    P.op("vector", lambda e: e.tensor_tensor(out=H1T[s][:, n * 512:(n + 1) * 512], in0=MF[:],
                                                             in1=H1T[s][:, n * 512:(n + 1) * 512], op=ALU.add),
                         reads=[bMF, bH1T[s]], writes=[bH1T[s]])
                    if n == 1:
                        P.op("sync", lambda e: e.dma_start(out=h2[i * 128:(i + 1) * 128, :], in_=H1T[s][:]),
                             reads=[bH1T[s]], writes=[bH2], dma_owner=bH1T[s])
                return f
            stages = (out_a, out_half(0), out_half(1))
            if i == NT - 1:
                for st in stages:
                    Bar(st)
            return stages

        prev_out = None
        for i in range(NT):
            prev_out = tile(i)

        n_items = len(items)
        issued = [False] * n_items
        ptsl = [None] * n_items
        provided = set()
        inflight = 0
        for k in range(n_items):
            it = items[k]
            if it[0] == "b":
                it[1]()
                if it[2] is not None:
                    provided.add(it[2])
                inflight -= it[3]
                continue
            if not issued[k]:
                ptsl[k] = it[1]()
                issued[k] = True
                inflight += 1
            m = k + 1
            while m < n_items and m - k <= 8 and inflight < DEPTH:
                im = items[m]
                if im[0] == "u" and not issued[m]:
                    if im[3] is not None and im[3] not in provided:
                        break
                    ptsl[m] = im[1]()
                    issued[m] = True
                    inflight += 1
                m += 1
            if it[2] is not None:
                it[2](ptsl[k])
                inflight -= 1
        P.emit()


def make_consts(T):
    NT = T // 128
    NCC = T // 16
    bf = ml_dtypes.bfloat16
    k = np.arange(T)
    kaug = np.stack([np.ones(T), k % 128, np.ones(T), k // 128]).astype(np.float32)
    pc = 16 * np.arange(NCC) + 31
    kaugc = np.stack([np.ones(NCC), pc % 128, np.ones(NCC), pc // 128]).astype(np.float32)
    qaug = np.zeros((2, 4, NT, 4, 128), np.float32)
    tl = np.arange(128)
    for g in range(2):
        for r in range(4):
            sl = SLOPES[g * 4 + r]
            for i in range(NT):
                qaug[g, 0, i, r] = -sl * tl
                qaug[g, 1, i, r] = sl
                qaug[g, 2, i, r] = -sl * 128.0 * i
                qaug[g, 3, i, r] = sl * 128.0
    qaug = qaug.reshape(2, 4, NT * 512)
    ns = T // 64
    t = np.arange(T)
    cur = t // 64
    blk = np.arange(64)
    ftab = np.zeros((T, 64), np.float32)
    valid = blk[None, :] <= cur[:, None]
    forced = (blk[None, :] == 0) | (blk[None, :] == cur[:, None]) | (blk[None, :] == cur[:, None] - 1)
    ftab[~valid] = -1e30
    ftab[forced] = 1e9
    ftab[:, ns:] = -1e30
    ftab = ftab.reshape(NT, 128, 64)
    c_start = np.arange(NCC) * 16
    s_start = np.arange(64) * 64
    ovl = np.clip(np.minimum(c_start[:, None] + 32, s_start[None, :] + 64)
                  - np.maximum(c_start[:, None], s_start[None, :]), 0, None) / 32.0
    ovl[NCC - 1, :] = 0.0
    return {"kaug": kaug.astype(bf), "kaugc": kaugc.astype(bf), "qaug": qaug.astype(bf), "ftab": ftab,
            "ovl": ovl.astype(np.float32).astype(bf)}


def build(T, debug=False, phases=("fa", "pp", "at", "fc")):
    NT = T // 128
    nc = bass.Bass("TRN2", target_bir_lowering=False)

    def din(name, shape, dt=F32):
        return nc.dram_tensor(name, list(shape), dt, kind="ExternalInput").ap()

    x = din("x", [T, D])
    ffn1_norm = din("ffn1_norm", [D])
    ffn1_w_in = din("ffn1_w_in", [D, 2 * FF])
    ffn1_w_out = din("ffn1_w_out", [FF, D])
    ffn2_norm = din("ffn2_norm", [D])
    ffn2_w_in = din("ffn2_w_in", [D, 2 * FF])
    ffn2_w_out = din("ffn2_w_out", [FF, D])
    final_norm = din("final_norm", [D])
    y = nc.dram_tensor("y", [T, D], F32, kind="ExternalOutput").ap()
    mix_norm = din("mix_norm", [D])
    w_mix_in = din("w_mix_in", [D, PROJ_W])
    w_mix_out = din("w_mix_out", [D, D])
    swa_sinks = din("swa_sinks", [8])
    Wd = {"w1": [din("cmp_k_w1", [2048, 256]), din("cmp_v_w1", [2048, 256])],
          "w2": [din("cmp_k_w2", [256, 64]), din("cmp_v_w2", [256, 64])],
          "pos": [din("cmp_k_pos", [32, 64]), din("cmp_v_pos", [32, 64])],
          "b1": [din("cmp_k_b1", [256]), din("cmp_v_b1", [256])],
          "wo": w_mix_out, "sinks": swa_sinks}
    NCC = T // 16
    Cd = {"kaug": din("c_kaug", [4, T], BF16), "kaugc": din("c_kaugc", [4, NCC], BF16),
          "qaug": din("c_qaug", [2, 4, NT * 512], BF16), "ftab": din("c_ftab", [NT, 128, 64]),
          "ovl": din("c_ovl", [NCC, 64], BF16)}
    kind = "ExternalOutput" if debug else "Internal"

    def scr(name, shape, dt):
        return nc.dram_tensor(name, list(shape), dt, kind=kind).ap()

    h1 = scr("h1s", [T, D], F32)
    h2 = scr("h2s", [T, D], F32)
    Sd = {"QA": scr("s_qa", [128, NT * 512], BF16), "QB": scr("s_qb", [128, NT * 512], BF16),
          "KT": scr("s_kt", [5, 128, T], BF16), "VT": scr("s_vt", [3, 128, NT, 2, 65], BF16),
          "GT": scr("s_gt", [T, 24], F32)}

    with ExitStack() as es:
        sems = [es.enter_context(nc.semaphore(f"s{i}")) for i in range(88)]
        P = Prog(nc, sems)
        if "fa" in phases:
            ffn_phase(nc, P, "fa", x, h1, ffn1_norm, ffn1_w_in, ffn1_w_out, T)
        if "pp" in phases:
            proj_phase(nc, P, h1, mix_norm, w_mix_in, Sd, T)
        if "at" in phases:
            attn_phase(nc, P, h1, h2, Sd, Cd, Wd, T)
        if "fc" in phases:
            ffn_phase(nc, P, "fc", h2, y, ffn2_norm, ffn2_w_in, ffn2_w_out, T, final_g=final_norm)
    return nc


_CACHE = {}


def kernel(**inputs):
    T = inputs["x"].shape[1]
    nb = inputs["x"].shape[0]
    if T not in _CACHE:
        _CACHE[T] = (build(T), make_consts(T))
    nc, consts = _CACHE[T]
    shared = {}
    for k, v in inputs.items():
        if k == "x":
            continue
        a = np.ascontiguousarray(np.asarray(v, dtype=np.float32))
        if k != "final_norm":
            a = a[0]
        shared[k] = np.ascontiguousarray(a)
    for k, v in consts.items():
        shared["c_" + k] = v
    xs = np.asarray(inputs["x"], dtype=np.float32)
    in_maps = []
    for b in range(nb):
        m = dict(shared)
        m["x"] = np.ascontiguousarray(xs[b])
        in_maps.append(m)
    res = run_bass_kernel_spmd(nc, in_maps, core_ids=list(range(nb)))
    return np.stack([np.asarray(r["y"], dtype=np.float32) for r in res.results], axis=0)
```

```python
import numpy as np
import ml_dtypes
import concourse.bass as bass
import concourse.mybir as mybir
from concourse.bass_utils import run_bass_kernel_spmd
from contextlib import ExitStack

F32 = mybir.dt.float32
BF16 = mybir.dt.bfloat16
AF = mybir.ActivationFunctionType
ALU = mybir.AluOpType

D = 1024
FF = 2816
NJ = FF // 128
PROJ_W = 2072
NEG = -30000.0
EPS = 1e-6
SLOPES = [2.0 ** (-(i + 1)) for i in range(8)]
SAME_ENGINE_SYNC = True
EPOCH = 100000


class Sem:
    def __init__(self, h):
        self.h = h
        self.count = 0


class Buf:
    __slots__ = ("name", "w", "r", "sem")

    def __init__(self, name):
        self.name = name
        self.w = []
        self.r = []
        self.sem = None


class Op:
    __slots__ = ("eng", "fn", "deps", "dma", "sem", "val", "signal")

    def __init__(self, eng, fn, dma):
        self.eng = eng
        self.fn = fn
        self.deps = []
        self.dma = dma
        self.sem = None
        self.val = 0
        self.signal = False


ENGS = ["tensor", "vector", "scalar", "gpsimd", "sync"]


class Prog:
    def __init__(self, nc, sems):
        self.nc = nc
        self.free_sems = [Sem(h) for h in sems]
        self.eng_sem = {}
        self.eng_cnt = {}
        for e in ENGS:
            self.eng_sem[e] = self.free_sems.pop()
            self.eng_cnt[e] = 0
        self.waited = {e: {} for e in ENGS}
        self.ops = {e: [] for e in ENGS}
        self.pending = {e: [] for e in ENGS}
        self.dma_bufs = []
        self.last = {e: None for e in ENGS}

    def buf(self, name):
        return Buf(name)

    def reg(self, eng, val):
        if val not in self.regcache:
            self.regcache[val] = eng.to_reg(val)
        return self.regcache[val]

    def op(self, eng, fn, reads=(), writes=(), dma_owner=None):
        o = Op(eng, fn, dma_owner is not None)
        deps = o.deps
        for b in reads:
            deps.extend(b.w)
        for b in writes:
            deps.extend(b.w)
            deps.extend(b.r)
        if self.pending[eng]:
            deps.extend(self.pending[eng])
            self.pending[eng] = []
        for b in reads:
            if o.dma:
                b.r.append(o)
            else:
                b.r = [x for x in b.r if x.dma or x.eng != eng] + [o]
        for b in writes:
            b.w = [o]
            b.r = []
        if dma_owner is not None:
            if dma_owner.sem is None:
                dma_owner.sem = self.free_sems.pop()
                self.dma_bufs.append(dma_owner)
            o.sem = dma_owner.sem
            o.sem.count += 16
            o.val = o.sem.count
        self.ops[eng].append(o)
        self.last[eng] = o
        return o

    def emit(self, final=False):
        nc = self.nc
        self.regcache = {}
        for e in ENGS:
            for o in self.ops[e]:
                for d in o.deps:
                    if not d.dma and not (d.eng == e and (e == "tensor" or not SAME_ENGINE_SYNC)):
                        d.signal = True
        for e in ENGS:
            if self.last[e] is not None and not self.last[e].dma:
                self.last[e].signal = True
        for e in ENGS:
            for o in self.ops[e]:
                if o.dma:
                    continue
                if o.signal:
                    if self.eng_cnt[e] >= EPOCH:
                        self.eng_sem[e] = self.free_sems.pop()
                        self.eng_cnt[e] = 0
                    self.eng_cnt[e] += 1
                    o.sem = self.eng_sem[e]
                    o.val = self.eng_cnt[e]
        end_tokens = []
        for e in ENGS:
            if self.last[e] is not None and not self.last[e].dma:
                end_tokens.append(self.last[e])
        dma_final = [(b.sem, b.sem.count) for b in self.dma_bufs]

        def run(e, eng):
            waited = self.waited[e]
            for o in self.ops[e]:
                need = {}
                for d in o.deps:
                    if d.sem is None:
                        continue
                    if (not d.dma) and d.eng == e:
                        if e == "tensor" or not SAME_ENGINE_SYNC:
                            continue
                    k = d.sem
                    if need.get(k, 0) < d.val:
                        need[k] = d.val
                for k, v in need.items():
                    if waited.get(k, 0) >= v:
                        continue
                    eng.wait_ge(k.h, v)
                    waited[k] = v
                ins = o.fn(eng)
                if o.dma:
                    ins.then_inc(o.sem.h, 16)
                elif o.signal:
                    ins.then_inc(o.sem.h, 1)
            if e == "sync":
                for s, v in dma_final:
                    if waited.get(s, 0) < v:
                        eng.wait_ge(s.h, v)
                        waited[s] = v
                for t in end_tokens:
                    if t.eng != e and waited.get(t.sem, 0) < t.val:
                        eng.wait_ge(t.sem.h, t.val)
                        waited[t.sem] = t.val

        with nc.Block() as block:
            @block.tensor
            def _(eng):
                run("tensor", eng)

            @block.vector
            def _(eng):
                run("vector", eng)

            @block.scalar
            def _(eng):
                run("scalar", eng)

            @block.gpsimd
            def _(eng):
                run("gpsimd", eng)

            @block.sync
            def _(eng):
                run("sync", eng)

        for b in self.dma_bufs:
            self.free_sems.insert(0, b.sem)
            b.sem = None
        self.dma_bufs = []
        self.ops = {e: [] for e in ENGS}
        self.last = {e: None for e in ENGS}


def apx(ap, extra):
    return bass.AP(ap.tensor, ap.offset, [list(x) for x in ap.ap] + [list(x) for x in extra])


def ffn_phase(nc, P, tag, src, dst, g_ap, w_in, w_out, T, final_g=None):
    NB = T // 256
    with ExitStack() as es:
        def sb(name, shape, dt):
            return es.enter_context(nc.sbuf_tensor(f"{tag}_{name}", shape, dt))

        def ps(name, shape, dt):
            return es.enter_context(nc.psum_tensor(f"{tag}_{name}", shape, dt))

        WIN = sb("win", [128, 8, 2 * FF], BF16)
        WOUT = sb("wout", [128, NJ, D], BF16)
        G = sb("g", [128, 8], F32)
        IDENT = sb("ident", [128, 128], BF16)
        NEGH = sb("negh", [128, 1], F32)
        XB = [sb(f"xb{i}", [128, 2, D], F32) for i in range(3)]
        XN = [sb(f"xn{i}", [128, D], BF16) for i in range(2)]
        XNT = [sb(f"xnt{i}", [128, 8, 256], BF16) for i in range(2)]
        SQ = sb("sq", [128, D], BF16)
        ST = [sb(f"st{i}", [128, 4], F32) for i in range(4)]
        SIL = [sb(f"sil{i}", [128, 256], F32) for i in range(2)]
        HT = [sb(f"ht{i}", [128, 256], BF16) for i in range(3)]
        if final_g is not None:
            FG = sb("fg", [128, D], F32)
            OUTT = [sb(f"outt{i}", [128, D], F32) for i in range(2)]
        TPS = ps("tps", [128, 4, 256], BF16)
        GU = [ps(f"gu{i}", [128, 512], F32) for i in range(3)]
        ACC = [ps(f"acc{i}", [128, 512], F32) for i in range(4)]

        bWIN = [[P.buf(f"win{k}_{h}") for h in range(4)] for k in range(8)]
        bWOUT = [P.buf(f"wout{i}") for i in range(NJ)]
        bG = P.buf("g")
        bC = P.buf("const")
        bXB = [P.buf(f"xb{i}") for i in range(3)]
        bXN = [P.buf(f"xn{i}") for i in range(2)]
        bXNT = [P.buf(f"xnt{i}") for i in range(2)]
        bSQ = P.buf("sq")
        bST = [P.buf(f"st{i}") for i in range(4)]
        bSIL = [P.buf(f"sil{i}") for i in range(2)]
        bHT = [P.buf(f"ht{i}") for i in range(3)]
        bTPS = P.buf("tps")
        bGU = [P.buf(f"gu{i}") for i in range(3)]
        bACC = [P.buf(f"acc{i}") for i in range(4)]
        bDST = P.buf("dst")
        if final_g is not None:
            bFG = P.buf("fg")
            bOUTT = [P.buf(f"outt{i}") for i in range(2)]

        P.op("gpsimd", lambda e: e.memset(IDENT[:], 1.0), writes=[bC])
        P.op("gpsimd", lambda e: e.affine_select(out=IDENT[:], in_=IDENT[:], pattern=[[-1, 128]],
                                                 compare_op=ALU.is_equal, fill=P.reg(e, 0.0), base=0, channel_multiplier=1),
             writes=[bC])
        P.op("gpsimd", lambda e: e.memset(NEGH[:], -0.5), writes=[bC])
        P.op("sync", lambda e: e.dma_start(out=G[:], in_=g_ap.rearrange("(c p) -> p c", p=128),
                                           allow_slow_non_contiguous=True), writes=[bG], dma_owner=bG)
        if final_g is not None:
            P.op("sync", lambda e: e.dma_start(out=FG[:], in_=final_g.partition_broadcast(128)),
                 writes=[bFG], dma_owner=bFG)

        def load(b):
            s = b % 3
            for t in range(2):
                r0 = (2 * b + t) * 128
                P.op("sync", lambda e, s=s, t=t, r0=r0: e.dma_start(out=XB[s][:, t, :], in_=src[r0:r0 + 128, :]),
                     writes=[bXB[s]], dma_owner=bXB[s])

        load(0)
        for hh in (0, 2, 1, 3):
            for k in range(8):
                c0 = hh * 1408
                P.op("gpsimd", lambda e, k=k, c0=c0: e.dma_start(out=WIN[:, k, c0:c0 + 1408],
                                                                in_=w_in[k * 128:(k + 1) * 128, c0:c0 + 1408]),
                     writes=[bWIN[k][hh]], dma_owner=bWIN[k][hh])
        if NB > 1:
            load(1)
        for j in range(NJ):
            P.op("gpsimd", lambda e, j=j: e.dma_start(out=WOUT[:, j, :], in_=w_out[j * 128:(j + 1) * 128, :]),
                 writes=[bWOUT[j]], dma_owner=bWOUT[j])

        def norm_tile(x_ap, bx, si, out_ap, bout, extra_reads=()):
            st, bst = ST[si], bST[si]
            P.op("scalar", lambda e: e.activation(out=SQ[:], in_=x_ap, func=AF.Square, accum_out=st[:, 0:1]),
                 reads=[bx], writes=[bSQ, bst])
            P.op("vector", lambda e: e.tensor_scalar(out=st[:, 1:2], in0=st[:, 0:1], scalar1=1.0 / D, scalar2=EPS,
                                                     op0=ALU.mult, op1=ALU.add), reads=[bst], writes=[bst])
            P.op("gpsimd", lambda e: e.tensor_tensor(out=st[:, 2:3], in0=st[:, 1:2], in1=NEGH[:], op=ALU.pow),
                 reads=[bst, bC], writes=[bst])
            P.op("scalar", lambda e: e.activation(out=out_ap, in_=x_ap, func=AF.Copy, scale=st[:, 2:3]),
                 reads=[bx, bst] + list(extra_reads), writes=[bout])

        def prologue(b):
            s = b % 3
            x2 = b % 2
            for t in range(2):
                norm_tile(XB[s][:, t, :], bXB[s], (2 * b + t) % 4, XN[t][:], bXN[t])
            for h in range(2):
                for t in range(2):
                    for kk in range(4):
                        k = 4 * h + kk
                        P.op("tensor", lambda e, t=t, k=k, kk=kk: e.transpose(
                            out=TPS[:, kk, t * 128:(t + 1) * 128], in_=XN[t][:, k * 128:(k + 1) * 128],
                            identity=IDENT[:]), reads=[bXN[t], bC], writes=[bTPS])
                for kk in range(4):
                    k = 4 * h + kk
                    P.op("vector", lambda e, k=k, kk=kk, x2=x2: e.tensor_scalar(
                        out=XNT[x2][:, k, :], in0=TPS[:, kk, :], scalar1=G[:, k:k + 1], scalar2=None, op0=ALU.mult),
                         reads=[bTPS, bG], writes=[bXNT[x2]])

        def gu(b, j):
            x2 = b % 2
            gb = (b * NJ + j) % 3
            for half in range(2):
                c0 = half * FF + j * 128
                for k in range(8):
                    P.op("tensor", lambda e, k=k, c0=c0, half=half, gb=gb, x2=x2: e.matmul(
                        GU[gb][:, half * 256:(half + 1) * 256], lhsT=WIN[:, k, c0:c0 + 128], rhs=XNT[x2][:, k, :],
                        start=(k == 0), stop=(k == 7)), reads=[bWIN[k][c0 // 1408], bXNT[x2]], writes=[bGU[gb]])

        def second(b, j):
            gb = (b * NJ + j) % 3
            hb = (b * NJ + j) % 3
            sl = (b * NJ + j) % 2
            P.op("scalar", lambda e: e.activation(out=SIL[sl][:], in_=GU[gb][:, 0:256], func=AF.Silu),
                 reads=[bGU[gb]], writes=[bSIL[sl]])
            P.op("vector", lambda e: e.tensor_tensor(out=HT[hb][:], in0=SIL[sl][:], in1=GU[gb][:, 256:512],
                                                     op=ALU.mult), reads=[bSIL[sl], bGU[gb]], writes=[bHT[hb]])
            for t in range(2):
                for n in range(2):
                    P.op("tensor", lambda e, t=t, n=n: e.matmul(
                        ACC[t * 2 + n][:], lhsT=HT[hb][:, t * 128:(t + 1) * 128], rhs=WOUT[:, j, n * 512:(n + 1) * 512],
                        start=(j == 0), stop=(j == NJ - 1)), reads=[bHT[hb], bWOUT[j]], writes=[bACC[t * 2 + n]])

        def epilogue(b):
            s = b % 3
            for t in range(2):
                for n in range(2):
                    P.op("vector", lambda e, t=t, n=n: e.scalar_tensor_tensor(
                        out=XB[s][:, t, n * 512:(n + 1) * 512], in0=ACC[t * 2 + n][:], scalar=0.5,
                        in1=XB[s][:, t, n * 512:(n + 1) * 512], op0=ALU.mult, op1=ALU.add),
                         reads=[bACC[t * 2 + n], bXB[s]], writes=[bXB[s]])
                r0 = (2 * b + t) * 128
                if final_g is None:
                    P.op("sync", lambda e, t=t, r0=r0: e.dma_start(out=dst[r0:r0 + 128, :], in_=XB[s][:, t, :]),
                         reads=[bXB[s]], writes=[bDST], dma_owner=bXB[s])
                else:
                    o2 = (2 * b + t) % 2
                    norm_tile(XB[s][:, t, :], bXB[s], (2 * b + t) % 4, OUTT[o2][:], bOUTT[o2])
                    P.op("vector", lambda e, o2=o2: e.tensor_tensor(out=OUTT[o2][:], in0=OUTT[o2][:], in1=FG[:],
                                                                    op=ALU.mult),
                         reads=[bOUTT[o2], bFG], writes=[bOUTT[o2]])
                    P.op("sync", lambda e, o2=o2, r0=r0: e.dma_start(out=dst[r0:r0 + 128, :], in_=OUTT[o2][:]),
                         reads=[bOUTT[o2]], writes=[bDST], dma_owner=bOUTT[o2])

        prologue(0)
        gu(0, 0)
        for b in range(NB):
            if b + 2 < NB:
                load(b + 2)
            for j in range(NJ):
                if j + 1 < NJ:
                    gu(b, j + 1)
                elif b + 1 < NB:
                    gu(b + 1, 0)
                if j == 6 and b + 1 < NB:
                    prologue(b + 1)
                second(b, j)
            epilogue(b)
        P.emit()


def norm_ops(P, x_ap, bx, SQ, bSQ, st, bst, NEGH, bC, out_ap, bout):
    P.op("scalar", lambda e: e.activation(out=SQ[:], in_=x_ap, func=AF.Square, accum_out=st[:, 0:1]),
         reads=[bx], writes=[bSQ, bst])
    P.op("vector", lambda e: e.tensor_scalar(out=st[:, 1:2], in0=st[:, 0:1], scalar1=1.0 / D, scalar2=EPS,
                                             op0=ALU.mult, op1=ALU.add), reads=[bst], writes=[bst])
    P.op("gpsimd", lambda e: e.tensor_tensor(out=st[:, 2:3], in0=st[:, 1:2], in1=NEGH[:], op=ALU.pow),
         reads=[bst, bC], writes=[bst])
    P.op("scalar", lambda e: e.activation(out=out_ap, in_=x_ap, func=AF.Copy, scale=st[:, 2:3]),
         reads=[bx, bst], writes=[bout])


KOFF = [512, 640, 768, 1024, 1816]
VOFF = [896, 1152, 1944]


def proj_phase(nc, P, h1, g_ap, w_mix, S, T):
    NB = T // 512
    with ExitStack() as es:
        def sb(name, shape, dt):
            return es.enter_context(nc.sbuf_tensor(f"pp_{name}", shape, dt))

        def ps(name, shape, dt):
            return es.enter_context(nc.psum_tensor(f"pp_{name}", shape, dt))

        WM = sb("wm", [128, 8, PROJ_W], BF16)
        WMQ = sb("wmq", [128, 8, 1024], BF16)
        G = sb("g", [128, 8], F32)
        IDENT = sb("ident", [128, 128], BF16)
        NEGH = sb("negh", [128, 1], F32)
        HB = [sb(f"hb{i}", [128, 4, D], F32) for i in range(2)]
        XN = [sb(f"xn{i}", [128, D], BF16) for i in range(4)]
        XNT = [sb(f"xnt{i}", [128, 8, 512], BF16) for i in range(2)]
        SQ = sb("sq", [128, D], BF16)
        ST = [sb(f"st{i}", [128, 4], F32) for i in range(4)]
        QST = [sb(f"qst{i}", [128, 4, 4, 128], BF16) for i in range(2)]
        QBST = [sb(f"qbst{i}", [128, 4, 4, 128], BF16) for i in range(2)]
        KST = [sb(f"kst{i}", [128, 5, 512], BF16) for i in range(2)]
        VST = [sb(f"vst{i}", [128, 4, 3, 2, 65], BF16) for i in range(2)]
        GST = [sb(f"gst{i}", [128, 4, 24], F32) for i in range(2)]
        TPS = [ps(f"tps{i}", [128, 2, 512], BF16) for i in range(2)]
        PJ = [ps(f"pj{i}", [128, 512], F32) for i in range(3)]
        PT = [ps(f"pt{i}", [128, 512], F32) for i in range(2)]

        bWM = [P.buf(f"wm{k}") for k in range(8)]
        bWMQ = [P.buf(f"wmq{k}") for k in range(8)]
        bG = P.buf("g"); bC = P.buf("c")
        bHB = [P.buf("hb") for i in range(2)]
        bXN = [P.buf("xn") for i in range(4)]
        bXNT = [P.buf("xnt") for i in range(2)]
        bSQ = P.buf("sq")
        bST = [P.buf("st") for i in range(4)]
        bQST = [P.buf("qst") for i in range(2)]
        bQBST = [P.buf("qbst") for i in range(2)]
        bKST = [P.buf("kst") for i in range(2)]
        bVST = [P.buf("vst") for i in range(2)]
        bGST = [P.buf("gst") for i in range(2)]
        bTPS = [P.buf("tps") for i in range(2)]
        bPJ = [P.buf("pj") for i in range(3)]
        bPT = [P.buf("pt") for i in range(2)]
        bOUT = P.buf("out")

        P.op("gpsimd", lambda e: e.memset(IDENT[:], 1.0), writes=[bC])
        P.op("gpsimd", lambda e: e.affine_select(out=IDENT[:], in_=IDENT[:], pattern=[[-1, 128]],
                                                 compare_op=ALU.is_equal, fill=P.reg(e, 0.0), base=0, channel_multiplier=1),
             writes=[bC])
        P.op("gpsimd", lambda e: e.memset(NEGH[:], -0.5), writes=[bC])
        for i in range(2):
            P.op("gpsimd", lambda e, i=i: e.memset(VST[i][:], 1.0), writes=[bVST[i]])
        P.op("sync", lambda e: e.dma_start(out=G[:], in_=g_ap.rearrange("(c p) -> p c", p=128),
                                           allow_slow_non_contiguous=True), writes=[bG], dma_owner=bG)

        def load(b):
            s = b % 2
            for t in range(4):
                r0 = (4 * b + t) * 128
                P.op("sync", lambda e, t=t, r0=r0: e.dma_start(out=HB[s][:, t, :], in_=h1[r0:r0 + 128, :]),
                     writes=[bHB[s]], dma_owner=bHB[s])

        load(0)
        for k in range(8):
            rows = slice(k * 128, (k + 1) * 128)
            for hh in range(2):
                c0 = hh * 1036
                P.op("gpsimd", lambda e, k=k, rows=rows, c0=c0: e.dma_start(
                    out=WM[:, k, c0:c0 + 1036], in_=w_mix[rows, c0:c0 + 1036]), writes=[bWM[k]], dma_owner=bWM[k])
        for k in range(8):
            for h, base in enumerate((0, 1304)):
                eng = "vector" if h == 0 else "gpsimd"
                P.op(eng, lambda e, k=k, h=h, base=base: e.tensor_copy(
                    out=WMQ[:, k, h * 512:(h + 1) * 512].rearrange("p (r g d) -> p r g d", r=4, g=2),
                    in_=WM[:, k, base:base + 512].rearrange("p (g r d) -> p r g d", g=2, r=4)),
                     reads=[bWM[k]], writes=[bWMQ[k]])
        cnt = {"pj": 0, "pt": 0, "tp": 0}
        def pro(b):
            s = b % 2
            if b + 1 < NB:
                load(b + 1)
            for t in range(4):
                norm_ops(P, HB[s][:, t, :], bHB[s], SQ, bSQ, ST[t], bST[t], NEGH, bC, XN[t][:], bXN[t])
            for kp in range(4):
                tp = cnt["tp"] % 2
                cnt["tp"] += 1
                for t in range(4):
                    for kk in range(2):
                        k = 2 * kp + kk
                        P.op("tensor", lambda e, t=t, k=k, kk=kk, tp=tp: e.transpose(
                            out=TPS[tp][:, kk, t * 128:(t + 1) * 128], in_=XN[t][:, k * 128:(k + 1) * 128],
                            identity=IDENT[:]), reads=[bXN[t], bC], writes=[bTPS[tp]])
                for kk in range(2):
                    k = 2 * kp + kk
                    P.op("vector", lambda e, k=k, kk=kk, tp=tp: e.tensor_scalar(
                        out=XNT[s][:, k, :], in0=TPS[tp][:, kk, :], scalar1=G[:, k:k + 1], scalar2=None,
                        op0=ALU.mult), reads=[bTPS[tp], bG], writes=[bXNT[s]])

        def block(b):
            s = b % 2

            def fm_group(colfn, evac, ei, bsrc=bWM):
                pj = cnt["pj"] % 3
                cnt["pj"] += 1
                for k in range(8):
                    P.op("tensor", lambda e, k=k, pj=pj: e.matmul(PJ[pj][:], lhsT=colfn(k), rhs=XNT[s][:, k, :],
                                                                 start=(k == 0), stop=(k == 7)),
                         reads=[bsrc[k], bXNT[s]], writes=[bPJ[pj]])
                evac(pj, ei)

            def qcols(base):
                def f(k):
                    return WMQ[:, k, base:base + 128]
                return f

            ei = 0
            for r in range(4):
                for (base, QS, bQS) in ((0, QST, bQST), (512, QBST, bQBST)):
                    def evac(pj, ei, r=r, QS=QS, bQS=bQS):
                        src = PJ[pj][:].rearrange("p (n t) -> p n t", t=128)
                        if ei % 2 == 0:
                            P.op("scalar", lambda e: e.activation(out=QS[s][:, :, r, :], in_=src, func=AF.Copy,
                                                                  scale=0.125), reads=[bPJ[pj]], writes=[bQS[s]])
                        else:
                            P.op("vector", lambda e: e.tensor_scalar(out=QS[s][:, :, r, :], in0=src, scalar1=0.125,
                                                                     scalar2=None, op0=ALU.mult),
                                 reads=[bPJ[pj]], writes=[bQS[s]])
                    fm_group(qcols(base + r * 128), evac, ei, bsrc=bWMQ)
                    ei += 1
            if b + 1 < NB:
                pro(b + 1)
            for w in range(5):
                def evac(pj, ei, w=w):
                    if ei % 2 == 0:
                        P.op("scalar", lambda e: e.copy(out=KST[s][:, w, :], in_=PJ[pj][:]),
                             reads=[bPJ[pj]], writes=[bKST[s]])
                    else:
                        P.op("vector", lambda e: e.tensor_copy(out=KST[s][:, w, :], in_=PJ[pj][:]),
                             reads=[bPJ[pj]], writes=[bKST[s]])
                fm_group(lambda k, w=w: WM[:, k, KOFF[w]:KOFF[w] + 128], evac, ei)
                ei += 1
            for t in range(4):
                pt = cnt["pt"] % 2
                cnt["pt"] += 1
                for (c0, n, o0) in ((896, 128, 0), (1152, 152, 128), (1944, 128, 280)):
                    for k in range(8):
                        P.op("tensor", lambda e, k=k, c0=c0, n=n, o0=o0, pt=pt, t=t: e.matmul(
                            PT[pt][:, o0:o0 + n], lhsT=XNT[s][:, k, t * 128:(t + 1) * 128], rhs=WM[:, k, c0:c0 + n],
                            start=(k == 0), stop=(k == 7)), reads=[bWM[k], bXNT[s]], writes=[bPT[pt]])
                for wi, o0 in enumerate((0, 128, 280)):
                    src = PT[pt][:, o0:o0 + 128].rearrange("p (g d) -> p g d", d=64)
                    if wi == 1:
                        P.op("scalar", lambda e, src=src, wi=wi, t=t: e.copy(out=VST[s][:, t, wi, :, 0:64], in_=src),
                             reads=[bPT[pt]], writes=[bVST[s]])
                    else:
                        P.op("vector", lambda e, src=src, wi=wi, t=t: e.tensor_copy(out=VST[s][:, t, wi, :, 0:64],
                                                                                   in_=src),
                             reads=[bPT[pt]], writes=[bVST[s]])
                P.op("vector", lambda e, t=t, pt=pt: e.tensor_copy(out=GST[s][:, t, :], in_=PT[pt][:, 256:280]),
                     reads=[bPT[pt]], writes=[bGST[s]])
            c0 = b * 2048
            P.op("sync", lambda e, c0=c0: e.dma_start(out=S["QA"][:, c0:c0 + 2048],
                                                      in_=QST[s][:].rearrange("p n r t -> p (n r t)")),
                 reads=[bQST[s]], writes=[bOUT], dma_owner=bQST[s])
            P.op("sync", lambda e, c0=c0: e.dma_start(out=S["QB"][:, c0:c0 + 2048],
                                                      in_=QBST[s][:].rearrange("p n r t -> p (n r t)")),
                 reads=[bQBST[s]], writes=[bOUT], dma_owner=bQBST[s])
            for w in range(5):
                P.op("sync", lambda e, w=w: e.dma_start(out=S["KT"][w, :, b * 512:(b + 1) * 512], in_=KST[s][:, w, :]),
                     reads=[bKST[s]], writes=[bOUT], dma_owner=bKST[s])
            for w in range(3):
                P.op("sync", lambda e, w=w: e.dma_start(out=S["VT"][w, :, 4 * b:4 * b + 4, :, :],
                                                        in_=VST[s][:, :, w, :, :]),
                     reads=[bVST[s]], writes=[bOUT], dma_owner=bVST[s])
            P.op("sync", lambda e: e.dma_start(
                out=S["GT"].rearrange("(n p) c -> p n c", p=128)[:, 4 * b:4 * b + 4, :], in_=GST[s][:]),
                 reads=[bGST[s]], writes=[bOUT], dma_owner=bGST[s])

        pro(0)
        for b in range(NB):
            block(b)
        P.emit()


def attn_phase(nc, P, h1, h2, S, C, W, T):
    NT = T // 128
    NCC = T // 16
    NCH = NCC // 128
    with ExitStack() as es:
        def sb(name, shape, dt):
            return es.enter_context(nc.sbuf_tensor(f"at_{name}", shape, dt))

        def ps(name, shape, dt):
            return es.enter_context(nc.psum_tensor(f"at_{name}", shape, dt))

        KG = [[sb(f"kg{br}{g}", [68, T], BF16) for g in range(2)] for br in range(3)]
        VR = [sb(f"vr{br}", [128, NT, 2, 65], BF16) for br in range(3)]
        KCG = [sb(f"kcg{g}", [68, NCC], BF16) for g in range(2)]
        VCX = [sb(f"vcx{g}", [128, NCH, 129], BF16) for g in range(2)]
        RAWT = sb("rawt", [128, 2, T], BF16)
        W1 = [sb(f"w1{i}", [128, 32, 256], BF16) for i in range(2)]
        W2 = [sb(f"w2{i}", [128, 2, 64], BF16) for i in range(2)]
        POSF = [sb(f"posf{i}", [64, 32], F32) for i in range(2)]
        POST = [sb(f"post{i}", [64, 32], BF16) for i in range(2)]
        B1 = [sb(f"b1{i}", [128, 2], F32) for i in range(2)]
        BIAS = [sb(f"bias{i}", [128, 2], F32) for i in range(2)]
        GX = sb("gx", [128, NCC], F32)
        T1 = sb("t1", [128, NCC], F32)
        HTC = [sb(f"htc{i}", [128, NCC], BF16) for i in range(2)]
        WO = sb("wo", [128, 8, D], BF16)
        EBIG = sb("ebig", [128, NT * 128], BF16)
        IDENT = sb("ident", [128, 128], BF16)
        CAUS = sb("caus", [128, 4, 128], BF16)
        CAUSC = sb("causc", [128, 4, 128], BF16)
        ZER = sb("zer", [128, 4, 128], BF16)
        SINKE = sb("sinke", [128, 8, 1], F32)
        QT = [[sb(f"qt{q}{i}", [68, 512], BF16) for i in range(2)] for q in range(4)]
        GATE = [sb(f"gate{i}", [128, 24], F32) for i in range(2)]
        FT = [sb(f"ft{i}", [128, 2, 64], F32) for i in range(2)]
        H1T = [sb(f"h1t{i}", [128, D], F32) for i in range(2)]
        PTT = [sb(f"ptt{i}", [128, 512], BF16) for i in range(4)]
        MASKC = [sb(f"maskc{i}", [128, 4, 128], BF16) for i in range(2)]
        NEGMT = [sb(f"negmt{g}", [128, 4, 128], BF16) for g in range(2)]
        NEGMS = [sb(f"negm{g}", [128, 64], BF16) for g in range(2)]
        OMIX = sb("omix", [128, 16, 64], F32)
        OMIXB = sb("omixb", [128, D], BF16)
        OT = sb("ot", [128, 8, 128], BF16)
        SG = sb("sg", [128, 24], F32)
        OAS = sb("oas", [128, 4, 129], F32)
        L4 = sb("l4", [128, 4, 1], F32)
        RL4 = sb("rl4", [128, 4, 1], F32)
        W4 = sb("w4", [128, 4, 1], F32)
        TMP = sb("tmp", [128, 4, 64], F32)
        PSLC = sb("pslc", [128, 64], F32)
        SCORE = sb("score", [128, 64], F32)
        WORK = sb("work", [128, 64], F32)
        M1 = sb("m1", [128, 8], F32)
        M2 = sb("m2", [128, 8], F32)

        NSTP = 3
        DEPTH = 3
        STP = [ps(f"stp{i}", [128, 512], F32) for i in range(NSTP)]
        OA = ps("oa", [128, 4, 512], F32)
        MF = ps("mf", [128, 512], F32)
        MBV = MF.bitcast(BF16)

        def MBc(c):
            return MBV[:, c * 128:(c + 1) * 128]

        B = P.buf
        bKG = [[B("kg") for g in range(2)] for br in range(3)]
        bVR = [B("vr") for br in range(3)]
        bKCG = [B("kcg") for g in range(2)]
        bVCX = [B("vcx") for g in range(2)]
        bRAWT = B("rawt"); bW1 = [B("w1"), B("w1")]; bW2 = [B("w2"), B("w2")]
        bPOSF = [B("pf"), B("pf")]; bPOST = [B("pt"), B("pt")]; bB1 = [B("b1"), B("b1")]; bBIAS = [B("bi"), B("bi")]
        bGX = B("gx"); bT1 = B("t1"); bHTC = [B("htc"), B("htc")]
        bWO = B("wo"); bC = B("c"); bSINK = B("sink")
        bQT = [[B("qt") for i in range(2)] for q in range(4)]
        bGATE = [B("gate"), B("gate")]; bFT = [B("ft"), B("ft")]; bH1T = [B("h1t"), B("h1t")]
        bPTT = [B("ptt") for i in range(4)]
        bMASKC = [B("mc"), B("mc")]
        bNEGMT = [B("nm"), B("nm")]; bNEGMS = [B("negm"), B("negm")]
        bOMIX = B("omix"); bOMIXB = B("omixb"); bOT = B("ot"); bSG = B("sg")
        bOAS = B("oas"); bL4 = B("l4"); bTMP = B("tmp"); bPSLC = B("pslc"); bSCORE = B("score"); bWORK = B("work")
        bM = B("m")
        bSTP = [B("stp") for _ in range(3)]; bOA = B("oa"); bMF = B("mf"); bMB = bMF
        bH2 = B("h2")

        P.op("gpsimd", lambda e: e.memset(IDENT[:], 1.0), writes=[bC])
        P.op("gpsimd", lambda e: e.affine_select(out=IDENT[:], in_=IDENT[:], pattern=[[-1, 128]],
                                                 compare_op=ALU.is_equal, fill=P.reg(e, 0.0), base=0, channel_multiplier=1),
             writes=[bC])
        P.op("gpsimd", lambda e: e.memset(EBIG[:], 1.0), writes=[bC])
        P.op("gpsimd", lambda e: e.affine_select(out=EBIG[:], in_=EBIG[:], pattern=[[-1, NT * 2], [0, 64]],
                                                 compare_op=ALU.is_equal, fill=P.reg(e, 0.0), base=0, channel_multiplier=1),
             writes=[bC])
        P.op("gpsimd", lambda e: e.memset(ZER[:], 0.0), writes=[bC])
        P.op("gpsimd", lambda e: e.memset(CAUS[:], NEG), writes=[bC])
        P.op("gpsimd", lambda e: e.affine_select(out=CAUS[:], in_=CAUS[:], pattern=[[0, 4], [-1, 128]],
                                                 compare_op=ALU.is_gt, fill=P.reg(e, 0.0), base=0, channel_multiplier=1),
             writes=[bC])
        P.op("gpsimd", lambda e: e.memset(CAUSC[:], NEG), writes=[bC])
        P.op("gpsimd", lambda e: e.affine_select(out=CAUSC[:], in_=CAUSC[:], pattern=[[0, 4], [1, 128]],
                                                 compare_op=ALU.is_ge, fill=P.reg(e, 0.0), base=0, channel_multiplier=-1),
             writes=[bC])
        for g in range(2):
            P.op("gpsimd", lambda e, g=g: e.memset(NEGMT[g][:], 0.0), writes=[bNEGMT[g]])
            P.op("gpsimd", lambda e, g=g: e.memset(KCG[g][:], 0.0), writes=[bKCG[g]])
            P.op("gpsimd", lambda e, g=g: e.memset(VCX[g][:], 0.0), writes=[bVCX[g]])
            P.op("gpsimd", lambda e, g=g: e.memset(VCX[g][:, :, 64:65], 1.0), writes=[bVCX[g]])

        P.op("sync", lambda e: e.dma_start(out=RAWT[:, 0, :], in_=S["KT"][0, :, :]), writes=[bRAWT], dma_owner=bRAWT)
        P.op("sync", lambda e: e.dma_start(out=RAWT[:, 1, :], in_=S["KT"][1, :, :]), writes=[bRAWT], dma_owner=bRAWT)
        for i in range(2):
            w1 = W["w1"][i].rearrange("(j d) h -> d j h", d=64)
            for half in range(2):
                for jj in range(4):
                    P.op("gpsimd", lambda e, i=i, half=half, jj=jj, w1=w1: e.dma_start(
                        out=W1[i][half * 64:(half + 1) * 64, jj * 8:(jj + 1) * 8, :], in_=w1[:, jj * 8:(jj + 1) * 8, :]),
                         writes=[bW1[i]], dma_owner=bW1[i])
            P.op("gpsimd", lambda e, i=i: e.dma_start(out=W2[i][:], in_=W["w2"][i].rearrange("(c p) d -> p c d", p=128)),
                 writes=[bW2[i]], dma_owner=bW2[i])
            P.op("sync", lambda e, i=i: e.dma_start(out=POSF[i][:], in_=W["pos"][i].rearrange("j d -> d j"),
                                                    allow_slow_non_contiguous=True), writes=[bPOSF[i]], dma_owner=bPOSF[i])
            P.op("sync", lambda e, i=i: e.dma_start(out=B1[i][:], in_=W["b1"][i].rearrange("(c p) -> p c", p=128),
                                                    allow_slow_non_contiguous=True), writes=[bB1[i]], dma_owner=bB1[i])
            P.op("vector", lambda e, i=i: e.tensor_copy(out=POST[i][:], in_=POSF[i][:]), reads=[bPOSF[i]],
                 writes=[bPOST[i]])
        for g in range(2):
            P.op("sync", lambda e, g=g: e.dma_start(out=KCG[g][64:68, :], in_=C["kaugc"][:, :]),
                 writes=[bKCG[g]], dma_owner=bKCG[g])
            P.op("sync", lambda e, g=g: e.dma_start(out=VCX[g][:, :, 65:129],
                                                    in_=C["ovl"].rearrange("(n p) s -> p n s", p=128)),
                 writes=[bVCX[g]], dma_owner=bVCX[g])
        for br, w in enumerate((2, 3, 4)):
            for g in range(2):
                P.op("sync", lambda e, br=br, w=w, g=g: e.dma_start(out=KG[br][g][0:64, :],
                                                                   in_=S["KT"][w, g * 64:(g + 1) * 64, :]),
                     writes=[bKG[br][g]], dma_owner=bKG[br][g])
                P.op("sync", lambda e, br=br, g=g: e.dma_start(out=KG[br][g][64:68, :], in_=C["kaug"][:, :]),
                     writes=[bKG[br][g]], dma_owner=bKG[br][g])
            P.op("sync", lambda e, br=br: e.dma_start(out=VR[br][:].rearrange("p n g d -> p (n g d)"),
                                                      in_=S["VT"][br].rearrange("p n g d -> p (n g d)")),
                 writes=[bVR[br]], dma_owner=bVR[br])
        for k in range(8):
            P.op("gpsimd", lambda e, k=k: e.dma_start(out=WO[:, k, :], in_=W["wo"][k * 128:(k + 1) * 128, :]),
                 writes=[bWO], dma_owner=bWO)
        P.op("sync", lambda e: e.dma_start(out=SINKE[:, :, 0], in_=W["sinks"].partition_broadcast(128)),
             writes=[bSINK], dma_owner=bSINK)
        P.op("scalar", lambda e: e.activation(out=SINKE[:], in_=SINKE[:], func=AF.Exp), reads=[bSINK], writes=[bSINK])

        def loads(i):
            s = i % 2
            for q, (src, g) in enumerate((("QA", 0), ("QA", 1), ("QB", 0), ("QB", 1))):
                P.op("sync", lambda e, q=q, src=src, g=g: e.dma_start(
                    out=QT[q][s][0:64, :], in_=S[src][g * 64:(g + 1) * 64, i * 512:(i + 1) * 512]),
                     writes=[bQT[q][s]], dma_owner=bQT[q][s])
                P.op("sync", lambda e, q=q, g=g: e.dma_start(out=QT[q][s][64:68, :],
                                                             in_=C["qaug"][g, :, i * 512:(i + 1) * 512]),
                     writes=[bQT[q][s]], dma_owner=bQT[q][s])
            P.op("sync", lambda e: e.dma_start(out=GATE[s][:], in_=S["GT"][i * 128:(i + 1) * 128, :]),
                 writes=[bGATE[s]], dma_owner=bGATE[s])
            P.op("sync", lambda e: e.dma_start(out=FT[s][:, 0, :], in_=C["ftab"][i, :, :]),
                 writes=[bFT[s]], dma_owner=bFT[s])
            P.op("sync", lambda e: e.dma_start(out=H1T[s][:], in_=h1[i * 128:(i + 1) * 128, :]),
                 writes=[bH1T[s]], dma_owner=bH1T[s])

        loads(0)

        cnt = {"st": 0, "pt": 0, "mc": 0}

        def st_next():
            v = cnt["st"] % 3
            cnt["st"] += 1
            return v

        for i in range(2):
            for hc in range(2):
                for j in range(32):
                    P.op("tensor", lambda e, i=i, hc=hc, j=j: e.matmul(
                        MF[:, hc:hc + 1], lhsT=W1[i][0:64, j, hc * 128:(hc + 1) * 128], rhs=POST[i][:, j:j + 1],
                        start=(j == 0), stop=(j == 31)), reads=[bW1[i], bPOST[i]], writes=[bMF])
            P.op("vector", lambda e, i=i: e.tensor_tensor(out=BIAS[i][:], in0=MF[:, 0:2], in1=B1[i][:], op=ALU.add),
                 reads=[bMF, bB1[i]], writes=[bBIAS[i]])
            for g in range(2):
                for hc in range(2):
                    sp = st_next()
                    for j in range(32):
                        P.op("tensor", lambda e, i=i, g=g, hc=hc, j=j, sp=sp: e.matmul(
                            STP[sp][:, 0:NCC - 1], lhsT=W1[i][g * 64:(g + 1) * 64, j, hc * 128:(hc + 1) * 128],
                            rhs=RAWT[g * 64:(g + 1) * 64, i, j:j + 16 * (NCC - 2) + 1:16],
                            start=(j == 0), stop=(j == 31)), reads=[bW1[i], bRAWT], writes=[bSTP[sp]])
                    n = NCC - 1
                    P.op("scalar", lambda e, i=i, hc=hc, sp=sp: e.activation(
                        out=GX[:, 0:n], in_=STP[sp][:, 0:n], func=AF.Identity, bias=BIAS[i][:, hc:hc + 1]),
                         reads=[bSTP[sp], bBIAS[i]], writes=[bGX])
                    P.op("vector", lambda e: e.tensor_tensor(out=T1[:, 0:n], in0=GX[:, 0:n], in1=GX[:, 0:n], op=ALU.mult),
                         reads=[bGX], writes=[bT1])
                    P.op("vector", lambda e: e.tensor_scalar(out=T1[:, 0:n], in0=T1[:, 0:n], scalar1=0.044715,
                                                             scalar2=1.0, op0=ALU.mult, op1=ALU.add),
                         reads=[bT1], writes=[bT1])
                    P.op("vector", lambda e: e.tensor_tensor(out=T1[:, 0:n], in0=T1[:, 0:n], in1=GX[:, 0:n], op=ALU.mult),
                         reads=[bT1, bGX], writes=[bT1])
                    P.op("scalar", lambda e: e.activation(out=T1[:, 0:n], in_=T1[:, 0:n], func=AF.Exp,
                                                          scale=-1.5957691216057308), reads=[bT1], writes=[bT1])
                    P.op("vector", lambda e: e.tensor_scalar(out=T1[:, 0:n], in0=T1[:, 0:n], scalar1=1.0, scalar2=None,
                                                             op0=ALU.add), reads=[bT1], writes=[bT1])
                    P.op("vector", lambda e: e.reciprocal(out=T1[:, 0:n], in_=T1[:, 0:n]), reads=[bT1], writes=[bT1])
                    P.op("gpsimd", lambda e, hc=hc: e.memset(HTC[hc][:, n:NCC], 0.0), writes=[bHTC[hc]])
                    P.op("vector", lambda e, hc=hc: e.tensor_tensor(out=HTC[hc][:, 0:n], in0=T1[:, 0:n], in1=GX[:, 0:n],
                                                                    op=ALU.mult), reads=[bT1, bGX], writes=[bHTC[hc]])
                if i == 0:
                    for hc in range(2):
                        P.op("tensor", lambda e, hc=hc: e.matmul(MF[0:64, 0:NCC], lhsT=W2[0][:, hc, :], rhs=HTC[hc][:],
                                                                 start=(hc == 0), stop=(hc == 1)),
                             reads=[bW2[0], bHTC[hc]], writes=[bMF])
                    P.op("vector", lambda e, g=g: e.tensor_copy(out=KCG[g][0:64, :], in_=MF[0:64, 0:NCC]),
                         reads=[bMF], writes=[bKCG[g]])
                else:
                    for cc in range(NCH):
                        for hc in range(2):
                            P.op("tensor", lambda e, hc=hc, cc=cc: e.matmul(
                                MF[:, 0:64], lhsT=HTC[hc][:, cc * 128:(cc + 1) * 128], rhs=W2[1][:, hc, :],
                                start=(hc == 0), stop=(hc == 1)), reads=[bW2[1], bHTC[hc]], writes=[bMF])
                        P.op("vector", lambda e, g=g, cc=cc: e.tensor_copy(out=VCX[g][:, cc, 0:64], in_=MF[:, 0:64]),
                             reads=[bMF], writes=[bVCX[g]])

        def unit(i, kt_ap, bk, q, s, extra, vt_ap, bv, first, last, width):
            sp = st_next()
            pt = cnt["pt"] % 4
            cnt["pt"] += 1
            nmm = 1 + len(extra)
            P.op("tensor", lambda e: e.matmul(STP[sp][:], lhsT=kt_ap, rhs=QT[q][s][:], start=True, stop=(nmm == 1)),
                 reads=[bk, bQT[q][s]], writes=[bSTP[sp]])
            for n, (l_ap, r_ap, rb) in enumerate(extra):
                P.op("tensor", lambda e, l_ap=l_ap, r_ap=r_ap, n=n: e.matmul(
                    STP[sp][:], lhsT=l_ap, rhs=r_ap, start=False, stop=(n == nmm - 2)),
                     reads=[bC] + rb, writes=[bSTP[sp]])
            P.op("scalar", lambda e: e.activation(out=PTT[pt][:], in_=STP[sp][:], func=AF.Exp),
                 reads=[bSTP[sp]], writes=[bPTT[pt]])
            return pt

        pend = []

        def flush():
            for f in pend:
                f()
            del pend[:]

        def pv_now(pt, vt_ap, bv, first, last, width):
            for r in range(4):
                P.op("tensor", lambda e, r=r: e.matmul(OA[:, r, 0:width], lhsT=PTT[pt][:, r * 128:(r + 1) * 128],
                                                       rhs=vt_ap, start=first, stop=last),
                     reads=[bPTT[pt], bv], writes=[bOA])

        def pv(pt, vt_ap, bv, first, last, width):
            prev = list(pend)
            del pend[:]
            pend.append(lambda: pv_now(pt, vt_ap, bv, first, last, width))
            for f in prev:
                f()

        def flat(t3):
            return t3[:].rearrange("p r t -> p (r t)")

        def finish(col0, gate_col, first_branch, sink_g=None, width=65, need_rl=False):
            P.op("vector", lambda e: e.tensor_copy(out=OAS[:, :, 0:width], in_=OA[:, :, 0:width]),
                 reads=[bOA], writes=[bOAS])
            if need_rl:
                P.op("vector", lambda e: e.tensor_scalar(out=L4[:], in0=OAS[:, :, 64:65], scalar1=1e-30, scalar2=None,
                                                         op0=ALU.max), reads=[bOAS], writes=[bL4])
                P.op("vector", lambda e: e.reciprocal(out=RL4[:], in_=L4[:]), reads=[bL4], writes=[bL4])
                P.op("vector", lambda e: e.tensor_tensor(out=W4[:, :, 0], in0=RL4[:, :, 0],
                                                         in1=SG[:, gate_col:gate_col + 10:3], op=ALU.mult),
                     reads=[bL4, bSG], writes=[bL4])
            elif sink_g is not None:
                P.op("vector", lambda e: e.tensor_tensor(out=L4[:], in0=OAS[:, :, 64:65],
                                                         in1=SINKE[:, sink_g * 4:sink_g * 4 + 4, :], op=ALU.add),
                     reads=[bOAS, bSINK], writes=[bL4])
                P.op("vector", lambda e: e.reciprocal(out=W4[:], in_=L4[:]), reads=[bL4], writes=[bL4])
            else:
                P.op("vector", lambda e: e.reciprocal(out=RL4[:], in_=OAS[:, :, 64:65]), reads=[bOAS], writes=[bL4])
                P.op("vector", lambda e: e.tensor_tensor(out=W4[:, :, 0], in0=RL4[:, :, 0],
                                                         in1=SG[:, gate_col:gate_col + 10:3], op=ALU.mult),
                     reads=[bL4, bSG], writes=[bL4])
            wb = apx(W4[:, :, 0], [[0, 64]])
            if first_branch:
                P.op("vector", lambda e: e.tensor_tensor(out=OMIX[:, col0:col0 + 4, :], in0=OAS[:, :, 0:64], in1=wb,
                                                         op=ALU.mult), reads=[bOAS, bL4], writes=[bOMIX])
            else:
                P.op("vector", lambda e: e.tensor_tensor(out=TMP[:], in0=OAS[:, :, 0:64], in1=wb, op=ALU.mult),
                     reads=[bOAS, bL4], writes=[bTMP])
                P.op("vector", lambda e: e.tensor_tensor(out=OMIX[:, col0:col0 + 4, :], in0=OMIX[:, col0:col0 + 4, :],
                                                         in1=TMP[:], op=ALU.add), reads=[bTMP, bOMIX], writes=[bOMIX])

        items = []

        def U(qk_fn, pv_fn, needs=None):
            items.append(("u", qk_fn, pv_fn, needs))

        def Bar(fn, provides=None, releases=0):
            items.append(("b", fn, provides, releases))

        def tile(i):
            s = i % 2

            def tile_start():
                P.op("scalar", lambda e: e.activation(out=SG[:], in_=GATE[s][:], func=AF.Exp, scale=-1.0),
                     reads=[bGATE[s]], writes=[bSG])
                P.op("vector", lambda e: e.tensor_scalar(out=SG[:], in0=SG[:], scalar1=1.0, scalar2=None, op0=ALU.add),
                     reads=[bSG], writes=[bSG])
                P.op("vector", lambda e: e.reciprocal(out=SG[:], in_=SG[:]), reads=[bSG], writes=[bSG])
            Bar(tile_start)

            def std_branch(br, g, q, j0, far, col0, gate_col, first_branch, sink_g=None, needs=None):
                for j in range(j0, i + 1):
                    def qk(j=j):
                        extra = []
                        if br == 0:
                            extra.append((EBIG[:, j * 128:(j + 1) * 128], flat(NEGMT[g]), [bNEGMT[g]]))
                        if j == i:
                            extra.append((IDENT[:], flat(CAUS), []))
                        if far is not None and j == i - far:
                            extra.append((IDENT[:], flat(CAUSC), []))
                        return unit(i, KG[br][g][:, j * 128:(j + 1) * 128], bKG[br][g], q, s, extra, None, None, 0, 0, 0)

                    def pvf(pt, j=j):
                        pv_now(pt, VR[br][:, j, g, :], bVR[br], j == j0, j == i, 65)
                    U(qk, pvf, needs)
                Bar(lambda: finish(col0, gate_col, first_branch, sink_g=sink_g))

            def cmp_group(g):
                q = g
                nch = min(NCH, (8 * i + 6) // 128 + 1)
                pts = []
                for cc in range(nch):
                    def qk(cc=cc):
                        shift = 128 * cc - 8 * i
                        extra = []
                        if shift + 127 >= -1:
                            m = cnt["mc"] % 2
                            cnt["mc"] += 1
                            P.op("gpsimd", lambda e, m=m, shift=shift: e.affine_select(
                                out=MASKC[m][:], in_=ZER[:], pattern=[[0, 4], [1, 128]], compare_op=ALU.is_ge,
                                fill=P.reg(e, NEG), base=-31 - 16 * shift, channel_multiplier=-16),
                                 reads=[bC], writes=[bMASKC[m]])
                            extra.append((IDENT[:], flat(MASKC[m]), [bMASKC[m]]))
                        pt = unit(i, KCG[g][:, cc * 128:(cc + 1) * 128], bKCG[g], q, s, extra, None, None, 0, 0, 0)
                        pts.append(pt)
                        return pt
                    U(qk, None)

                def cmp_finish():
                    for r in range(4):
                        for cc in range(nch):
                            P.op("tensor", lambda e, r=r, cc=cc: e.matmul(
                                OA[:, r, 0:129], lhsT=PTT[pts[cc]][:, r * 128:(r + 1) * 128], rhs=VCX[g][:, cc, :],
                                start=(cc == 0), stop=(cc == nch - 1)), reads=[bPTT[pts[cc]], bVCX[g]], writes=[bOA])
                    finish(g * 4, g * 12 + 0, True, width=129, need_rl=True)
                    P.op("vector", lambda e: e.tensor_tensor(out=TMP[:], in0=OAS[:, :, 65:129],
                                                             in1=apx(RL4[:, :, 0], [[0, 64]]), op=ALU.mult),
                         reads=[bOAS, bL4], writes=[bTMP])
                    P.op("vector", lambda e: e.tensor_tensor(out=TMP[:, 0:2, :], in0=TMP[:, 0:2, :], in1=TMP[:, 2:4, :],
                                                             op=ALU.add), reads=[bTMP], writes=[bTMP])
                    P.op("vector", lambda e: e.tensor_tensor(out=PSLC[:], in0=TMP[:, 0, :], in1=TMP[:, 1, :], op=ALU.add),
                         reads=[bTMP], writes=[bPSLC])
                    P.op("vector", lambda e: e.tensor_tensor(out=SCORE[:], in0=PSLC[:], in1=FT[s][:, 0, :], op=ALU.add),
                         reads=[bPSLC, bFT[s]], writes=[bSCORE])
                    P.op("vector", lambda e: e.max(out=M1[:], in_=SCORE[:]), reads=[bSCORE], writes=[bM])
                    P.op("vector", lambda e: e.match_replace(out=WORK[:], in_to_replace=M1[:], in_values=SCORE[:],
                                                             imm_value=-3.0e38), reads=[bSCORE, bM], writes=[bWORK])
                    P.op("vector", lambda e: e.max(out=M2[:], in_=WORK[:]), reads=[bWORK], writes=[bM])
                    P.op("vector", lambda e: e.tensor_scalar(out=NEGMS[g][:], in0=SCORE[:], scalar1=M2[:, 7:8],
                                                             scalar2=NEG, op0=ALU.is_lt, op1=ALU.mult),
                         reads=[bSCORE, bM], writes=[bNEGMS[g]])
                return cmp_finish, nch

            def negmt_make(g):
                def f():
                    P.op("tensor", lambda e: e.transpose(out=MBc(0)[0:64, :], in_=NEGMS[g][:], identity=IDENT[:]),
                         reads=[bNEGMS[g], bC], writes=[bMB])
                    for r in range(4):
                        P.op("scalar", lambda e, r=r: e.copy(out=NEGMT[g][0:64, r, :], in_=MBc(0)[0:64, :]),
                             reads=[bMB], writes=[bNEGMT[g]])
                Bar(f, provides=("negmt", i, g))

            fins = [cmp_group(g) for g in range(2)]
            if i > 0:
                Bar(prev_out[0])
            for (fn, nrel) in fins:
                Bar(fn, releases=nrel)
            if i > 0:
                Bar(prev_out[1])
            for g in range(2):
                std_branch(1, g, g, max(0, i - 4), 4, g * 4, g * 12 + 2, False)
            if i > 0:
                Bar(prev_out[2])
            if i + 1 < NT:
                Bar(lambda: loads(i + 1))
            for g in range(2):
                std_branch(2, g, 2 + g, max(0, i - 1), 1, 8 + g * 4, None, True, sink_g=g)
            for g in range(2):
                negmt_make(g)
                std_branch(0, g, g, 0, None, g * 4, g * 12 + 1, False, needs=("negmt", i, g))

            def out_a():
                P.op("scalar", lambda e: e.copy(out=OMIXB[:], in_=OMIX[:].rearrange("p h d -> p (h d)")),
                     reads=[bOMIX], writes=[bOMIXB])
                for c in range(8):
                    P.op("tensor", lambda e, c=c: e.transpose(out=MBc(c), in_=OMIXB[:, c * 128:(c + 1) * 128],
                                                              identity=IDENT[:]), reads=[bOMIXB, bC], writes=[bMB])
                P.op("vector", lambda e: e.tensor_copy(out=OT[:], in_=MBV[:, :].rearrange("p (c t) -> p c t", c=8)),
                     reads=[bMB], writes=[bOT])

            def out_half(n):
                def f():
                    for c in range(8):
                        P.op("tensor", lambda e, c=c: e.matmul(MF[:], lhsT=OT[:, c, :],
                                                              rhs=WO[:, c, n * 512:(n + 1) * 512],
                                                              start=(c == 0), stop=(c == 7)),
                             reads=[bOT, bWO], writes=[bMF])
                    P.op("vector", lambda e: e.tensor_tensor(out=H1T[s][:, n * 512:(n + 1) * 512], in0=MF[:],
                                                             in1=H1T[s][:, n * 512:(n + 1) * 512], op=ALU.add),
                         reads=[bMF, bH1T[s]], writes=[bH1T[s]])
                    if n == 1:
                        P.op("sync", lambda e: e.dma_start(out=h2[i * 128:(i + 1) * 128, :], in_=H1T[s][:]),
                             reads=[bH1T[s]], writes=[bH2], dma_owner=bH1T[s])
                return f
            stages = (out_a, out_half(0), out_half(1))
            if i == NT - 1:
                for st in stages:
                    Bar(st)
            return stages

        prev_out = None
        for i in range(NT):
            prev_out = tile(i)

        n_items = len(items)
        issued = [False] * n_items
        ptsl = [None] * n_items
        provided = set()
        inflight = 0
        for k in range(n_items):
            it = items[k]
            if it[0] == "b":
                it[1]()
                if it[2] is not None:
                    provided.add(it[2])
                inflight -= it[3]
                continue
            if not issued[k]:
                ptsl[k] = it[1]()
                issued[k] = True
                inflight += 1
            m = k + 1
            while m < n_items and m - k <= 8 and inflight < DEPTH:
                im = items[m]
                if im[0] == "u" and not issued[m]:
                    if im[3] is not None and im[3] not in provided:
                        break
                    ptsl[m] = im[1]()
                    issued[m] = True
                    inflight += 1
                m += 1
            if it[2] is not None:
                it[2](ptsl[k])
                inflight -= 1
        P.emit()


def make_consts(T):
    NT = T // 128
    NCC = T // 16
    bf = ml_dtypes.bfloat16
    k = np.arange(T)
    kaug = np.stack([np.ones(T), k % 128, np.ones(T), k // 128]).astype(np.float32)
    pc = 16 * np.arange(NCC) + 31
    kaugc = np.stack([np.ones(NCC), pc % 128, np.ones(NCC), pc // 128]).astype(np.float32)
    qaug = np.zeros((2, 4, NT, 4, 128), np.float32)
    tl = np.arange(128)
    for g in range(2):
        for r in range(4):
            sl = SLOPES[g * 4 + r]
            for i in range(NT):
                qaug[g, 0, i, r] = -sl * tl
                qaug[g, 1, i, r] = sl
                qaug[g, 2, i, r] = -sl * 128.0 * i
                qaug[g, 3, i, r] = sl * 128.0
    qaug = qaug.reshape(2, 4, NT * 512)
    ns = T // 64
    t = np.arange(T)
    cur = t // 64
    blk = np.arange(64)
    ftab = np.zeros((T, 64), np.float32)
    valid = blk[None, :] <= cur[:, None]
    forced = (blk[None, :] == 0) | (blk[None, :] == cur[:, None]) | (blk[None, :] == cur[:, None] - 1)
    ftab[~valid] = -1e30
    ftab[forced] = 1e9
    ftab[:, ns:] = -1e30
    ftab = ftab.reshape(NT, 128, 64)
    c_start = np.arange(NCC) * 16
    s_start = np.arange(64) * 64
    ovl = np.clip(np.minimum(c_start[:, None] + 32, s_start[None, :] + 64)
                  - np.maximum(c_start[:, None], s_start[None, :]), 0, None) / 32.0
    ovl[NCC - 1, :] = 0.0
    return {"kaug": kaug.astype(bf), "kaugc": kaugc.astype(bf), "qaug": qaug.astype(bf), "ftab": ftab,
            "ovl": ovl.astype(np.float32).astype(bf)}


def build(T, debug=False, phases=("fa", "pp", "at", "fc")):
    NT = T // 128
    nc = bass.Bass("TRN2", target_bir_lowering=False)

    def din(name, shape, dt=F32):
        return nc.dram_tensor(name, list(shape), dt, kind="ExternalInput").ap()

    x = din("x", [T, D])
    ffn1_norm = din("ffn1_norm", [D])
    ffn1_w_in = din("ffn1_w_in", [D, 2 * FF])
    ffn1_w_out = din("ffn1_w_out", [FF, D])
    ffn2_norm = din("ffn2_norm", [D])
    ffn2_w_in = din("ffn2_w_in", [D, 2 * FF])
    ffn2_w_out = din("ffn2_w_out", [FF, D])
    final_norm = din("final_norm", [D])
    y = nc.dram_tensor("y", [T, D], F32, kind="ExternalOutput").ap()
    mix_norm = din("mix_norm", [D])
    w_mix_in = din("w_mix_in", [D, PROJ_W])
    w_mix_out = din("w_mix_out", [D, D])
    swa_sinks = din("swa_sinks", [8])
    Wd = {"w1": [din("cmp_k_w1", [2048, 256]), din("cmp_v_w1", [2048, 256])],
          "w2": [din("cmp_k_w2", [256, 64]), din("cmp_v_w2", [256, 64])],
          "pos": [din("cmp_k_pos", [32, 64]), din("cmp_v_pos", [32, 64])],
          "b1": [din("cmp_k_b1", [256]), din("cmp_v_b1", [256])],
          "wo": w_mix_out, "sinks": swa_sinks}
    NCC = T // 16
    Cd = {"kaug": din("c_kaug", [4, T], BF16), "kaugc": din("c_kaugc", [4, NCC], BF16),
          "qaug": din("c_qaug", [2, 4, NT * 512], BF16), "ftab": din("c_ftab", [NT, 128, 64]),
          "ovl": din("c_ovl", [NCC, 64], BF16)}
    kind = "ExternalOutput" if debug else "Internal"

    def scr(name, shape, dt):
        return nc.dram_tensor(name, list(shape), dt, kind=kind).ap()

    h1 = scr("h1s", [T, D], F32)
    h2 = scr("h2s", [T, D], F32)
    Sd = {"QA": scr("s_qa", [128, NT * 512], BF16), "QB": scr("s_qb", [128, NT * 512], BF16),
          "KT": scr("s_kt", [5, 128, T], BF16), "VT": scr("s_vt", [3, 128, NT, 2, 65], BF16),
          "GT": scr("s_gt", [T, 24], F32)}

    with ExitStack() as es:
        sems = [es.enter_context(nc.semaphore(f"s{i}")) for i in range(88)]
        P = Prog(nc, sems)
        if "fa" in phases:
            ffn_phase(nc, P, "fa", x, h1, ffn1_norm, ffn1_w_in, ffn1_w_out, T)
        if "pp" in phases:
            proj_phase(nc, P, h1, mix_norm, w_mix_in, Sd, T)
        if "at" in phases:
            attn_phase(nc, P, h1, h2, Sd, Cd, Wd, T)
        if "fc" in phases:
            ffn_phase(nc, P, "fc", h2, y, ffn2_norm, ffn2_w_in, ffn2_w_out, T, final_g=final_norm)
    return nc


_CACHE = {}


def kernel(**inputs):
    T = inputs["x"].shape[1]
    nb = inputs["x"].shape[0]
    if T not in _CACHE:
        _CACHE[T] = (build(T), make_consts(T))
    nc, consts = _CACHE[T]
    shared = {}
    for k, v in inputs.items():
        if k == "x":
            continue
        a = np.ascontiguousarray(np.asarray(v, dtype=np.float32))
        if k != "final_norm":
            a = a[0]
        shared[k] = np.ascontiguousarray(a)
    for k, v in consts.items():
        shared["c_" + k] = v
    xs = np.asarray(inputs["x"], dtype=np.float32)
    in_maps = []
    for b in range(nb):
        m = dict(shared)
        m["x"] = np.ascontiguousarray(xs[b])
        in_maps.append(m)
    res = run_bass_kernel_spmd(nc, in_maps, core_ids=list(range(nb)))
    return np.stack([np.asarray(r["y"], dtype=np.float32) for r in res.results], axis=0)
```

```python
import numpy as np
import ml_dtypes
import concourse.bass as bass
import concourse.mybir as mybir
from concourse.bass_utils import run_bass_kernel_spmd
from contextlib import ExitStack

F32 = mybir.dt.float32
BF16 = mybir.dt.bfloat16
AF = mybir.ActivationFunctionType
ALU = mybir.AluOpType

D = 1024
FF = 2816
NJ = FF // 128
PROJ_W = 2072
NEG = -30000.0
EPS = 1e-6
SLOPES = [2.0 ** (-(i + 1)) for i in range(8)]
SAME_ENGINE_SYNC = True
EPOCH = 100000


class Sem:
    def __init__(self, h):
        self.h = h
        self.count = 0


class Buf:
    __slots__ = ("name", "w", "r", "sem")

    def __init__(self, name):
        self.name = name
        self.w = []
        self.r = []
        self.sem = None


class Op:
    __slots__ = ("eng", "fn", "deps", "dma", "sem", "val", "signal")

    def __init__(self, eng, fn, dma):
        self.eng = eng
        self.fn = fn
        self.deps = []
        self.dma = dma
        self.sem = None
        self.val = 0
        self.signal = False


ENGS = ["tensor", "vector", "scalar", "gpsimd", "sync"]


class Prog:
    def __init__(self, nc, sems):
        self.nc = nc
        self.free_sems = [Sem(h) for h in sems]
        self.eng_sem = {}
        self.eng_cnt = {}
        for e in ENGS:
            self.eng_sem[e] = self.free_sems.pop()
            self.eng_cnt[e] = 0
        self.waited = {e: {} for e in ENGS}
        self.ops = {e: [] for e in ENGS}
        self.pending = {e: [] for e in ENGS}
        self.dma_bufs = []
        self.last = {e: None for e in ENGS}

    def buf(self, name):
        return Buf(name)

    def reg(self, eng, val):
        if val not in self.regcache:
            self.regcache[val] = eng.to_reg(val)
        return self.regcache[val]

    def op(self, eng, fn, reads=(), writes=(), dma_owner=None):
        o = Op(eng, fn, dma_owner is not None)
        deps = o.deps
        for b in reads:
            deps.extend(b.w)
        for b in writes:
            deps.extend(b.w)
            deps.extend(b.r)
        if self.pending[eng]:
            deps.extend(self.pending[eng])
            self.pending[eng] = []
        for b in reads:
            if o.dma:
                b.r.append(o)
            else:
                b.r = [x for x in b.r if x.dma or x.eng != eng] + [o]
        for b in writes:
            b.w = [o]
            b.r = []
        if dma_owner is not None:
            if dma_owner.sem is None:
                dma_owner.sem = self.free_sems.pop()
                self.dma_bufs.append(dma_owner)
            o.sem = dma_owner.sem
            o.sem.count += 16
            o.val = o.sem.count
        self.ops[eng].append(o)
        self.last[eng] = o
        return o

    def emit(self, final=False):
        nc = self.nc
        self.regcache = {}
        for e in ENGS:
            for o in self.ops[e]:
                for d in o.deps:
                    if not d.dma and not (d.eng == e and (e == "tensor" or not SAME_ENGINE_SYNC)):
                        d.signal = True
        for e in ENGS:
            if self.last[e] is not None and not self.last[e].dma:
                self.last[e].signal = True
        for e in ENGS:
            for o in self.ops[e]:
                if o.dma:
                    continue
                if o.signal:
                    if self.eng_cnt[e] >= EPOCH:
                        self.eng_sem[e] = self.free_sems.pop()
                        self.eng_cnt[e] = 0
                    self.eng_cnt[e] += 1
                    o.sem = self.eng_sem[e]
                    o.val = self.eng_cnt[e]
        end_tokens = []
        for e in ENGS:
            if self.last[e] is not None and not self.last[e].dma:
                end_tokens.append(self.last[e])
        dma_final = [(b.sem, b.sem.count) for b in self.dma_bufs]

        def run(e, eng):
            waited = self.waited[e]
            for o in self.ops[e]:
                need = {}
                for d in o.deps:
                    if d.sem is None:
                        continue
                    if (not d.dma) and d.eng == e:
                        if e == "tensor" or not SAME_ENGINE_SYNC:
                            continue
                    k = d.sem
                    if need.get(k, 0) < d.val:
                        need[k] = d.val
                for k, v in need.items():
                    if waited.get(k, 0) >= v:
                        continue
                    eng.wait_ge(k.h, v)
                    waited[k] = v
                ins = o.fn(eng)
                if o.dma:
                    ins.then_inc(o.sem.h, 16)
                elif o.signal:
                    ins.then_inc(o.sem.h, 1)
            if e == "sync":
                for s, v in dma_final:
                    if waited.get(s, 0) < v:
                        eng.wait_ge(s.h, v)
                        waited[s] = v
                for t in end_tokens:
                    if t.eng != e and waited.get(t.sem, 0) < t.val:
                        eng.wait_ge(t.sem.h, t.val)
                        waited[t.sem] = t.val

        with nc.Block() as block:
            @block.tensor
            def _(eng):
                run("tensor", eng)

            @block.vector
            def _(eng):
                run("vector", eng)

            @block.scalar
            def _(eng):
                run("scalar", eng)

            @block.gpsimd
            def _(eng):
                run("gpsimd", eng)

            @block.sync
            def _(eng):
                run("sync", eng)

        for b in self.dma_bufs:
            self.free_sems.insert(0, b.sem)
            b.sem = None
        self.dma_bufs = []
        self.ops = {e: [] for e in ENGS}
        self.last = {e: None for e in ENGS}


def apx(ap, extra):
    return bass.AP(ap.tensor, ap.offset, [list(x) for x in ap.ap] + [list(x) for x in extra])


def ffn_phase(nc, P, tag, src, dst, g_ap, w_in, w_out, T, final_g=None):
    NB = T // 256
    with ExitStack() as es:
        def sb(name, shape, dt):
            return es.enter_context(nc.sbuf_tensor(f"{tag}_{name}", shape, dt))

        def ps(name, shape, dt):
            return es.enter_context(nc.psum_tensor(f"{tag}_{name}", shape, dt))

        WIN = sb("win", [128, 8, 2 * FF], BF16)
        WOUT = sb("wout", [128, NJ, D], BF16)
        G = sb("g", [128, 8], F32)
        IDENT = sb("ident", [128, 128], BF16)
        NEGH = sb("negh", [128, 1], F32)
        XB = [sb(f"xb{i}", [128, 2, D], F32) for i in range(3)]
        XN = [sb(f"xn{i}", [128, D], BF16) for i in range(2)]
        XNT = [sb(f"xnt{i}", [128, 8, 256], BF16) for i in range(2)]
        SQ = sb("sq", [128, D], BF16)
        ST = [sb(f"st{i}", [128, 4], F32) for i in range(4)]
        SIL = [sb(f"sil{i}", [128, 256], F32) for i in range(2)]
        HT = [sb(f"ht{i}", [128, 256], BF16) for i in range(3)]
        if final_g is not None:
            FG = sb("fg", [128, D], F32)
            OUTT = [sb(f"outt{i}", [128, D], F32) for i in range(2)]
        TPS = ps("tps", [128, 4, 256], BF16)
        GU = [ps(f"gu{i}", [128, 512], F32) for i in range(3)]
        ACC = [ps(f"acc{i}", [128, 512], F32) for i in range(4)]

        bWIN = [[P.buf(f"win{k}_{h}") for h in range(4)] for k in range(8)]
        bWOUT = [P.buf(f"wout{i}") for i in range(NJ)]
        bG = P.buf("g")
        bC = P.buf("const")
        bXB = [P.buf(f"xb{i}") for i in range(3)]
        bXN = [P.buf(f"xn{i}") for i in range(2)]
        bXNT = [P.buf(f"xnt{i}") for i in range(2)]
        bSQ = P.buf("sq")
        bST = [P.buf(f"st{i}") for i in range(4)]
        bSIL = [P.buf(f"sil{i}") for i in range(2)]
        bHT = [P.buf(f"ht{i}") for i in range(3)]
        bTPS = P.buf("tps")
        bGU = [P.buf(f"gu{i}") for i in range(3)]
        bACC = [P.buf(f"acc{i}") for i in range(4)]
        bDST = P.buf("dst")
        if final_g is not None:
            bFG = P.buf("fg")
            bOUTT = [P.buf(f"outt{i}") for i in range(2)]

        P.op("gpsimd", lambda e: e.memset(IDENT[:], 1.0), writes=[bC])
        P.op("gpsimd", lambda e: e.affine_select(out=IDENT[:], in_=IDENT[:], pattern=[[-1, 128]],
                                                 compare_op=ALU.is_equal, fill=P.reg(e, 0.0), base=0, channel_multiplier=1),
             writes=[bC])
        P.op("gpsimd", lambda e: e.memset(NEGH[:], -0.5), writes=[bC])
        P.op("sync", lambda e: e.dma_start(out=G[:], in_=g_ap.rearrange("(c p) -> p c", p=128),
                                           allow_slow_non_contiguous=True), writes=[bG], dma_owner=bG)
        if final_g is not None:
            P.op("sync", lambda e: e.dma_start(out=FG[:], in_=final_g.partition_broadcast(128)),
                 writes=[bFG], dma_owner=bFG)

        def load(b):
            s = b % 3
            for t in range(2):
                r0 = (2 * b + t) * 128
                P.op("sync", lambda e, s=s, t=t, r0=r0: e.dma_start(out=XB[s][:, t, :], in_=src[r0:r0 + 128, :]),
                     writes=[bXB[s]], dma_owner=bXB[s])

        load(0)
        for hh in (0, 2, 1, 3):
            for k in range(8):
                c0 = hh * 1408
                P.op("gpsimd", lambda e, k=k, c0=c0: e.dma_start(out=WIN[:, k, c0:c0 + 1408],
                                                                in_=w_in[k * 128:(k + 1) * 128, c0:c0 + 1408]),
                     writes=[bWIN[k][hh]], dma_owner=bWIN[k][hh])
        if NB > 1:
            load(1)
        for j in range(NJ):
            P.op("gpsimd", lambda e, j=j: e.dma_start(out=WOUT[:, j, :], in_=w_out[j * 128:(j + 1) * 128, :]),
                 writes=[bWOUT[j]], dma_owner=bWOUT[j])

        def norm_tile(x_ap, bx, si, out_ap, bout, extra_reads=()):
            st, bst = ST[si], bST[si]
            P.op("scalar", lambda e: e.activation(out=SQ[:], in_=x_ap, func=AF.Square, accum_out=st[:, 0:1]),
                 reads=[bx], writes=[bSQ, bst])
            P.op("vector", lambda e: e.tensor_scalar(out=st[:, 1:2], in0=st[:, 0:1], scalar1=1.0 / D, scalar2=EPS,
                                                     op0=ALU.mult, op1=ALU.add), reads=[bst], writes=[bst])
            P.op("gpsimd", lambda e: e.tensor_tensor(out=st[:, 2:3], in0=st[:, 1:2], in1=NEGH[:], op=ALU.pow),
                 reads=[bst, bC], writes=[bst])
            P.op("scalar", lambda e: e.activation(out=out_ap, in_=x_ap, func=AF.Copy, scale=st[:, 2:3]),
                 reads=[bx, bst] + list(extra_reads), writes=[bout])

        def prologue(b):
            s = b % 3
            x2 = b % 2
            for t in range(2):
                norm_tile(XB[s][:, t, :], bXB[s], (2 * b + t) % 4, XN[t][:], bXN[t])
            for h in range(2):
                for t in range(2):
                    for kk in range(4):
                        k = 4 * h + kk
                        P.op("tensor", lambda e, t=t, k=k, kk=kk: e.transpose(
                            out=TPS[:, kk, t * 128:(t + 1) * 128], in_=XN[t][:, k * 128:(k + 1) * 128],
                            identity=IDENT[:]), reads=[bXN[t], bC], writes=[bTPS])
                for kk in range(4):
                    k = 4 * h + kk
                    P.op("vector", lambda e, k=k, kk=kk, x2=x2: e.tensor_scalar(
                        out=XNT[x2][:, k, :], in0=TPS[:, kk, :], scalar1=G[:, k:k + 1], scalar2=None, op0=ALU.mult),
                         reads=[bTPS, bG], writes=[bXNT[x2]])

        def gu(b, j):
            x2 = b % 2
            gb = (b * NJ + j) % 3
            for half in range(2):
                c0 = half * FF + j * 128
                for k in range(8):
                    P.op("tensor", lambda e, k=k, c0=c0, half=half, gb=gb, x2=x2: e.matmul(
                        GU[gb][:, half * 256:(half + 1) * 256], lhsT=WIN[:, k, c0:c0 + 128], rhs=XNT[x2][:, k, :],
                        start=(k == 0), stop=(k == 7)), reads=[bWIN[k][c0 // 1408], bXNT[x2]], writes=[bGU[gb]])

        def second(b, j):
            gb = (b * NJ + j) % 3
            hb = (b * NJ + j) % 3
            sl = (b * NJ + j) % 2
            P.op("scalar", lambda e: e.activation(out=SIL[sl][:], in_=GU[gb][:, 0:256], func=AF.Silu),
                 reads=[bGU[gb]], writes=[bSIL[sl]])
            P.op("vector", lambda e: e.tensor_tensor(out=HT[hb][:], in0=SIL[sl][:], in1=GU[gb][:, 256:512],
                                                     op=ALU.mult), reads=[bSIL[sl], bGU[gb]], writes=[bHT[hb]])
            for t in range(2):
                for n in range(2):
                    P.op("tensor", lambda e, t=t, n=n: e.matmul(
                        ACC[t * 2 + n][:], lhsT=HT[hb][:, t * 128:(t + 1) * 128], rhs=WOUT[:, j, n * 512:(n + 1) * 512],
                        start=(j == 0), stop=(j == NJ - 1)), reads=[bHT[hb], bWOUT[j]], writes=[bACC[t * 2 + n]])

        def epilogue(b):
            s = b % 3
            for t in range(2):
                for n in range(2):
                    P.op("vector", lambda e, t=t, n=n: e.scalar_tensor_tensor(
                        out=XB[s][:, t, n * 512:(n + 1) * 512], in0=ACC[t * 2 + n][:], scalar=0.5,
                        in1=XB[s][:, t, n * 512:(n + 1) * 512], op0=ALU.mult, op1=ALU.add),
                         reads=[bACC[t * 2 + n], bXB[s]], writes=[bXB[s]])
                r0 = (2 * b + t) * 128
                if final_g is None:
                    P.op("sync", lambda e, t=t, r0=r0: e.dma_start(out=dst[r0:r0 + 128, :], in_=XB[s][:, t, :]),
                         reads=[bXB[s]], writes=[bDST], dma_owner=bXB[s])
                else:
                    o2 = (2 * b + t) % 2
                    norm_tile(XB[s][:, t, :], bXB[s], (2 * b + t) % 4, OUTT[o2][:], bOUTT[o2])
                    P.op("vector", lambda e, o2=o2: e.tensor_tensor(out=OUTT[o2][:], in0=OUTT[o2][:], in1=FG[:],
                                                                    op=ALU.mult),
                         reads=[bOUTT[o2], bFG], writes=[bOUTT[o2]])
                    P.op("sync", lambda e, o2=o2, r0=r0: e.dma_start(out=dst[r0:r0 + 128, :], in_=OUTT[o2][:]),
                         reads=[bOUTT[o2]], writes=[bDST], dma_owner=bOUTT[o2])

        prologue(0)
        gu(0, 0)
        for b in range(NB):
            if b + 2 < NB:
                load(b + 2)
            for j in range(NJ):
                if j + 1 < NJ:
                    gu(b, j + 1)
                elif b + 1 < NB:
                    gu(b + 1, 0)
                if j == 6 and b + 1 < NB:
                    prologue(b + 1)
                second(b, j)
            epilogue(b)
        P.emit()


def norm_ops(P, x_ap, bx, SQ, bSQ, st, bst, NEGH, bC, out_ap, bout):
    P.op("scalar", lambda e: e.activation(out=SQ[:], in_=x_ap, func=AF.Square, accum_out=st[:, 0:1]),
         reads=[bx], writes=[bSQ, bst])
    P.op("vector", lambda e: e.tensor_scalar(out=st[:, 1:2], in0=st[:, 0:1], scalar1=1.0 / D, scalar2=EPS,
                                             op0=ALU.mult, op1=ALU.add), reads=[bst], writes=[bst])
    P.op("gpsimd", lambda e: e.tensor_tensor(out=st[:, 2:3], in0=st[:, 1:2], in1=NEGH[:], op=ALU.pow),
         reads=[bst, bC], writes=[bst])
    P.op("scalar", lambda e: e.activation(out=out_ap, in_=x_ap, func=AF.Copy, scale=st[:, 2:3]),
         reads=[bx, bst], writes=[bout])


KOFF = [512, 640, 768, 1024, 1816]
VOFF = [896, 1152, 1944]


def proj_phase(nc, P, h1, g_ap, w_mix, S, T):
    NB = T // 512
    with ExitStack() as es:
        def sb(name, shape, dt):
            return es.enter_context(nc.sbuf_tensor(f"pp_{name}", shape, dt))

        def ps(name, shape, dt):
            return es.enter_context(nc.psum_tensor(f"pp_{name}", shape, dt))

        WM = sb("wm", [128, 8, PROJ_W], BF16)
        WMQ = sb("wmq", [128, 8, 1024], BF16)
        G = sb("g", [128, 8], F32)
        IDENT = sb("ident", [128, 128], BF16)
        NEGH = sb("negh", [128, 1], F32)
        HB = [sb(f"hb{i}", [128, 4, D], F32) for i in range(2)]
        XN = [sb(f"xn{i}", [128, D], BF16) for i in range(4)]
        XNT = [sb(f"xnt{i}", [128, 8, 512], BF16) for i in range(2)]
        SQ = sb("sq", [128, D], BF16)
        ST = [sb(f"st{i}", [128, 4], F32) for i in range(4)]
        QST = [sb(f"qst{i}", [128, 4, 4, 128], BF16) for i in range(2)]
        QBST = [sb(f"qbst{i}", [128, 4, 4, 128], BF16) for i in range(2)]
        KST = [sb(f"kst{i}", [128, 5, 512], BF16) for i in range(2)]
        VST = [sb(f"vst{i}", [128, 4, 3, 2, 65], BF16) for i in range(2)]
        GST = [sb(f"gst{i}", [128, 4, 24], F32) for i in range(2)]
        TPS = [ps(f"tps{i}", [128, 2, 512], BF16) for i in range(2)]
        PJ = [ps(f"pj{i}", [128, 512], F32) for i in range(3)]
        PT = [ps(f"pt{i}", [128, 512], F32) for i in range(2)]

        bWM = [P.buf(f"wm{k}") for k in range(8)]
        bWMQ = [P.buf(f"wmq{k}") for k in range(8)]
        bG = P.buf("g"); bC = P.buf("c")
        bHB = [P.buf("hb") for i in range(2)]
        bXN = [P.buf("xn") for i in range(4)]
        bXNT = [P.buf("xnt") for i in range(2)]
        bSQ = P.buf("sq")
        bST = [P.buf("st") for i in range(4)]
        bQST = [P.buf("qst") for i in range(2)]
        bQBST = [P.buf("qbst") for i in range(2)]
        bKST = [P.buf("kst") for i in range(2)]
        bVST = [P.buf("vst") for i in range(2)]
        bGST = [P.buf("gst") for i in range(2)]
        bTPS = [P.buf("tps") for i in range(2)]
        bPJ = [P.buf("pj") for i in range(3)]
        bPT = [P.buf("pt") for i in range(2)]
        bOUT = P.buf("out")

        P.op("gpsimd", lambda e: e.memset(IDENT[:], 1.0), writes=[bC])
        P.op("gpsimd", lambda e: e.affine_select(out=IDENT[:], in_=IDENT[:], pattern=[[-1, 128]],
                                                 compare_op=ALU.is_equal, fill=P.reg(e, 0.0), base=0, channel_multiplier=1),
             writes=[bC])
        P.op("gpsimd", lambda e: e.memset(NEGH[:], -0.5), writes=[bC])
        for i in range(2):
            P.op("gpsimd", lambda e, i=i: e.memset(VST[i][:], 1.0), writes=[bVST[i]])
        P.op("sync", lambda e: e.dma_start(out=G[:], in_=g_ap.rearrange("(c p) -> p c", p=128),
                                           allow_slow_non_contiguous=True), writes=[bG], dma_owner=bG)

        def load(b):
            s = b % 2
            for t in range(4):
                r0 = (4 * b + t) * 128
                P.op("sync", lambda e, t=t, r0=r0: e.dma_start(out=HB[s][:, t, :], in_=h1[r0:r0 + 128, :]),
                     writes=[bHB[s]], dma_owner=bHB[s])

        load(0)
        for k in range(8):
            rows = slice(k * 128, (k + 1) * 128)
            for hh in range(2):
                c0 = hh * 1036
                P.op("gpsimd", lambda e, k=k, rows=rows, c0=c0: e.dma_start(
                    out=WM[:, k, c0:c0 + 1036], in_=w_mix[rows, c0:c0 + 1036]), writes=[bWM[k]], dma_owner=bWM[k])
        for k in range(8):
            for h, base in enumerate((0, 1304)):
                eng = "vector" if h == 0 else "gpsimd"
                P.op(eng, lambda e, k=k, h=h, base=base: e.tensor_copy(
                    out=WMQ[:, k, h * 512:(h + 1) * 512].rearrange("p (r g d) -> p r g d", r=4, g=2),
                    in_=WM[:, k, base:base + 512].rearrange("p (g r d) -> p r g d", g=2, r=4)),
                     reads=[bWM[k]], writes=[bWMQ[k]])
        cnt = {"pj": 0, "pt": 0, "tp": 0}
        def pro(b):
            s = b % 2
            if b + 1 < NB:
                load(b + 1)
            for t in range(4):
                norm_ops(P, HB[s][:, t, :], bHB[s], SQ, bSQ, ST[t], bST[t], NEGH, bC, XN[t][:], bXN[t])
            for kp in range(4):
                tp = cnt["tp"] % 2
                cnt["tp"] += 1
                for t in range(4):
                    for kk in range(2):
                        k = 2 * kp + kk
                        P.op("tensor", lambda e, t=t, k=k, kk=kk, tp=tp: e.transpose(
                            out=TPS[tp][:, kk, t * 128:(t + 1) * 128], in_=XN[t][:, k * 128:(k + 1) * 128],
                            identity=IDENT[:]), reads=[bXN[t], bC], writes=[bTPS[tp]])
                for kk in range(2):
                    k = 2 * kp + kk
                    P.op("vector", lambda e, k=k, kk=kk, tp=tp: e.tensor_scalar(
                        out=XNT[s][:, k, :], in0=TPS[tp][:, kk, :], scalar1=G[:, k:k + 1], scalar2=None,
                        op0=ALU.mult), reads=[bTPS[tp], bG], writes=[bXNT[s]])

        def block(b):
            s = b % 2

            def fm_group(colfn, evac, ei, bsrc=bWM):
                pj = cnt["pj"] % 3
                cnt["pj"] += 1
                for k in range(8):
                    P.op("tensor", lambda e, k=k, pj=pj: e.matmul(PJ[pj][:], lhsT=colfn(k), rhs=XNT[s][:, k, :],
                                                                 start=(k == 0), stop=(k == 7)),
                         reads=[bsrc[k], bXNT[s]], writes=[bPJ[pj]])
                evac(pj, ei)

            def qcols(base):
                def f(k):
                    return WMQ[:, k, base:base + 128]
                return f

            ei = 0
            for r in range(4):
                for (base, QS, bQS) in ((0, QST, bQST), (512, QBST, bQBST)):
                    def evac(pj, ei, r=r, QS=QS, bQS=bQS):
                        src = PJ[pj][:].rearrange("p (n t) -> p n t", t=128)
                        if ei % 2 == 0:
                            P.op("scalar", lambda e: e.activation(out=QS[s][:, :, r, :], in_=src, func=AF.Copy,
                                                                  scale=0.125), reads=[bPJ[pj]], writes=[bQS[s]])
                        else:
                            P.op("vector", lambda e: e.tensor_scalar(out=QS[s][:, :, r, :], in0=src, scalar1=0.125,
                                                                     scalar2=None, op0=ALU.mult),
                                 reads=[bPJ[pj]], writes=[bQS[s]])
                    fm_group(qcols(base + r * 128), evac, ei, bsrc=bWMQ)
                    ei += 1
            if b + 1 < NB:
                pro(b + 1)
            for w in range(5):
                def evac(pj, ei, w=w):
                    if ei % 2 == 0:
                        P.op("scalar", lambda e: e.copy(out=KST[s][:, w, :], in_=PJ[pj][:]),
                             reads=[bPJ[pj]], writes=[bKST[s]])
                    else:
                        P.op("vector", lambda e: e.tensor_copy(out=KST[s][:, w, :], in_=PJ[pj][:]),
                             reads=[bPJ[pj]], writes=[bKST[s]])
                fm_group(lambda k, w=w: WM[:, k, KOFF[w]:KOFF[w] + 128], evac, ei)
                ei += 1
            for t in range(4):
                pt = cnt["pt"] % 2
                cnt["pt"] += 1
                for (c0, n, o0) in ((896, 128, 0), (1152, 152, 128), (1944, 128, 280)):
                    for k in range(8):
                        P.op("tensor", lambda e, k=k, c0=c0, n=n, o0=o0, pt=pt, t=t: e.matmul(
                            PT[pt][:, o0:o0 + n], lhsT=XNT[s][:, k, t * 128:(t + 1) * 128], rhs=WM[:, k, c0:c0 + n],
                            start=(k == 0), stop=(k == 7)), reads=[bWM[k], bXNT[s]], writes=[bPT[pt]])
                for wi, o0 in enumerate((0, 128, 280)):
                    src = PT[pt][:, o0:o0 + 128].rearrange("p (g d) -> p g d", d=64)
                    if wi == 1:
                        P.op("scalar", lambda e, src=src, wi=wi, t=t: e.copy(out=VST[s][:, t, wi, :, 0:64], in_=src),
                             reads=[bPT[pt]], writes=[bVST[s]])
                    else:
                        P.op("vector", lambda e, src=src, wi=wi, t=t: e.tensor_copy(out=VST[s][:, t, wi, :, 0:64],
                                                                                   in_=src),
                             reads=[bPT[pt]], writes=[bVST[s]])
                P.op("vector", lambda e, t=t, pt=pt: e.tensor_copy(out=GST[s][:, t, :], in_=PT[pt][:, 256:280]),
                     reads=[bPT[pt]], writes=[bGST[s]])
            c0 = b * 2048
            P.op("sync", lambda e, c0=c0: e.dma_start(out=S["QA"][:, c0:c0 + 2048],
                                                      in_=QST[s][:].rearrange("p n r t -> p (n r t)")),
                 reads=[bQST[s]], writes=[bOUT], dma_owner=bQST[s])
            P.op("sync", lambda e, c0=c0: e.dma_start(out=S["QB"][:, c0:c0 + 2048],
                                                      in_=QBST[s][:].rearrange("p n r t -> p (n r t)")),
                 reads=[bQBST[s]], writes=[bOUT], dma_owner=bQBST[s])
            for w in range(5):
                P.op("sync", lambda e, w=w: e.dma_start(out=S["KT"][w, :, b * 512:(b + 1) * 512], in_=KST[s][:, w, :]),
                     reads=[bKST[s]], writes=[bOUT], dma_owner=bKST[s])
            for w in range(3):
                P.op("sync", lambda e, w=w: e.dma_start(out=S["VT"][w, :, 4 * b:4 * b + 4, :, :],
                                                        in_=VST[s][:, :, w, :, :]),
                     reads=[bVST[s]], writes=[bOUT], dma_owner=bVST[s])
            P.op("sync", lambda e: e.dma_start(
                out=S["GT"].rearrange("(n p) c -> p n c", p=128)[:, 4 * b:4 * b + 4, :], in_=GST[s][:]),
                 reads=[bGST[s]], writes=[bOUT], dma_owner=bGST[s])

        pro(0)
        for b in range(NB):
            block(b)
        P.emit()


def attn_phase(nc, P, h1, h2, S, C, W, T):
    NT = T // 128
    NCC = T // 16
    NCH = NCC // 128
    with ExitStack() as es:
        def sb(name, shape, dt):
            return es.enter_context(nc.sbuf_tensor(f"at_{name}", shape, dt))

        def ps(name, shape, dt):
            return es.enter_context(nc.psum_tensor(f"at_{name}", shape, dt))

        KG = [[sb(f"kg{br}{g}", [68, T], BF16) for g in range(2)] for br in range(3)]
        VR = [sb(f"vr{br}", [128, NT, 2, 65], BF16) for br in range(3)]
        KCG = [sb(f"kcg{g}", [68, NCC], BF16) for g in range(2)]
        VCX = [sb(f"vcx{g}", [128, NCH, 129], BF16) for g in range(2)]
        RAWT = sb("rawt", [128, 2, T], BF16)
        W1 = [sb(f"w1{i}", [128, 32, 256], BF16) for i in range(2)]
        W2 = [sb(f"w2{i}", [128, 2, 64], BF16) for i in range(2)]
        POSF = [sb(f"posf{i}", [64, 32], F32) for i in range(2)]
        POST = [sb(f"post{i}", [64, 32], BF16) for i in range(2)]
        B1 = [sb(f"b1{i}", [128, 2], F32) for i in range(2)]
        BIAS = [sb(f"bias{i}", [128, 2], F32) for i in range(2)]
        GX = sb("gx", [128, NCC], F32)
        T1 = sb("t1", [128, NCC], F32)
        HTC = [sb(f"htc{i}", [128, NCC], BF16) for i in range(2)]
        WO = sb("wo", [128, 8, D], BF16)
        EBIG = sb("ebig", [128, NT * 128], BF16)
        IDENT = sb("ident", [128, 128], BF16)
        CAUS = sb("caus", [128, 4, 128], BF16)
        CAUSC = sb("causc", [128, 4, 128], BF16)
        ZER = sb("zer", [128, 4, 128], BF16)
        SINKE = sb("sinke", [128, 8, 1], F32)
        QT = [[sb(f"qt{q}{i}", [68, 512], BF16) for i in range(2)] for q in range(4)]
        GATE = [sb(f"gate{i}", [128, 24], F32) for i in range(2)]
        FT = [sb(f"ft{i}", [128, 2, 64], F32) for i in range(2)]
        H1T = [sb(f"h1t{i}", [128, D], F32) for i in range(2)]
        PTT = [sb(f"ptt{i}", [128, 512], BF16) for i in range(4)]
        MASKC = [sb(f"maskc{i}", [128, 4, 128], BF16) for i in range(2)]
        NEGMT = [sb(f"negmt{g}", [128, 4, 128], BF16) for g in range(2)]
        NEGMS = [sb(f"negm{g}", [128, 64], BF16) for g in range(2)]
        OMIX = sb("omix", [128, 16, 64], F32)
        OMIXB = sb("omixb", [128, D], BF16)
        OT = sb("ot", [128, 8, 128], BF16)
        SG = sb("sg", [128, 24], F32)
        OAS = sb("oas", [128, 4, 129], F32)
        L4 = sb("l4", [128, 4, 1], F32)
        RL4 = sb("rl4", [128, 4, 1], F32)
        W4 = sb("w4", [128, 4, 1], F32)
        TMP = sb("tmp", [128, 4, 64], F32)
        PSLC = sb("pslc", [128, 64], F32)
        SCORE = sb("score", [128, 64], F32)
        WORK = sb("work", [128, 64], F32)
        M1 = sb("m1", [128, 8], F32)
        M2 = sb("m2", [128, 8], F32)

        NSTP = 3
        DEPTH = 4
        STP = [ps(f"stp{i}", [128, 512], F32) for i in range(NSTP)]
        OA = ps("oa", [128, 4, 512], F32)
        MF = ps("mf", [128, 512], F32)
        MBV = MF.bitcast(BF16)

        def MBc(c):
            return MBV[:, c * 128:(c + 1) * 128]

        B = P.buf
        bKG = [[B("kg") for g in range(2)] for br in range(3)]
        bVR = [B("vr") for br in range(3)]
        bKCG = [B("kcg") for g in range(2)]
        bVCX = [B("vcx") for g in range(2)]
        bRAWT = B("rawt"); bW1 = [B("w1"), B("w1")]; bW2 = [B("w2"), B("w2")]
        bPOSF = [B("pf"), B("pf")]; bPOST = [B("pt"), B("pt")]; bB1 = [B("b1"), B("b1")]; bBIAS = [B("bi"), B("bi")]
        bGX = B("gx"); bT1 = B("t1"); bHTC = [B("htc"), B("htc")]
        bWO = B("wo"); bC = B("c"); bSINK = B("sink")
        bQT = [[B("qt") for i in range(2)] for q in range(4)]
        bGATE = [B("gate"), B("gate")]; bFT = [B("ft"), B("ft")]; bH1T = [B("h1t"), B("h1t")]
        bPTT = [B("ptt") for i in range(4)]
        bMASKC = [B("mc"), B("mc")]
        bNEGMT = [B("nm"), B("nm")]; bNEGMS = [B("negm"), B("negm")]
        bOMIX = B("omix"); bOMIXB = B("omixb"); bOT = B("ot"); bSG = B("sg")
        bOAS = B("oas"); bL4 = B("l4"); bTMP = B("tmp"); bPSLC = B("pslc"); bSCORE = B("score"); bWORK = B("work")
        bM = B("m")
        bSTP = [B("stp") for _ in range(3)]; bOA = B("oa"); bMF = B("mf"); bMB = bMF
        bH2 = B("h2")

        P.op("gpsimd", lambda e: e.memset(IDENT[:], 1.0), writes=[bC])
        P.op("gpsimd", lambda e: e.affine_select(out=IDENT[:], in_=IDENT[:], pattern=[[-1, 128]],
                                                 compare_op=ALU.is_equal, fill=P.reg(e, 0.0), base=0, channel_multiplier=1),
             writes=[bC])
        P.op("gpsimd", lambda e: e.memset(EBIG[:], 1.0), writes=[bC])
        P.op("gpsimd", lambda e: e.affine_select(out=EBIG[:], in_=EBIG[:], pattern=[[-1, NT * 2], [0, 64]],
                                                 compare_op=ALU.is_equal, fill=P.reg(e, 0.0), base=0, channel_multiplier=1),
             writes=[bC])
        P.op("gpsimd", lambda e: e.memset(ZER[:], 0.0), writes=[bC])
        P.op("gpsimd", lambda e: e.memset(CAUS[:], NEG), writes=[bC])
        P.op("gpsimd", lambda e: e.affine_select(out=CAUS[:], in_=CAUS[:], pattern=[[0, 4], [-1, 128]],
                                                 compare_op=ALU.is_gt, fill=P.reg(e, 0.0), base=0, channel_multiplier=1),
             writes=[bC])
        P.op("gpsimd", lambda e: e.memset(CAUSC[:], NEG), writes=[bC])
        P.op("gpsimd", lambda e: e.affine_select(out=CAUSC[:], in_=CAUSC[:], pattern=[[0, 4], [1, 128]],
                                                 compare_op=ALU.is_ge, fill=P.reg(e, 0.0), base=0, channel_multiplier=-1),
             writes=[bC])
        for g in range(2):
            P.op("gpsimd", lambda e, g=g: e.memset(NEGMT[g][:], 0.0), writes=[bNEGMT[g]])
            P.op("gpsimd", lambda e, g=g: e.memset(KCG[g][:], 0.0), writes=[bKCG[g]])
            P.op("gpsimd", lambda e, g=g: e.memset(VCX[g][:], 0.0), writes=[bVCX[g]])
            P.op("gpsimd", lambda e, g=g: e.memset(VCX[g][:, :, 64:65], 1.0), writes=[bVCX[g]])

        P.op("sync", lambda e: e.dma_start(out=RAWT[:, 0, :], in_=S["KT"][0, :, :]), writes=[bRAWT], dma_owner=bRAWT)
        P.op("sync", lambda e: e.dma_start(out=RAWT[:, 1, :], in_=S["KT"][1, :, :]), writes=[bRAWT], dma_owner=bRAWT)
        for i in range(2):
            w1 = W["w1"][i].rearrange("(j d) h -> d j h", d=64)
            for half in range(2):
                for jj in range(4):
                    P.op("gpsimd", lambda e, i=i, half=half, jj=jj, w1=w1: e.dma_start(
                        out=W1[i][half * 64:(half + 1) * 64, jj * 8:(jj + 1) * 8, :], in_=w1[:, jj * 8:(jj + 1) * 8, :]),
                         writes=[bW1[i]], dma_owner=bW1[i])
            P.op("gpsimd", lambda e, i=i: e.dma_start(out=W2[i][:], in_=W["w2"][i].rearrange("(c p) d -> p c d", p=128)),
                 writes=[bW2[i]], dma_owner=bW2[i])
            P.op("sync", lambda e, i=i: e.dma_start(out=POSF[i][:], in_=W["pos"][i].rearrange("j d -> d j"),
                                                    allow_slow_non_contiguous=True), writes=[bPOSF[i]], dma_owner=bPOSF[i])
            P.op("sync", lambda e, i=i: e.dma_start(out=B1[i][:], in_=W["b1"][i].rearrange("(c p) -> p c", p=128),
                                                    allow_slow_non_contiguous=True), writes=[bB1[i]], dma_owner=bB1[i])
            P.op("vector", lambda e, i=i: e.tensor_copy(out=POST[i][:], in_=POSF[i][:]), reads=[bPOSF[i]],
                 writes=[bPOST[i]])
        for g in range(2):
            P.op("sync", lambda e, g=g: e.dma_start(out=KCG[g][64:68, :], in_=C["kaugc"][:, :]),
                 writes=[bKCG[g]], dma_owner=bKCG[g])
            P.op("sync", lambda e, g=g: e.dma_start(out=VCX[g][:, :, 65:129],
                                                    in_=C["ovl"].rearrange("(n p) s -> p n s", p=128)),
                 writes=[bVCX[g]], dma_owner=bVCX[g])
        for br, w in enumerate((2, 3, 4)):
            for g in range(2):
                P.op("sync", lambda e, br=br, w=w, g=g: e.dma_start(out=KG[br][g][0:64, :],
                                                                   in_=S["KT"][w, g * 64:(g + 1) * 64, :]),
                     writes=[bKG[br][g]], dma_owner=bKG[br][g])
                P.op("sync", lambda e, br=br, g=g: e.dma_start(out=KG[br][g][64:68, :], in_=C["kaug"][:, :]),
                     writes=[bKG[br][g]], dma_owner=bKG[br][g])
            P.op("sync", lambda e, br=br: e.dma_start(out=VR[br][:].rearrange("p n g d -> p (n g d)"),
                                                      in_=S["VT"][br].rearrange("p n g d -> p (n g d)")),
                 writes=[bVR[br]], dma_owner=bVR[br])
        for k in range(8):
            P.op("gpsimd", lambda e, k=k: e.dma_start(out=WO[:, k, :], in_=W["wo"][k * 128:(k + 1) * 128, :]),
                 writes=[bWO], dma_owner=bWO)
        P.op("sync", lambda e: e.dma_start(out=SINKE[:, :, 0], in_=W["sinks"].partition_broadcast(128)),
             writes=[bSINK], dma_owner=bSINK)
        P.op("scalar", lambda e: e.activation(out=SINKE[:], in_=SINKE[:], func=AF.Exp), reads=[bSINK], writes=[bSINK])

        def loads(i):
            s = i % 2
            for q, (src, g) in enumerate((("QA", 0), ("QA", 1), ("QB", 0), ("QB", 1))):
                P.op("sync", lambda e, q=q, src=src, g=g: e.dma_start(
                    out=QT[q][s][0:64, :], in_=S[src][g * 64:(g + 1) * 64, i * 512:(i + 1) * 512]),
                     writes=[bQT[q][s]], dma_owner=bQT[q][s])
                P.op("sync", lambda e, q=q, g=g: e.dma_start(out=QT[q][s][64:68, :],
                                                             in_=C["qaug"][g, :, i * 512:(i + 1) * 512]),
                     writes=[bQT[q][s]], dma_owner=bQT[q][s])
            P.op("sync", lambda e: e.dma_start(out=GATE[s][:], in_=S["GT"][i * 128:(i + 1) * 128, :]),
                 writes=[bGATE[s]], dma_owner=bGATE[s])
            P.op("sync", lambda e: e.dma_start(out=FT[s][:, 0, :], in_=C["ftab"][i, :, :]),
                 writes=[bFT[s]], dma_owner=bFT[s])
            P.op("sync", lambda e: e.dma_start(out=H1T[s][:], in_=h1[i * 128:(i + 1) * 128, :]),
                 writes=[bH1T[s]], dma_owner=bH1T[s])

        loads(0)

        cnt = {"st": 0, "pt": 0, "mc": 0}

        def st_next():
            v = cnt["st"] % 3
            cnt["st"] += 1
            return v

        for i in range(2):
            for hc in range(2):
                for j in range(32):
                    P.op("tensor", lambda e, i=i, hc=hc, j=j: e.matmul(
                        MF[:, hc:hc + 1], lhsT=W1[i][0:64, j, hc * 128:(hc + 1) * 128], rhs=POST[i][:, j:j + 1],
                        start=(j == 0), stop=(j == 31)), reads=[bW1[i], bPOST[i]], writes=[bMF])
            P.op("vector", lambda e, i=i: e.tensor_tensor(out=BIAS[i][:], in0=MF[:, 0:2], in1=B1[i][:], op=ALU.add),
                 reads=[bMF, bB1[i]], writes=[bBIAS[i]])
            for g in range(2):
                for hc in range(2):
                    sp = st_next()
                    for j in range(32):
                        P.op("tensor", lambda e, i=i, g=g, hc=hc, j=j, sp=sp: e.matmul(
                            STP[sp][:, 0:NCC - 1], lhsT=W1[i][g * 64:(g + 1) * 64, j, hc * 128:(hc + 1) * 128],
                            rhs=RAWT[g * 64:(g + 1) * 64, i, j:j + 16 * (NCC - 2) + 1:16],
                            start=(j == 0), stop=(j == 31)), reads=[bW1[i], bRAWT], writes=[bSTP[sp]])
                    n = NCC - 1
                    P.op("scalar", lambda e, i=i, hc=hc, sp=sp: e.activation(
                        out=GX[:, 0:n], in_=STP[sp][:, 0:n], func=AF.Identity, bias=BIAS[i][:, hc:hc + 1]),
                         reads=[bSTP[sp], bBIAS[i]], writes=[bGX])
                    P.op("vector", lambda e: e.tensor_tensor(out=T1[:, 0:n], in0=GX[:, 0:n], in1=GX[:, 0:n], op=ALU.mult),
                         reads=[bGX], writes=[bT1])
                    P.op("vector", lambda e: e.tensor_scalar(out=T1[:, 0:n], in0=T1[:, 0:n], scalar1=0.044715,
                                                             scalar2=1.0, op0=ALU.mult, op1=ALU.add),
                         reads=[bT1], writes=[bT1])
                    P.op("vector", lambda e: e.tensor_tensor(out=T1[:, 0:n], in0=T1[:, 0:n], in1=GX[:, 0:n], op=ALU.mult),
                         reads=[bT1, bGX], writes=[bT1])
                    P.op("scalar", lambda e: e.activation(out=T1[:, 0:n], in_=T1[:, 0:n], func=AF.Exp,
                                                          scale=-1.5957691216057308), reads=[bT1], writes=[bT1])
                    P.op("vector", lambda e: e.tensor_scalar(out=T1[:, 0:n], in0=T1[:, 0:n], scalar1=1.0, scalar2=None,
                                                             op0=ALU.add), reads=[bT1], writes=[bT1])
                    P.op("vector", lambda e: e.reciprocal(out=T1[:, 0:n], in_=T1[:, 0:n]), reads=[bT1], writes=[bT1])
                    P.op("gpsimd", lambda e, hc=hc: e.memset(HTC[hc][:, n:NCC], 0.0), writes=[bHTC[hc]])
                    P.op("vector", lambda e, hc=hc: e.tensor_tensor(out=HTC[hc][:, 0:n], in0=T1[:, 0:n], in1=GX[:, 0:n],
                                                                    op=ALU.mult), reads=[bT1, bGX], writes=[bHTC[hc]])
                if i == 0:
                    for hc in range(2):
                        P.op("tensor", lambda e, hc=hc: e.matmul(MF[0:64, 0:NCC], lhsT=W2[0][:, hc, :], rhs=HTC[hc][:],
                                                                 start=(hc == 0), stop=(hc == 1)),
                             reads=[bW2[0], bHTC[hc]], writes=[bMF])
                    P.op("vector", lambda e, g=g: e.tensor_copy(out=KCG[g][0:64, :], in_=MF[0:64, 0:NCC]),
                         reads=[bMF], writes=[bKCG[g]])
                else:
                    for cc in range(NCH):
                        for hc in range(2):
                            P.op("tensor", lambda e, hc=hc, cc=cc: e.matmul(
                                MF[:, 0:64], lhsT=HTC[hc][:, cc * 128:(cc + 1) * 128], rhs=W2[1][:, hc, :],
                                start=(hc == 0), stop=(hc == 1)), reads=[bW2[1], bHTC[hc]], writes=[bMF])
                        P.op("vector", lambda e, g=g, cc=cc: e.tensor_copy(out=VCX[g][:, cc, 0:64], in_=MF[:, 0:64]),
                             reads=[bMF], writes=[bVCX[g]])

        def unit(i, kt_ap, bk, q, s, extra, vt_ap, bv, first, last, width):
            sp = st_next()
            pt = cnt["pt"] % 4
            cnt["pt"] += 1
            nmm = 1 + len(extra)
            P.op("tensor", lambda e: e.matmul(STP[sp][:], lhsT=kt_ap, rhs=QT[q][s][:], start=True, stop=(nmm == 1)),
                 reads=[bk, bQT[q][s]], writes=[bSTP[sp]])
            for n, (l_ap, r_ap, rb) in enumerate(extra):
                P.op("tensor", lambda e, l_ap=l_ap, r_ap=r_ap, n=n: e.matmul(
                    STP[sp][:], lhsT=l_ap, rhs=r_ap, start=False, stop=(n == nmm - 2)),
                     reads=[bC] + rb, writes=[bSTP[sp]])
            P.op("scalar", lambda e: e.activation(out=PTT[pt][:], in_=STP[sp][:], func=AF.Exp),
                 reads=[bSTP[sp]], writes=[bPTT[pt]])
            return pt

        pend = []

        def flush():
            for f in pend:
                f()
            del pend[:]

        def pv_now(pt, vt_ap, bv, first, last, width):
            for r in range(4):
                P.op("tensor", lambda e, r=r: e.matmul(OA[:, r, 0:width], lhsT=PTT[pt][:, r * 128:(r + 1) * 128],
                                                       rhs=vt_ap, start=first, stop=last),
                     reads=[bPTT[pt], bv], writes=[bOA])

        def pv(pt, vt_ap, bv, first, last, width):
            prev = list(pend)
            del pend[:]
            pend.append(lambda: pv_now(pt, vt_ap, bv, first, last, width))
            for f in prev:
                f()

        def flat(t3):
            return t3[:].rearrange("p r t -> p (r t)")

        def finish(col0, gate_col, first_branch, sink_g=None, width=65, need_rl=False):
            P.op("vector", lambda e: e.tensor_copy(out=OAS[:, :, 0:width], in_=OA[:, :, 0:width]),
                 reads=[bOA], writes=[bOAS])
            if need_rl:
                P.op("vector", lambda e: e.tensor_scalar(out=L4[:], in0=OAS[:, :, 64:65], scalar1=1e-30, scalar2=None,
                                                         op0=ALU.max), reads=[bOAS], writes=[bL4])
                P.op("vector", lambda e: e.reciprocal(out=RL4[:], in_=L4[:]), reads=[bL4], writes=[bL4])
                P.op("vector", lambda e: e.tensor_tensor(out=W4[:, :, 0], in0=RL4[:, :, 0],
                                                         in1=SG[:, gate_col:gate_col + 10:3], op=ALU.mult),
                     reads=[bL4, bSG], writes=[bL4])
            elif sink_g is not None:
                P.op("vector", lambda e: e.tensor_tensor(out=L4[:], in0=OAS[:, :, 64:65],
                                                         in1=SINKE[:, sink_g * 4:sink_g * 4 + 4, :], op=ALU.add),
                     reads=[bOAS, bSINK], writes=[bL4])
                P.op("vector", lambda e: e.reciprocal(out=W4[:], in_=L4[:]), reads=[bL4], writes=[bL4])
            else:
                P.op("vector", lambda e: e.reciprocal(out=RL4[:], in_=OAS[:, :, 64:65]), reads=[bOAS], writes=[bL4])
                P.op("vector", lambda e: e.tensor_tensor(out=W4[:, :, 0], in0=RL4[:, :, 0],
                                                         in1=SG[:, gate_col:gate_col + 10:3], op=ALU.mult),
                     reads=[bL4, bSG], writes=[bL4])
            wb = apx(W4[:, :, 0], [[0, 64]])
            if first_branch:
                P.op("vector", lambda e: e.tensor_tensor(out=OMIX[:, col0:col0 + 4, :], in0=OAS[:, :, 0:64], in1=wb,
                                                         op=ALU.mult), reads=[bOAS, bL4], writes=[bOMIX])
            else:
                P.op("vector", lambda e: e.tensor_tensor(out=TMP[:], in0=OAS[:, :, 0:64], in1=wb, op=ALU.mult),
                     reads=[bOAS, bL4], writes=[bTMP])
                P.op("vector", lambda e: e.tensor_tensor(out=OMIX[:, col0:col0 + 4, :], in0=OMIX[:, col0:col0 + 4, :],
                                                         in1=TMP[:], op=ALU.add), reads=[bTMP, bOMIX], writes=[bOMIX])

        items = []

        def U(qk_fn, pv_fn, needs=None):
            items.append(("u", qk_fn, pv_fn, needs))

        def Bar(fn, provides=None, releases=0):
            items.append(("b", fn, provides, releases))

        def tile(i):
            s = i % 2

            def tile_start():
                P.op("scalar", lambda e: e.activation(out=SG[:], in_=GATE[s][:], func=AF.Exp, scale=-1.0),
                     reads=[bGATE[s]], writes=[bSG])
                P.op("vector", lambda e: e.tensor_scalar(out=SG[:], in0=SG[:], scalar1=1.0, scalar2=None, op0=ALU.add),
                     reads=[bSG], writes=[bSG])
                P.op("vector", lambda e: e.reciprocal(out=SG[:], in_=SG[:]), reads=[bSG], writes=[bSG])
            Bar(tile_start)

            def std_branch(br, g, q, j0, far, col0, gate_col, first_branch, sink_g=None, needs=None):
                for j in range(j0, i + 1):
                    def qk(j=j):
                        extra = []
                        if br == 0:
                            extra.append((EBIG[:, j * 128:(j + 1) * 128], flat(NEGMT[g]), [bNEGMT[g]]))
                        if j == i:
                            extra.append((IDENT[:], flat(CAUS), []))
                        if far is not None and j == i - far:
                            extra.append((IDENT[:], flat(CAUSC), []))
                        return unit(i, KG[br][g][:, j * 128:(j + 1) * 128], bKG[br][g], q, s, extra, None, None, 0, 0, 0)

                    def pvf(pt, j=j):
                        pv_now(pt, VR[br][:, j, g, :], bVR[br], j == j0, j == i, 65)
                    U(qk, pvf, needs)
                Bar(lambda: finish(col0, gate_col, first_branch, sink_g=sink_g))

            def cmp_group(g):
                q = g
                nch = min(NCH, (8 * i + 6) // 128 + 1)
                pts = []
                for cc in range(nch):
                    def qk(cc=cc):
                        shift = 128 * cc - 8 * i
                        extra = []
                        if shift + 127 >= -1:
                            m = cnt["mc"] % 2
                            cnt["mc"] += 1
                            P.op("gpsimd", lambda e, m=m, shift=shift: e.affine_select(
                                out=MASKC[m][:], in_=ZER[:], pattern=[[0, 4], [1, 128]], compare_op=ALU.is_ge,
                                fill=P.reg(e, NEG), base=-31 - 16 * shift, channel_multiplier=-16),
                                 reads=[bC], writes=[bMASKC[m]])
                            extra.append((IDENT[:], flat(MASKC[m]), [bMASKC[m]]))
                        pt = unit(i, KCG[g][:, cc * 128:(cc + 1) * 128], bKCG[g], q, s, extra, None, None, 0, 0, 0)
                        pts.append(pt)
                        return pt
                    U(qk, None)

                def cmp_finish():
                    for r in range(4):
                        for cc in range(nch):
                            P.op("tensor", lambda e, r=r, cc=cc: e.matmul(
                                OA[:, r, 0:129], lhsT=PTT[pts[cc]][:, r * 128:(r + 1) * 128], rhs=VCX[g][:, cc, :],
                                start=(cc == 0), stop=(cc == nch - 1)), reads=[bPTT[pts[cc]], bVCX[g]], writes=[bOA])
                    finish(g * 4, g * 12 + 0, True, width=129, need_rl=True)
                    P.op("vector", lambda e: e.tensor_tensor(out=TMP[:], in0=OAS[:, :, 65:129],
                                                             in1=apx(RL4[:, :, 0], [[0, 64]]), op=ALU.mult),
                         reads=[bOAS, bL4], writes=[bTMP])
                    P.op("vector", lambda e: e.tensor_tensor(out=TMP[:, 0:2, :], in0=TMP[:, 0:2, :], in1=TMP[:, 2:4, :],
                                                             op=ALU.add), reads=[bTMP], writes=[bTMP])
                    P.op("vector", lambda e: e.tensor_tensor(out=PSLC[:], in0=TMP[:, 0, :], in1=TMP[:, 1, :], op=ALU.add),
                         reads=[bTMP], writes=[bPSLC])
                    P.op("vector", lambda e: e.tensor_tensor(out=SCORE[:], in0=PSLC[:], in1=FT[s][:, 0, :], op=ALU.add),
                         reads=[bPSLC, bFT[s]], writes=[bSCORE])
                    P.op("vector", lambda e: e.max(out=M1[:], in_=SCORE[:]), reads=[bSCORE], writes=[bM])
                    P.op("vector", lambda e: e.match_replace(out=WORK[:], in_to_replace=M1[:], in_values=SCORE[:],
                                                             imm_value=-3.0e38), reads=[bSCORE, bM], writes=[bWORK])
                    P.op("vector", lambda e: e.max(out=M2[:], in_=WORK[:]), reads=[bWORK], writes=[bM])
                    P.op("vector", lambda e: e.tensor_scalar(out=NEGMS[g][:], in0=SCORE[:], scalar1=M2[:, 7:8],
                                                             scalar2=NEG, op0=ALU.is_lt, op1=ALU.mult),
                         reads=[bSCORE, bM], writes=[bNEGMS[g]])
                return cmp_finish, nch

            def negmt_make(g):
                def f():
                    P.op("tensor", lambda e: e.transpose(out=MBc(0)[0:64, :], in_=NEGMS[g][:], identity=IDENT[:]),
                         reads=[bNEGMS[g], bC], writes=[bMB])
                    for r in range(4):
                        P.op("scalar", lambda e, r=r: e.copy(out=NEGMT[g][0:64, r, :], in_=MBc(0)[0:64, :]),
                             reads=[bMB], writes=[bNEGMT[g]])
                Bar(f, provides=("negmt", i, g))

            fins = [cmp_group(g) for g in range(2)]
            if i > 0:
                Bar(prev_out[0])
            for (fn, nrel) in fins:
                Bar(fn, releases=nrel)
            if i > 0:
                Bar(prev_out[1])
            for g in range(2):
                std_branch(1, g, g, max(0, i - 4), 4, g * 4, g * 12 + 2, False)
            if i > 0:
                Bar(prev_out[2])
            if i + 1 < NT:
                Bar(lambda: loads(i + 1))
            for g in range(2):
                std_branch(2, g, 2 + g, max(0, i - 1), 1, 8 + g * 4, None, True, sink_g=g)
            for g in range(2):
                negmt_make(g)
                std_branch(0, g, g, 0, None, g * 4, g * 12 + 1, False, needs=("negmt", i, g))

            def out_a():
                P.op("scalar", lambda e: e.copy(out=OMIXB[:], in_=OMIX[:].rearrange("p h d -> p (h d)")),
                     reads=[bOMIX], writes=[bOMIXB])
                for c in range(8):
                    P.op("tensor", lambda e, c=c: e.transpose(out=MBc(c), in_=OMIXB[:, c * 128:(c + 1) * 128],
                                                              identity=IDENT[:]), reads=[bOMIXB, bC], writes=[bMB])
                P.op("vector", lambda e: e.tensor_copy(out=OT[:], in_=MBV[:, :].rearrange("p (c t) -> p c t", c=8)),
                     reads=[bMB], writes=[bOT])

            def out_half(n):
                def f():
                    for c in range(8):
                        P.op("tensor", lambda e, c=c: e.matmul(MF[:], lhsT=OT[:, c, :],
                                                              rhs=WO[:, c, n * 512:(n + 1) * 512],
                                                              start=(c == 0), stop=(c == 7)),
                             reads=[bOT, bWO], writes=[bMF])
                    P.op("vector", lambda e: e.tensor_tensor(out=H1T[s][:, n * 512:(n + 1) * 512], in0=MF[:],
                                                             in1=H1T[s][:, n * 512:(n + 1) * 512], op=ALU.add),
                         reads=[bMF, bH1T[s]], writes=[bH1T[s]])
                    if n == 1:
                        P.op("sync", lambda e: e.dma_start(out=h2[i * 128:(i + 1) * 128, :], in_=H1T[s][:]),
                             reads=[bH1T[s]], writes=[bH2], dma_owner=bH1T[s])
                return f
            stages = (out_a, out_half(0), out_half(1))
            if i == NT - 1:
                for st in stages:
                    Bar(st)
            return stages

        prev_out = None
        for i in range(NT):
            prev_out = tile(i)

        n_items = len(items)
        issued = [False] * n_items
        ptsl = [None] * n_items
        provided = set()
        inflight = 0
        for k in range(n_items):
            it = items[k]
            if it[0] == "b":
                it[1]()
                if it[2] is not None:
                    provided.add(it[2])
                inflight -= it[3]
                continue
            if not issued[k]:
                ptsl[k] = it[1]()
                issued[k] = True
                inflight += 1
            m = k + 1
            while m < n_items and m - k <= 8 and inflight < DEPTH:
                im = items[m]
                if im[0] == "u" and not issued[m]:
                    if im[3] is not None and im[3] not in provided:
                        break
                    ptsl[m] = im[1]()
                    issued[m] = True
                    inflight += 1
                m += 1
            if it[2] is not None:
                it[2](ptsl[k])
                inflight -= 1
        P.emit()


def make_consts(T):
    NT = T // 128
    NCC = T // 16
    bf = ml_dtypes.bfloat16
    k = np.arange(T)
    kaug = np.stack([np.ones(T), k % 128, np.ones(T), k // 128]).astype(np.float32)
    pc = 16 * np.arange(NCC) + 31
    kaugc = np.stack([np.ones(NCC), pc % 128, np.ones(NCC), pc // 128]).astype(np.float32)
    qaug = np.zeros((2, 4, NT, 4, 128), np.float32)
    tl = np.arange(128)
    for g in range(2):
        for r in range(4):
            sl = SLOPES[g * 4 + r]
            for i in range(NT):
                qaug[g, 0, i, r] = -sl * tl
                qaug[g, 1, i, r] = sl
                qaug[g, 2, i, r] = -sl * 128.0 * i
                qaug[g, 3, i, r] = sl * 128.0
    qaug = qaug.reshape(2, 4, NT * 512)
    ns = T // 64
    t = np.arange(T)
    cur = t // 64
    blk = np.arange(64)
    ftab = np.zeros((T, 64), np.float32)
    valid = blk[None, :] <= cur[:, None]
    forced = (blk[None, :] == 0) | (blk[None, :] == cur[:, None]) | (blk[None, :] == cur[:, None] - 1)
    ftab[~valid] = -1e30
    ftab[forced] = 1e9
    ftab[:, ns:] = -1e30
    ftab = ftab.reshape(NT, 128, 64)
    c_start = np.arange(NCC) * 16
    s_start = np.arange(64) * 64
    ovl = np.clip(np.minimum(c_start[:, None] + 32, s_start[None, :] + 64)
                  - np.maximum(c_start[:, None], s_start[None, :]), 0, None) / 32.0
    ovl[NCC - 1, :] = 0.0
    return {"kaug": kaug.astype(bf), "kaugc": kaugc.astype(bf), "qaug": qaug.astype(bf), "ftab": ftab,
            "ovl": ovl.astype(np.float32).astype(bf)}


def build(T, debug=False, phases=("fa", "pp", "at", "fc")):
    NT = T // 128
    nc = bass.Bass("TRN2", target_bir_lowering=False)

    def din(name, shape, dt=F32):
        return nc.dram_tensor(name, list(shape), dt, kind="ExternalInput").ap()

    x = din("x", [T, D])
    ffn1_norm = din("ffn1_norm", [D])
    ffn1_w_in = din("ffn1_w_in", [D, 2 * FF])
    ffn1_w_out = din("ffn1_w_out", [FF, D])
    ffn2_norm = din("ffn2_norm", [D])
    ffn2_w_in = din("ffn2_w_in", [D, 2 * FF])
    ffn2_w_out = din("ffn2_w_out", [FF, D])
    final_norm = din("final_norm", [D])
    y = nc.dram_tensor("y", [T, D], F32, kind="ExternalOutput").ap()
    mix_norm = din("mix_norm", [D])
    w_mix_in = din("w_mix_in", [D, PROJ_W])
    w_mix_out = din("w_mix_out", [D, D])
    swa_sinks = din("swa_sinks", [8])
    Wd = {"w1": [din("cmp_k_w1", [2048, 256]), din("cmp_v_w1", [2048, 256])],
          "w2": [din("cmp_k_w2", [256, 64]), din("cmp_v_w2", [256, 64])],
          "pos": [din("cmp_k_pos", [32, 64]), din("cmp_v_pos", [32, 64])],
          "b1": [din("cmp_k_b1", [256]), din("cmp_v_b1", [256])],
          "wo": w_mix_out, "sinks": swa_sinks}
    NCC = T // 16
    Cd = {"kaug": din("c_kaug", [4, T], BF16), "kaugc": din("c_kaugc", [4, NCC], BF16),
          "qaug": din("c_qaug", [2, 4, NT * 512], BF16), "ftab": din("c_ftab", [NT, 128, 64]),
          "ovl": din("c_ovl", [NCC, 64], BF16)}
    kind = "ExternalOutput" if debug else "Internal"

    def scr(name, shape, dt):
        return nc.dram_tensor(name, list(shape), dt, kind=kind).ap()

    h1 = scr("h1s", [T, D], F32)
    h2 = scr("h2s", [T, D], F32)
    Sd = {"QA": scr("s_qa", [128, NT * 512], BF16), "QB": scr("s_qb", [128, NT * 512], BF16),
          "KT": scr("s_kt", [5, 128, T], BF16), "VT": scr("s_vt", [3, 128, NT, 2, 65], BF16),
          "GT": scr("s_gt", [T, 24], F32)}

    with ExitStack() as es:
        sems = [es.enter_context(nc.semaphore(f"s{i}")) for i in range(88)]
        P = Prog(nc, sems)
        if "fa" in phases:
            ffn_phase(nc, P, "fa", x, h1, ffn1_norm, ffn1_w_in, ffn1_w_out, T)
        if "pp" in phases:
            proj_phase(nc, P, h1, mix_norm, w_mix_in, Sd, T)
        if "at" in phases:
            attn_phase(nc, P, h1, h2, Sd, Cd, Wd, T)
        if "fc" in phases:
            ffn_phase(nc, P, "fc", h2, y, ffn2_norm, ffn2_w_in, ffn2_w_out, T, final_g=final_norm)
    return nc


_CACHE = {}


def kernel(**inputs):
    T = inputs["x"].shape[1]
    nb = inputs["x"].shape[0]
    if T not in _CACHE:
        _CACHE[T] = (build(T), make_consts(T))
    nc, consts = _CACHE[T]
    shared = {}
    for k, v in inputs.items():
        if k == "x":
            continue
        a = np.ascontiguousarray(np.asarray(v, dtype=np.float32))
        if k != "final_norm":
            a = a[0]
        shared[k] = np.ascontiguousarray(a)
    for k, v in consts.items():
        shared["c_" + k] = v
    xs = np.asarray(inputs["x"], dtype=np.float32)
    in_maps = []
    for b in range(nb):
        m = dict(shared)
        m["x"] = np.ascontiguousarray(xs[b])
        in_maps.append(m)
    res = run_bass_kernel_spmd(nc, in_maps, core_ids=list(range(nb)))
    return np.stack([np.asarray(r["y"], dtype=np.float32) for r in res.results], axis=0)
```
